# Optimizing a Trainium2 kernel written in Bass

```python
import math
import jax, jax.numpy as jnp
from jax import lax
import numpy as np

D_MODEL = 4096
BATCH = 8
SEQ = 2048
DEPTH = 2

EPS = 1e-6
CONV_K = 4
N_EVEN = (DEPTH + 1) // 2
N_ODD = DEPTH // 2

A_WIDTH = D_MODEL // 2
A_BLOCKS = 16
A_BLOCK_DIM = A_WIDTH // A_BLOCKS
RGLRU_C = 8.0
B_HEADS = 16
B_HEAD_DIM = 128
B_WIDTH = B_HEADS * B_HEAD_DIM
B_KV_LATENT = 512
IDX_HEADS = 32
IDX_DIM = 64
IDX_TOPK_MAX = 256
Q_BLOCK = 128
C_WIDTH = D_MODEL // 2
C_HEAD_DIM = 64
C_HEADS = C_WIDTH // C_HEAD_DIM
C_GROUPS = 4
C_HEADS_PER_GROUP = C_HEADS // C_GROUPS
C_STATE = 128
C_CONV_DIM = C_WIDTH + 2 * C_GROUPS * C_STATE
C_CHUNK = 128
D_HEADS = 16
D_HEAD_DIM = 128
D_WIDTH = D_HEADS * D_HEAD_DIM
D_CHUNK = 32

AB_SPLITS = (A_WIDTH, A_WIDTH, B_WIDTH, B_KV_LATENT, IDX_HEADS * IDX_DIM, IDX_DIM, IDX_HEADS, B_WIDTH)
CD_SPLITS = (C_WIDTH, C_CONV_DIM, C_HEADS, D_WIDTH, D_WIDTH, D_WIDTH, D_WIDTH)
AB_COLS = sum(AB_SPLITS)
CD_COLS = sum(CD_SPLITS)

kernel_name = "hybrid_rglru_dsa_ssd_hgrn2"


def _split(t, sizes):
    return jnp.split(t, np.cumsum(sizes)[:-1].tolist(), axis=-1)


def _rmsnorm(x, g):
    xf = x.astype(jnp.float32)
    y = xf * lax.rsqrt(jnp.mean(xf * xf, axis=-1, keepdims=True) + EPS)
    return (y * g.astype(jnp.float32)).astype(x.dtype)


def _rmsnorm_grouped(x, g, groups):
    shp = x.shape
    xg = x.reshape(*shp[:-1], groups, shp[-1] // groups).astype(jnp.float32)
    y = xg * lax.rsqrt(jnp.mean(xg * xg, axis=-1, keepdims=True) + EPS)
    return (y.reshape(shp) * g.astype(jnp.float32)).astype(x.dtype)


def _causal_dwconv(x, w, b):
    k, c = w.shape
    y = lax.conv_general_dilated(x, w[:, None, :], window_strides=(1,), padding=[(k - 1, 0)],
                                 dimension_numbers=('NWC', 'WIO', 'NWC'), feature_group_count=c)
    return y + b


def _lin_combine(c1, c2):
    a1, b1 = c1
    a2, b2 = c2
    return a1 * a2, a2 * b1 + b2


def _rglru(x, wa, ba, wx, bx, lam):
    bsz, seqlen, _ = x.shape
    xb = x.reshape(bsz, seqlen, A_BLOCKS, A_BLOCK_DIM)
    r = jax.nn.sigmoid(jnp.einsum('bsgi,gij->bsgj', xb, wa).reshape(bsz, seqlen, A_WIDTH) + ba)
    i = jax.nn.sigmoid(jnp.einsum('bsgi,gij->bsgj', xb, wx).reshape(bsz, seqlen, A_WIDTH) + bx)
    log_a = -RGLRU_C * r * jax.nn.softplus(-lam)
    a = jnp.exp(log_a)
    b = jnp.sqrt(-jnp.expm1(2.0 * log_a)) * (i * x)
    _, h = lax.associative_scan(_lin_combine, (a, b), axis=1)
    return h


def _dsa_attention(q, ckv, qi, ki, wi, kv_norm, w_uk, w_uv):
    bsz, seqlen, _ = q.shape
    topk = min(IDX_TOPK_MAX, seqlen // 4)
    nblk = seqlen // Q_BLOCK
    ckv = _rmsnorm(ckv, kv_norm)
    q = q.reshape(bsz, seqlen, B_HEADS, B_HEAD_DIM)
    qi = qi.reshape(bsz, seqlen, IDX_HEADS, IDX_DIM)
    wi = wi * IDX_HEADS ** -0.5
    key_pos = jnp.arange(seqlen, dtype=jnp.int32)

    def to_blocks(t):
        return t.reshape(bsz, nblk, Q_BLOCK, *t.shape[2:]).swapaxes(0, 1)

    def block(args):
        q_b, qi_b, wi_b, start = args
        qpos = start + jnp.arange(Q_BLOCK, dtype=jnp.int32)
        causal = key_pos[None, :] <= qpos[:, None]
        rel = jax.nn.relu(jnp.einsum('bthd,bsd->bths', qi_b, ki) * IDX_DIM ** -0.5)
        score = jnp.einsum('bth,bths->bts', wi_b, rel).astype(jnp.float32)
        score = jnp.where(causal[None], score, -jnp.inf)
        _, sel = lax.top_k(score, topk)
        valid = sel <= qpos[None, :, None]
        c_sel = jax.vmap(lambda c, idx: c[idx])(ckv, sel)
        q_lat = jnp.einsum('bthd,hcd->bthc', q_b, w_uk)
        logits = jnp.einsum('bthc,btkc->bthk', q_lat, c_sel).astype(jnp.float32) * B_HEAD_DIM ** -0.5
        logits = jnp.where(valid[:, :, None, :], logits, -jnp.inf)
        p = jax.nn.softmax(logits, axis=-1).astype(c_sel.dtype)
        o_lat = jnp.einsum('bthk,btkc->bthc', p, c_sel)
        return jnp.einsum('bthc,hcd->bthd', o_lat, w_uv)

    starts = jnp.arange(nblk, dtype=jnp.int32) * Q_BLOCK
    out = lax.map(block, (to_blocks(q), to_blocks(qi), to_blocks(wi), starts))
    return out.swapaxes(0, 1).reshape(bsz, seqlen, B_WIDTH)


def _ab_mixer(h, in_w, conv_w, conv_b, wa, ba, wx, bx, lam, kv_norm, w_uk, w_uv, out_w):
    xa, ga, q, ckv, qi, ki, wi, gb = _split(h @ in_w, AB_SPLITS)
    xa = _causal_dwconv(xa, conv_w, conv_b)
    ya = _rglru(xa, wa, ba, wx, bx, lam) * jax.nn.silu(ga)
    yb = _dsa_attention(q, ckv, qi, ki, wi, kv_norm, w_uk, w_uv) * jax.nn.silu(gb)
    return jnp.concatenate([ya, yb], axis=-1) @ out_w


def _ssd(x, dt, a_log, bm, cm, d_skip):
    bsz, seqlen = x.shape[:2]
    nc = seqlen // C_CHUNK
    a = -jnp.exp(a_log)
    xc = x.reshape(bsz, nc, C_CHUNK, *x.shape[2:])
    dtc = dt.reshape(bsz, nc, C_CHUNK, *dt.shape[2:])
    bc = bm.reshape(bsz, nc, C_CHUNK, *bm.shape[2:])
    cc = cm.reshape(bsz, nc, C_CHUNK, *cm.shape[2:])
    a_cs = jnp.cumsum(dtc * a, axis=2)
    xdt = xc * dtc[..., None]
    tri = jnp.tril(jnp.ones((C_CHUNK, C_CHUNK), dtype=bool))
    diff = a_cs[:, :, :, None] - a_cs[:, :, None, :]
    seg = jnp.exp(jnp.where(tri[:, :, None, None], diff, -jnp.inf))
    cb = jnp.einsum('bclgn,bcsgn->bclsg', cc, bc)
    y_diag = jnp.einsum('bclsg,bclsgj,bcsgjp->bclgjp', cb, seg, xdt)
    decay_states = jnp.exp(a_cs[:, :, -1:] - a_cs)
    states = jnp.einsum('bcsgn,bcsgj,bcsgjp->bcgjpn', bc, decay_states, xdt)
    chunk_decay = jnp.exp(a_cs[:, :, -1])

    def step(carry, inp):
        dec, st = inp
        return (dec[..., None, None] * carry + st).astype(carry.dtype), carry

    init = jnp.zeros_like(states[:, 0])
    _, prev = lax.scan(step, init, (chunk_decay.swapaxes(0, 1), states.swapaxes(0, 1)))
    prev = prev.swapaxes(0, 1)
    y_off = jnp.einsum('bclgn,bcgjpn,bclgj->bclgjp', cc, prev, jnp.exp(a_cs))
    y = y_diag + y_off + xc * d_skip[..., None]
    return y.reshape(bsz, seqlen, C_WIDTH)


def _hgrn2(q, f_raw, v, lb):
    bsz, seqlen, _ = q.shape
    nc = seqlen // D_CHUNK
    f = lb + (1.0 - lb) * jax.nn.sigmoid(f_raw)
    k = 1.0 - f
    shp = (bsz, nc, D_CHUNK, D_HEADS, D_HEAD_DIM)
    q, k, v, logf = (t.reshape(shp) for t in (q, k, v, jnp.log(f)))
    bcs = jnp.cumsum(logf, axis=2)
    qt = q * jnp.exp(bcs)
    kt = k * jnp.exp(-bcs)
    tri = jnp.tril(jnp.ones((D_CHUNK, D_CHUNK), dtype=bool))
    att = jnp.where(tri, jnp.einsum('bcthk,bcshk->bchts', qt, kt), 0.0)
    o_intra = jnp.einsum('bchts,bcshv->bcthv', att, v)
    b_last = bcs[:, :, -1]
    states = jnp.einsum('bcshk,bcshv->bchkv', k * jnp.exp(b_last[:, :, None] - bcs), v)

    def step(carry, inp):
        dec, st = inp
        return (dec[..., None] * carry + st).astype(carry.dtype), carry

    init = jnp.zeros_like(states[:, 0])
    _, prev = lax.scan(step, init, (jnp.exp(b_last).swapaxes(0, 1), states.swapaxes(0, 1)))
    o_inter = jnp.einsum('bcthk,bchkv->bcthv', qt, prev.swapaxes(0, 1))
    return (o_intra + o_inter).reshape(bsz, seqlen, D_WIDTH)


def _cd_mixer(h, layer, in_w, conv_w, conv_b, dt_bias, a_log, d_skip, ssd_norm, hgrn_norm,
              lower_bounds, out_w):
    bsz, seqlen, _ = h.shape
    z, xbc, dt, q, f_raw, iv, gd = _split(h @ in_w, CD_SPLITS)
    xbc = jax.nn.silu(_causal_dwconv(xbc, conv_w, conv_b))
    xc, bm, cm = _split(xbc, (C_WIDTH, C_GROUPS * C_STATE, C_GROUPS * C_STATE))
    dt = jax.nn.softplus(dt + dt_bias)
    g, j = C_GROUPS, C_HEADS_PER_GROUP
    y = _ssd(xc.reshape(bsz, seqlen, g, j, C_HEAD_DIM), dt.reshape(bsz, seqlen, g, j),
             a_log.reshape(g, j), bm.reshape(bsz, seqlen, g, C_STATE),
             cm.reshape(bsz, seqlen, g, C_STATE), d_skip.reshape(g, j))
    yc = _rmsnorm_grouped(y * jax.nn.silu(z), ssd_norm, C_GROUPS)
    p = jax.nn.softmax(lower_bounds.astype(jnp.float32), axis=0)
    lb = (jnp.cumsum(p, axis=0) - p[0])[layer].astype(f_raw.dtype)
    yd = _hgrn2(jax.nn.silu(q), f_raw, iv, lb)
    yd = _rmsnorm_grouped(yd, hgrn_norm, D_HEADS) * jax.nn.silu(gd)
    return jnp.concatenate([yc, yd], axis=-1) @ out_w


def setup_inputs(seed: int = 0) -> dict:
    key = jax.random.key(seed)
    ks = iter(jax.random.split(key, 40))

    def nrm(shape, scale):
        return scale * jax.random.normal(next(ks), shape, jnp.float32)

    def gain(shape):
        return 1.0 + nrm(shape, 0.02)

    u = jax.random.uniform(next(ks), (N_EVEN, A_WIDTH), jnp.float32, 0.9, 0.999)
    a_base = u ** (1.0 / RGLRU_C)
    rg_lambda = jnp.log(a_base) - jnp.log1p(-a_base)
    dt0 = jnp.exp(jax.random.uniform(next(ks), (N_ODD, C_HEADS), jnp.float32,
                                     math.log(1e-3), math.log(1e-1)))
    dt_bias = dt0 + jnp.log(-jnp.expm1(-dt0))
    a_log = jnp.log(jax.random.uniform(next(ks), (N_ODD, C_HEADS), jnp.float32, 1.0, 16.0))
    return {
        "x": nrm((BATCH, SEQ, D_MODEL), 1.0),
        "norm_w": gain((DEPTH, D_MODEL)),
        "final_norm": gain((D_MODEL,)),
        "ab_in_w": nrm((N_EVEN, D_MODEL, AB_COLS), D_MODEL ** -0.5),
        "ab_conv_w": nrm((N_EVEN, CONV_K, A_WIDTH), CONV_K ** -0.5),
        "ab_conv_b": nrm((N_EVEN, A_WIDTH), 0.01),
        "ab_rg_wa": nrm((N_EVEN, A_BLOCKS, A_BLOCK_DIM, A_BLOCK_DIM), A_BLOCK_DIM ** -0.5),
        "ab_rg_ba": nrm((N_EVEN, A_WIDTH), 0.01),
        "ab_rg_wx": nrm((N_EVEN, A_BLOCKS, A_BLOCK_DIM, A_BLOCK_DIM), A_BLOCK_DIM ** -0.5),
        "ab_rg_bx": nrm((N_EVEN, A_WIDTH), 0.01),
        "ab_rg_lambda": rg_lambda,
        "ab_kv_norm": gain((N_EVEN, B_KV_LATENT)),
        "ab_w_uk": nrm((N_EVEN, B_HEADS, B_KV_LATENT, B_HEAD_DIM), B_KV_LATENT ** -0.5),
        "ab_w_uv": nrm((N_EVEN, B_HEADS, B_KV_LATENT, B_HEAD_DIM), B_KV_LATENT ** -0.5),
        "ab_out_w": nrm((N_EVEN, A_WIDTH + B_WIDTH, D_MODEL), (A_WIDTH + B_WIDTH) ** -0.5),
        "cd_in_w": nrm((N_ODD, D_MODEL, CD_COLS), D_MODEL ** -0.5),
        "cd_conv_w": nrm((N_ODD, CONV_K, C_CONV_DIM), CONV_K ** -0.5),
        "cd_conv_b": nrm((N_ODD, C_CONV_DIM), 0.01),
        "cd_dt_bias": dt_bias,
        "cd_a_log": a_log,
        "cd_d_skip": gain((N_ODD, C_HEADS)),
        "cd_ssd_norm": gain((N_ODD, C_WIDTH)),
        "cd_hgrn_norm": gain((N_ODD, D_WIDTH)),
        "cd_out_w": nrm((N_ODD, C_WIDTH + D_WIDTH, D_MODEL), (C_WIDTH + D_WIDTH) ** -0.5),
        "hgrn_lower_bounds": nrm((DEPTH, D_WIDTH), 0.1),
    }


def reference(x, norm_w, final_norm, ab_in_w, ab_conv_w, ab_conv_b, ab_rg_wa, ab_rg_ba,
              ab_rg_wx, ab_rg_bx, ab_rg_lambda, ab_kv_norm, ab_w_uk, ab_w_uv, ab_out_w,
              cd_in_w, cd_conv_w, cd_conv_b, cd_dt_bias, cd_a_log, cd_d_skip, cd_ssd_norm,
              cd_hgrn_norm, cd_out_w, hgrn_lower_bounds):
    h = x
    for layer in range(DEPTH):
        j = layer // 2
        hn = _rmsnorm(h, norm_w[layer])
        if layer % 2 == 0:
            h = h + _ab_mixer(hn, ab_in_w[j], ab_conv_w[j], ab_conv_b[j], ab_rg_wa[j], ab_rg_ba[j],
                              ab_rg_wx[j], ab_rg_bx[j], ab_rg_lambda[j], ab_kv_norm[j],
                              ab_w_uk[j], ab_w_uv[j], ab_out_w[j])
        else:
            h = h + _cd_mixer(hn, layer, cd_in_w[j], cd_conv_w[j], cd_conv_b[j], cd_dt_bias[j],
                              cd_a_log[j], cd_d_skip[j], cd_ssd_norm[j], cd_hgrn_norm[j],
                              hgrn_lower_bounds, cd_out_w[j])
    return _rmsnorm(h, final_norm)
```

```python
import numpy as np
import concourse.bass as bass
import concourse.mybir as mybir
from concourse.bass_utils import run_bass_kernel_spmd

F32 = mybir.dt.float32
BF16 = mybir.dt.bfloat16
AF = mybir.ActivationFunctionType
ALU = mybir.AluOpType
T = 2048
D = 4096
AB_COLS = 10848
CD_COLS = 13344
EPOCH = 30000
SB_BASE = 16640
SB_CAP = SB_BASE + 207 * 1024
NEG = -1.0e30


class Ins:
    __slots__ = ('eng', 'fn', 'deps', 'signal', 'seq', 'dkey', 'dcnt', 'order', 'isdma')


class Buf:
    __slots__ = ('name', 'w', 'r')

    def __init__(self, name=''):
        self.name = name
        self.w = {}
        self.r = {}


class Tn:
    def __init__(self, h, name=''):
        self.h = h
        self.b = Buf(name)

    def __getitem__(self, k):
        return self.h[k]


class Prog:
    ENGS = ('tensor', 'vector', 'scalar', 'gpsimd', 'sync')

    def __init__(self, nc, ndma=12):
        self.nc = nc
        self.streams = {e: [] for e in self.ENGS}
        self.order = 0
        self.ndma = ndma
        self.rr = {}
        self.dlast = {}
        self.dcount = {}
        self.lastc = {}
        self.barrier_deps = {e: None for e in self.ENGS}

    def add(self, eng, fn, reads=(), writes=(), dma=False):
        ins = Ins()
        ins.eng = eng
        ins.fn = fn
        ins.signal = False
        ins.seq = None
        ins.isdma = dma
        ins.dcnt = 0
        ins.order = self.order
        self.order += 1
        deps = {}

        def need(d):
            k = d.dkey
            if k not in deps or deps[k].order < d.order:
                deps[k] = d
        for b in reads:
            for d in b.w.values():
                need(d)
        for b in writes:
            for d in b.w.values():
                need(d)
            for d in b.r.values():
                need(d)
        bd = self.barrier_deps[eng]
        if bd is not None:
            for d in bd:
                need(d)
            self.barrier_deps[eng] = None
        if dma:
            k = self.rr.get(eng, 0)
            self.rr[eng] = (k + 1) % self.ndma
            ins.dkey = ('d', eng, k)
            prev = self.dlast.get(ins.dkey)
            if prev is not None:
                need(prev)
            self.dlast[ins.dkey] = ins
            ins.dcnt = self.dcount.get(ins.dkey, 0) + 16
            self.dcount[ins.dkey] = ins.dcnt
        else:
            ins.dkey = eng
            if eng == 'tensor':
                deps.pop('tensor', None)
            self.lastc[eng] = ins
        for d in deps.values():
            if not d.isdma:
                d.signal = True
        ins.deps = list(deps.values())
        for b in writes:
            if b.r:
                b.w = {}
                b.r = {}
            b.w[ins.dkey] = ins
        for b in reads:
            b.r[ins.dkey] = ins
        self.streams[eng].append(ins)
        return ins

    def barrier(self):
        deps = list(self.lastc.values()) + list(self.dlast.values())
        for e in self.ENGS:
            self.barrier_deps[e] = list(deps)

    def emit(self):
        nc = self.nc
        self.barrier()
        for e in self.ENGS:
            self.add(e, None)
        nsig = {}
        for e, st in self.streams.items():
            n = 0
            for ins in st:
                if (not ins.isdma) and ins.signal:
                    n += 1
                    ins.seq = n
            nsig[e] = n
        csem = {e: [nc.alloc_semaphore(name=f"c_{e}_{i}") for i in range((nsig[e] + EPOCH - 1) // EPOCH)]
                for e in self.ENGS}
        dsem = {k: nc.alloc_semaphore(name=f"d_{k[1]}_{k[2]}") for k in self.dcount}
        streams = self.streams

        def run(ename, eng):
            waited = {}
            maxep = {}
            for ins in streams[ename]:
                for d in ins.deps:
                    if d.isdma:
                        key = d.dkey
                        sem = dsem[key]
                        val = d.dcnt
                    else:
                        ep = (d.seq - 1) // EPOCH
                        if maxep.get(d.eng, -1) > ep:
                            continue
                        key = (d.eng, ep)
                        sem = csem[d.eng][ep]
                        val = (d.seq - 1) % EPOCH + 1
                    if waited.get(key, 0) >= val:
                        continue
                    eng.wait_ge(sem, val)
                    waited[key] = val
                    if not d.isdma:
                        maxep[d.eng] = max(maxep.get(d.eng, -1), ep)
                if ins.fn is None:
                    continue
                r = ins.fn(eng)
                if ins.isdma:
                    r.then_inc(dsem[ins.dkey], 16)
                elif ins.signal:
                    ep = (ins.seq - 1) // EPOCH
                    r.then_inc(csem[ename][ep], 1)

        with nc.Block() as block:
            @block.tensor
            def _(e):
                run('tensor', e)

            @block.vector
            def _(e):
                run('vector', e)

            @block.scalar
            def _(e):
                run('scalar', e)

            @block.gpsimd
            def _(e):
                run('gpsimd', e)

            @block.sync
            def _(e):
                run('sync', e)


def _size(dt):
    return 2 if dt == BF16 else 4


class SBAlloc:
    def __init__(self, nc):
        self.nc = nc
        self.ptr = SB_BASE
        self.n = 0

    def alloc(self, shape, dtype, name='t'):
        nb = int(np.prod(shape[1:])) * _size(dtype)
        off = (self.ptr + 63) // 64 * 64
        self.ptr = off + nb
        assert self.ptr <= SB_CAP, f"SBUF overflow {self.ptr} at {name}"
        self.n += 1
        h = self.nc.alloc_sbuf_tensor_at(f"{name}_{self.n}", list(shape), dtype, offset=off)
        return Tn(h, name)

    def mark(self):
        return self.ptr

    def reset(self, m):
        self.ptr = m


def bl(x):
    out = []
    for t in x:
        if isinstance(t, Tn):
            out.append(t.b)
        elif isinstance(t, Buf):
            out.append(t)
        else:
            out.extend(t)
    return out


class K:
    def __init__(self, P):
        self.P = P
        self.alt = 0

    def mm(self, out, lhsT, rhs, start, stop, R, W):
        self.P.add('tensor', lambda e: e.matmul(out, lhsT, rhs, start=start, stop=stop), bl(R), bl(W))

    def tr(self, out, in_, ident, R, W):
        self.P.add('tensor', lambda e: e.transpose(out, in_, ident), bl(R), bl(W))

    def act(self, out, in_, func, R, W, bias=None, scale=None, accum=None):
        kw = {}
        if bias is not None:
            kw['bias'] = bias
        if scale is not None:
            kw['scale'] = scale
        if accum is not None:
            kw['accum_out'] = accum
        self.P.add('scalar', lambda e: e.activation(out, in_, func, **kw), bl(R), bl(W))

    def tt(self, eng, out, in0, in1, op, R, W):
        self.P.add(eng, lambda e: e.tensor_tensor(out, in0, in1, op), bl(R), bl(W))

    def ts(self, eng, out, in0, s1, s2, op0, op1, R, W):
        if s2 is None:
            self.P.add(eng, lambda e: e.tensor_scalar(out, in0, s1, None, op0), bl(R), bl(W))
        else:
            self.P.add(eng, lambda e: e.tensor_scalar(out, in0, s1, s2, op0, op1), bl(R), bl(W))

    def stt(self, out, in0, scalar, in1, op0, op1, R, W):
        self.P.add('vector', lambda e: e.scalar_tensor_tensor(out, in0, scalar, in1, op0, op1), bl(R), bl(W))

    def cp(self, eng, out, in_, R, W):
        if eng == 'scalar':
            self.P.add('scalar', lambda e: e.activation(out, in_, AF.Copy), bl(R), bl(W))
        else:
            self.P.add(eng, lambda e: e.tensor_copy(out, in_), bl(R), bl(W))

    def cpa(self, out, in_, R, W):
        self.alt ^= 1
        self.cp('scalar' if self.alt else 'vector', out, in_, R, W)

    def recip(self, out, in_, R, W):
        self.P.add('vector', lambda e: e.reciprocal(out, in_), bl(R), bl(W))

    def scan(self, out, d0, d1, init, op0, op1, R, W):
        self.P.add('vector', lambda e: e.tensor_tensor_scan(out, d0, d1, init, op0, op1), bl(R), bl(W))

    def memset(self, eng, out, val, W):
        self.P.add(eng, lambda e: e.memset(out, val), [], bl(W))

    def dma(self, q, out, in_, R, W):
        self.P.add(q, lambda e: e.dma_start(out=out, in_=in_), bl(R), bl(W), dma=True)


PP = {}
_o = 0
for _n, _w in (('cw0', 64), ('cb0', 16), ('ba', 16), ('bx', 16), ('lam', 16), ('kvn', 4),
               ('cw1', 96), ('cb1', 24), ('dtb', 1), ('ssdn', 16), ('hgn', 16), ('lbs', 32)):
    PP[_n] = (_o, _w)
    _o += _w
NPP = _o
NCST = 640


def pack_params(inp):
    pp = np.zeros((128, NPP), np.float32)

    def put(name, arr):
        o, w = PP[name]
        pp[:arr.shape[0], o:o + w] = arr.reshape(arr.shape[0], -1)
    put('cw0', inp['ab_conv_w'][0].reshape(4, 16, 128).transpose(2, 1, 0))
    put('cb0', inp['ab_conv_b'][0].reshape(16, 128).T)
    put('ba', inp['ab_rg_ba'][0].reshape(16, 128).T)
    put('bx', inp['ab_rg_bx'][0].reshape(16, 128).T)
    put('lam', inp['ab_rg_lambda'][0].reshape(16, 128).T)
    put('kvn', inp['ab_kv_norm'][0].reshape(4, 128).T)
    put('cw1', inp['cd_conv_w'][0].reshape(4, 24, 128).transpose(2, 1, 0))
    put('cb1', inp['cd_conv_b'][0].reshape(24, 128).T)
    put('dtb', inp['cd_dt_bias'][0].reshape(32, 1))
    put('ssdn', inp['cd_ssd_norm'][0].reshape(16, 128).T)
    put('hgn', inp['cd_hgrn_norm'][0].reshape(16, 128).T)
    put('lbs', inp['hgrn_lower_bounds'].reshape(2, 16, 128).transpose(2, 0, 1))
    return pp


def make_consts():
    c = np.zeros((128, NCST), np.float32)
    p = np.arange(128)[:, None]
    j = np.arange(128)[None, :]
    c[:, 0:128] = (p == j)
    c[:, 128:256] = (p <= j)
    c[:, 256:384] = (p > j)
    c[:, 384:512] = np.where(j <= p, 0.0, NEG)
    c[:, 512:640] = 1.0
    rst = np.ones((128, T), np.float32)
    rst[:, ::128] = 0.0
    return c, rst


def build(debug=False, stop_after=None):
    nc = bass.Bass("TRN2", target_bir_lowering=False)
    P = Prog(nc)
    k = K(P)
    sb = SBAlloc(nc)

    def din(name, shape, dt=F32):
        return Tn(nc.dram_tensor(name, list(shape), dt, kind="ExternalInput").ap(), name)

    def dscr(name, shape, dt, out=False):
        kind = "ExternalOutput" if (out or debug) else "Internal"
        return Tn(nc.dram_tensor(name, list(shape), dt, kind=kind).ap(), name)

    x = din("x", [T, D])
    norm_w = din("norm_w", [2, D])
    final_norm = din("final_norm", [D])
    ab_in_w = din("ab_in_w", [D, AB_COLS])
    wa = din("ab_rg_wa", [16, 128, 128])
    wx = din("ab_rg_wx", [16, 128, 128])
    w_uk = din("ab_w_uk", [16, 512, 128])
    w_uv = din("ab_w_uv", [16, 512, 128])
    ab_out_w = din("ab_out_w", [D, D])
    cd_in_w = din("cd_in_w", [D, CD_COLS])
    a_log = din("cd_a_log", [32])
    d_skip = din("cd_d_skip", [32])
    cd_out_w = din("cd_out_w", [D, D])
    pp_d = din("pp", [128, NPP])
    cst_d = din("cst", [128, NCST])
    rst_d = din("rst", [128, T])
    out_d = dscr("out", [T, D], F32, out=True)

    yT = dscr("s_yT", [D, T], BF16)
    h1 = dscr("s_h1", [T, D], F32)
    h2 = dscr("s_h2", [T, D], F32)
    ckv_raw = dscr("s_ckv", [512, T], F32)
    qlat_s = dscr("s_qlat", [16, 4, 128, T], BF16)
    qiT_s = dscr("s_qi", [2048, T], BF16)
    sgb_s = dscr("s_sgb", [2048, T], BF16)
    sz_s = dscr("s_sz", [2048, T], BF16)
    xc_s = dscr("s_xc", [2048, T], BF16)
    BT_s = dscr("s_B", [512, T], BF16)
    CT_s = dscr("s_C", [512, T], BF16)
    qt_s = dscr("s_qt", [2048, T], BF16)
    kt_s = dscr("s_kt", [2048, T], BF16)
    kd_s = dscr("s_kd", [2048, T], BF16)
    sgd_s = dscr("s_sgd", [2048, T], BF16)
    v_s = dscr("s_v", [T, 2048], BF16)

    ps = nc.alloc_psum_tensor("ps", [128, 4096], F32)
    pb = [Buf(f"bank{i}") for i in range(8)]

    def bank(i):
        return ps[:, i * 512:(i + 1) * 512]

    def bank_bf(i):
        return ps[:, i * 512:(i + 1) * 512].bitcast(BF16)

    class Grp:
        def __init__(self, g):
            self.ap = ps[:, g * 2048:(g + 1) * 2048]
            self.bufs = pb[g * 4:(g + 1) * 4]
    G = [Grp(0), Grp(1)]

    cst = sb.alloc([128, NCST], F32, 'cst')
    pp = sb.alloc([128, NPP], F32, 'pp')
    cbf = sb.alloc([128, 384], BF16, 'cbf')
    k.dma('sync', cst[:, :], cst_d[:, :], [], [cst])
    k.dma('sync', pp[:, :], pp_d[:, :], [], [pp])
    k.cp('vector', cbf[:, 0:256], cst[:, 0:256], [cst], [cbf])
    k.cp('vector', cbf[:, 256:384], cst[:, 512:640], [cst], [cbf])
    ident_f = cst[:, 0:128]
    le_f = cst[:, 128:256]
    gt_f = cst[:, 256:384]
    cbias_f = cst[:, 384:512]
    ones_f = cst[:, 512:640]
    ident_b = cbf[:, 0:128]
    le_b = cbf[:, 128:256]
    ones_b = cbf[:, 256:384]

    def ppc(name, j=0, n=1):
        o, w = PP[name]
        return pp[:, o + j:o + j + n]

    kiT2 = sb.alloc([128, T], BF16, 'kiT2')
    wi_tok = sb.alloc([128, 16, 32], F32, 'wi_tok')
    dt_tok = sb.alloc([128, 16, 32], F32, 'dt_tok')
    dtA_tok = sb.alloc([128, 16, 32], F32, 'dtA_tok')
    ebl = sb.alloc([128, 16, 16], F32, 'ebl')
    base_mark = sb.mark()

    def phase_norm_T(src, nw_ap, hnT):
        m = sb.mark()
        nwb = sb.alloc([128, D], F32, 'nwb')
        xt = sb.alloc([128, D], F32, 'xt')
        junk = sb.alloc([128, D], BF16, 'junk')
        xn = sb.alloc([128, D], BF16, 'xn')
        st = sb.alloc([128, 4], F32, 'st')
        k.dma('sync', nwb[:, :], nw_ap.partition_broadcast(128), [], [nwb])
        for tt in range(16):
            k.dma('sync', xt[:, :], src[tt * 128:(tt + 1) * 128, :], [src], [xt])
            k.act(junk[:, :], xt[:, :], AF.Square, [xt], [junk, st], accum=st[:, 0:1])
            k.ts('vector', st[:, 1:2], st[:, 0:1], 1.0 / D, 1e-6, ALU.mult, ALU.add, [st], [st])
            k.act(st[:, 2:3], st[:, 1:2], AF.Sqrt, [st], [st])
            k.recip(st[:, 3:4], st[:, 2:3], [st], [st])
            k.stt(xn[:, :], xt[:, :], st[:, 3:4], nwb[:, :], ALU.mult, ALU.mult, [xt, st, nwb], [xn])
            for g in range(4):
                bi = (tt * 4 + g) % 8
                bb = bank_bf(bi)
                for j in range(8):
                    c = g * 8 + j
                    k.tr(bb[:, j * 128:(j + 1) * 128], xn[:, c * 128:(c + 1) * 128], ident_b,
                         [xn, cbf], [pb[bi]])
                k.cpa(hnT[:, g * 8:(g + 1) * 8, tt * 128:(tt + 1) * 128],
                      bb.rearrange("p (a b) -> p a b", a=8), [pb[bi]], [hnT])
        P.barrier()
        sb.reset(m)

    def linear_fm(hnT, W, blocks, wslots, evac, wtag):
        for bi, segs in enumerate(blocks):
            ws = wslots[bi % len(wslots)]
            ncols = 0
            for (d0, s0, n) in segs:
                k.dma('gpsimd', ws[:, :, d0:d0 + n],
                      W[:, s0:s0 + n].rearrange("(kc p) c -> p kc c", p=128), [], [ws])
                ncols = max(ncols, d0 + n)
            grp = G[bi % 2]
            for kc in range(32):
                for tc in range(4):
                    k.mm(grp.ap[0:ncols, tc * 512:(tc + 1) * 512], ws[:, kc, 0:ncols],
                         hnT[:, kc, tc * 512:(tc + 1) * 512], kc == 0, kc == 31, [ws, hnT], [grp.bufs])
            evac(bi, grp, ncols)

    def conv_block(grp, a1, a2, cwname, cbname, blk):
        k.cp('scalar', a1[:, 3:3 + T], grp.ap, [grp.bufs], [a1])
        k.act(a2[:, :], a1[:, 3:3 + T], AF.Identity, [a1, pp], [a2],
              bias=ppc(cbname, blk), scale=ppc(cwname, blk * 4 + 3))
        for kk in range(3):
            k.stt(a2[:, :], a1[:, kk:kk + T], ppc(cwname, blk * 4 + kk), a2[:, :], ALU.mult, ALU.add,
                  [a1, pp, a2], [a2])

    hnT = sb.alloc([128, 32, T], BF16, 'hnT')
    phase_norm_T(x, norm_w[0, :], hnT)
    l0_mark = sb.mark()
    wslots = [sb.alloc([128, 32, 128], BF16, f'w{i}') for i in range(2)]

    m = sb.mark()
    a1 = sb.alloc([128, T + 4], F32, 'a1')
    a2 = sb.alloc([128, T], F32, 'a2')
    a3 = sb.alloc([128, T], F32, 'a3')
    xcb = sb.alloc([128, T], BF16, 'xcb')
    sga = sb.alloc([128, T], BF16, 'sga')
    yab = sb.alloc([128, T], BF16, 'yab')
    gw = sb.alloc([128, 2, 128], BF16, 'gw')
    s1 = sb.alloc([128, 48], F32, 's1')
    k.memset('vector', a1[:, 0:3], 0.0, [a1])
    k.act(s1[:, 0:16], ppc('lam', 0, 16), AF.Exp, [pp], [s1], scale=-1.0)
    k.act(s1[:, 0:16], s1[:, 0:16], AF.Ln, [s1], [s1], bias=1.0)
    k.ts('vector', s1[:, 16:32], s1[:, 0:16], -8.0, None, ALU.mult, None, [s1], [s1])
    k.ts('vector', s1[:, 32:48], s1[:, 0:16], -16.0, None, ALU.mult, None, [s1], [s1])

    def evac_rg(bi, grp, ncols):
        g = bi // 2
        if bi % 2 == 0:
            conv_block(grp, a1, a2, 'cw0', 'cb0', g)
            k.cp('gpsimd', xcb[:, :], a2[:, :], [a2], [xcb])
            k.dma('gpsimd', gw[:, 0, :], wa[g, :, :], [], [gw])
            k.dma('gpsimd', gw[:, 1, :], wx[g, :, :], [], [gw])
        else:
            k.act(sga[:, :], grp.ap, AF.Silu, [grp.bufs], [sga])
            g0 = G[0]
            for tc in range(4):
                k.mm(g0.ap[:, tc * 512:(tc + 1) * 512], gw[:, 0, :], xcb[:, tc * 512:(tc + 1) * 512],
                     True, True, [gw, xcb], [g0.bufs])
            k.act(a1[:, 0:T], g0.ap, AF.Sigmoid, [g0.bufs, pp], [a1], bias=ppc('ba', g))
            g1 = G[1]
            for tc in range(4):
                k.mm(g1.ap[:, tc * 512:(tc + 1) * 512], gw[:, 1, :], xcb[:, tc * 512:(tc + 1) * 512],
                     True, True, [gw, xcb], [g1.bufs])
            k.act(a3[:, :], a1[:, 0:T], AF.Exp, [a1, s1], [a3], scale=s1[:, 32 + g:33 + g])
            k.act(a1[:, 0:T], a1[:, 0:T], AF.Exp, [a1, s1], [a1], scale=s1[:, 16 + g:17 + g])
            k.act(a3[:, :], a3[:, :], AF.Sqrt, [a3], [a3], bias=1.0, scale=-1.0)
            k.tt('vector', a3[:, :], a3[:, :], a2[:, :], ALU.mult, [a3, a2], [a3])
            k.act(a2[:, :], g1.ap, AF.Sigmoid, [g1.bufs, pp], [a2], bias=ppc('bx', g))
            k.tt('vector', a3[:, :], a3[:, :], a2[:, :], ALU.mult, [a3, a2], [a3])
            k.scan(a2[:, :], a1[:, 0:T], a3[:, :], 0.0, ALU.mult, ALU.add, [a1, a3], [a2])
            k.tt('vector', yab[:, :], a2[:, :], sga[:, :], ALU.mult, [a2, sga], [yab])
            k.dma('sync', yT[g * 128:(g + 1) * 128, :], yab[:, :], [yab], [yT])
            k.memset('vector', a1[:, 0:3], 0.0, [a1])

    blocks = []
    for g in range(16):
        blocks.append([(0, g * 128, 128)])
        blocks.append([(0, 2048 + g * 128, 128)])
    linear_fm(hnT, ab_in_w, blocks, wslots, evac_rg, 'rg')
    P.barrier()
    sb.reset(m)

    sb.reset(l0_mark)
    wslots = [sb.alloc([128, 32, 128], BF16, f'w{i}') for i in range(2)]
    m = sb.mark()
    ev = [sb.alloc([128, T], F32, f'ev{i}') for i in range(1)]
    evb = [sb.alloc([128, T], BF16, f'evb{i}') for i in range(3)]
    wukT = sb.alloc([128, 16, 512], BF16, 'wukT')
    wuk_ld = sb.alloc([128, 4, 128], BF16, 'wuk_ld')
    cnt = [0]

    for h in range(16):
        k.dma('gpsimd', wuk_ld[:, :, :], w_uk[h, :, :].rearrange("(cc p) d -> p cc d", p=128), [], [wuk_ld])
        bi = h % 8
        bb = bank_bf(bi)
        for cc in range(4):
            k.tr(bb[:, cc * 128:(cc + 1) * 128], wuk_ld[:, cc, :], ident_b, [wuk_ld, cbf], [pb[bi]])
        k.cpa(wukT[:, h, :], bb[:, 0:512], [pb[bi]], [wukT])

    def evac_ckv(bi, grp, ncols):
        t_ = ev[0]
        k.cpa(t_[:, :], grp.ap, [grp.bufs], [t_])
        k.dma('sync', ckv_raw[bi * 128:(bi + 1) * 128, :], t_[:, :], [t_], [ckv_raw])
    linear_fm(hnT, ab_in_w, [[(0, 6144 + i * 128, 128)] for i in range(4)], wslots, evac_ckv, 'ckv')

    def evac_q(h, grp, ncols):
        qb = evb[2]
        k.cp('scalar', qb[:, :], grp.ap, [grp.bufs], [qb])
        for cc in range(4):
            g2 = G[(h + 1 + cc) % 2]
            for tc in range(4):
                k.mm(g2.ap[:, tc * 512:(tc + 1) * 512], wukT[:, h, cc * 128:(cc + 1) * 128],
                     qb[:, tc * 512:(tc + 1) * 512], True, True, [wukT, qb], [g2.bufs])
            ob = evb[cc % 2]
            if cc % 2 == 0:
                k.act(ob[:, :], g2.ap, AF.Copy, [g2.bufs], [ob], scale=float(128 ** -0.5))
            else:
                k.ts('vector', ob[:, :], g2.ap, float(128 ** -0.5), None, ALU.mult, None, [g2.bufs], [ob])
            k.dma('sync', qlat_s[h, cc, :, :], ob[:, :], [ob], [qlat_s])
    linear_fm(hnT, ab_in_w, [[(0, 4096 + h * 128, 128)] for h in range(16)], wslots, evac_q, 'q')

    def evac_qi(bi, grp, ncols):
        ob = evb[bi % 3]
        k.cpa(ob[:, :], grp.ap, [grp.bufs], [ob])
        k.dma('sync', qiT_s[bi * 128:(bi + 1) * 128, :], ob[:, :], [ob], [qiT_s])
    linear_fm(hnT, ab_in_w, [[(0, 6656 + i * 128, 128)] for i in range(16)], wslots, evac_qi, 'qi')

    def evac_ki(bi, grp, ncols):
        k.cp('scalar', kiT2[:, :], grp.ap, [grp.bufs], [kiT2])
    linear_fm(hnT, ab_in_w, [[(0, 8704, 64), (64, 8704, 64)]], wslots, evac_ki, 'ki')

    def evac_wi(bi, grp, ncols):
        t_ = ev[0]
        k.cp('scalar', t_[0:32, :], grp.ap[0:32, :], [grp.bufs], [t_])
        for tt in range(16):
            bi2 = 4 + tt % 4
            k.tr(bank(bi2)[:, 0:32], t_[0:32, tt * 128:(tt + 1) * 128], ident_f[0:32, 0:32], [t_, cst], [pb[bi2]])
            k.cp('vector', wi_tok[:, tt, :], bank(bi2)[:, 0:32], [pb[bi2]], [wi_tok])
    linear_fm(hnT, ab_in_w, [[(0, 8768, 32)]], wslots, evac_wi, 'wi')

    def evac_gb(bi, grp, ncols):
        ob = evb[bi % 3]
        k.act(ob[:, :], grp.ap, AF.Silu, [grp.bufs], [ob])
        k.dma('sync', sgb_s[bi * 128:(bi + 1) * 128, :], ob[:, :], [ob], [sgb_s])
    linear_fm(hnT, ab_in_w, [[(0, 8800 + i * 128, 128)] for i in range(16)], wslots, evac_gb, 'gb')
    P.barrier()
    sb.reset(m)

    sb.reset(base_mark)
    ckvT = sb.alloc([128, 4, T], BF16, 'ckvT')
    ckv_tok = sb.alloc([128, 16, 512], BF16, 'ckv_tok')
    wuv = sb.alloc([128, 16, 4, 128], BF16, 'wuv')
    thr_c = sb.alloc([128, 1], F32, 'thr_c')
    k.memset('vector', thr_c[:, :], -1.0e29, [thr_c])
    for h in range(16):
        k.dma('gpsimd', wuv[:, h, :, :], w_uv[h, :, :].rearrange("(cc p) d -> p cc d", p=128), [], [wuv])
    m2 = sb.mark()
    craw = sb.alloc([128, 4, T], F32, 'craw')
    csq = sb.alloc([128, 4, T], F32, 'csq')
    rstd = sb.alloc([128, T], F32, 'rstd')
    k.dma('sync', craw[:, :, :], ckv_raw[:, :].rearrange("(cc p) t -> p cc t", p=128), [ckv_raw], [craw])
    k.act(csq[:, :, :], craw[:, :, :], AF.Square, [craw], [csq])
    for tc in range(4):
        for cc in range(4):
            k.mm(G[0].ap[:, tc * 512:(tc + 1) * 512], ones_f, csq[:, cc, tc * 512:(tc + 1) * 512],
                 cc == 0, cc == 3, [cst, csq], [G[0].bufs])
    k.ts('vector', rstd[:, :], G[0].ap, 1.0 / 512, 1e-6, ALU.mult, ALU.add, [G[0].bufs], [rstd])
    k.act(rstd[:, :], rstd[:, :], AF.Sqrt, [rstd], [rstd])
    k.recip(rstd[:, :], rstd[:, :], [rstd], [rstd])
    for cc in range(4):
        k.stt(ckvT[:, cc, :], craw[:, cc, :], ppc('kvn', cc), rstd[:, :], ALU.mult, ALU.mult,
              [craw, pp, rstd], [ckvT])
    for tt in range(16):
        bi = tt % 8
        bb = bank_bf(bi)
        for cc in range(4):
            k.tr(bb[:, cc * 128:(cc + 1) * 128], ckvT[:, cc, tt * 128:(tt + 1) * 128], ident_b,
                 [ckvT, cbf], [pb[bi]])
        k.cpa(ckv_tok[:, tt, :], bb[:, 0:512], [pb[bi]], [ckv_tok])
    P.barrier()
    sb.reset(m2)

    acc2 = [sb.alloc([128, T], F32, f'acc{i}') for i in range(2)]
    wk = [sb.alloc([128, T], F32, f'wk{i}') for i in range(2)]
    rt = [sb.alloc([128, T], F32, f'rt{i}') for i in range(2)]
    m8 = sb.alloc([128, 8], F32, 'm8')
    m01 = sb.alloc([128, T], BF16, 'm01')
    maskT2 = [sb.alloc([128, 16, 128], BF16, f'maskT{i}') for i in range(2)]
    qi_t = [sb.alloc([128, 16, 128], BF16, f'qi_t{i}') for i in range(2)]
    ql_t = [sb.alloc([128, 16, 4, 128], BF16, f'ql_t{i}') for i in range(2)]
    sgb_t = [sb.alloc([128, 16, 128], BF16, f'sgb_t{i}') for i in range(2)]
    eb = [sb.alloc([128, 512], BF16, f'eb{i}') for i in range(2)]
    ptb = [sb.alloc([128, 512], BF16, f'ptb{i}') for i in range(2)]
    rden = sb.alloc([128, 512], F32, 'rden')
    olb = sb.alloc([128, 4, 512], BF16, 'olb')
    yt1 = sb.alloc([128, 512], F32, 'yt1')
    yob = [sb.alloc([128, 4, 128], BF16, f'yob{i}') for i in range(2)]

    def steps_SC(i):
        S = (i + 1) * 128
        qit = qi_t[i % 2]
        acc = acc2[i % 2]
        nsc = (S + 511) // 512
        st = []

        def s_load():
            k.dma('sync', qit[:, :, :], qiT_s[:, i * 128:(i + 1) * 128].rearrange("(b p) t -> p b t", p=128),
                  [qiT_s], [qit])
        st.append(s_load)

        def mk_head(hh):
            def f():
                blk, half = hh // 2, hh % 2
                r_ = rt[hh % 2]
                for sc in range(nsc):
                    w_ = min(512, S - sc * 512)
                    k.mm(bank(1)[:, 0:w_], qit[half * 64:(half + 1) * 64, blk, :],
                         kiT2[half * 64:(half + 1) * 64, sc * 512:sc * 512 + w_], True, True, [qit, kiT2], [pb[1]])
                    k.act(r_[:, sc * 512:sc * 512 + w_], bank(1)[:, 0:w_], AF.Relu, [pb[1]], [r_])
                if hh == 0:
                    k.ts('vector', acc[:, 0:S], r_[:, 0:S], wi_tok[:, i, 0:1], None, ALU.mult, None,
                         [r_, wi_tok], [acc])
                else:
                    k.stt(acc[:, 0:S], r_[:, 0:S], wi_tok[:, i, hh:hh + 1], acc[:, 0:S], ALU.mult, ALU.add,
                          [r_, wi_tok, acc], [acc])
            return f
        for hh in range(32):
            st.append(mk_head(hh))

        def s_bias():
            k.tt('vector', acc[:, i * 128:S], acc[:, i * 128:S], cbias_f, ALU.add, [acc, cst], [acc])
        st.append(s_bias)
        return st

    def steps_TK(i):
        S = (i + 1) * 128
        acc = acc2[i % 2]
        maskT = maskT2[i % 2]
        qlt = ql_t[i % 2]
        sgt = sgb_t[i % 2]
        st = []

        def s_load():
            k.dma('sync', qlt[:, :, :, :], qlat_s[:, :, :, i * 128:(i + 1) * 128].rearrange("h c p t -> p h c t"),
                  [qlat_s], [qlt])
            k.dma('sync', sgt[:, :, :], sgb_s[:, i * 128:(i + 1) * 128].rearrange("(b p) t -> p b t", p=128),
                  [sgb_s], [sgt])
        st.append(s_load)
        if i >= 2:
            def mk_round(r):
                def f():
                    cur = acc if r == 0 else wk[(r - 1) % 2]
                    k.P.add('vector', (lambda c, S_: (lambda e: e.max(m8[:, :], c[:, 0:S_])))(cur, S), bl([cur]), bl([m8]))
                    if r < 31:
                        nxt = wk[r % 2]
                        k.P.add('vector', (lambda c, n, S_: (lambda e: e.match_replace(n[:, 0:S_], m8[:, :], c[:, 0:S_], -3.0e38)))(cur, nxt, S),
                                bl([cur, m8]), bl([nxt]))
                return f
            for r in range(32):
                st.append(mk_round(r))

        def s_mask():
            if i >= 2:
                thr, thr_b = m8[:, 7:8], [m8]
            else:
                thr, thr_b = thr_c[:, 0:1], [thr_c]
            k.ts('vector', m01[:, 0:S], acc[:, 0:S], thr, None, ALU.is_lt, None, [acc] + thr_b, [m01])
            k.ts('vector', m01[:, 0:S], m01[:, 0:S], -30000.0, None, ALU.mult, None, [m01], [m01])
            for j0 in range(0, i + 1, 8):
                bb = bank_bf(7)
                nj = min(8, i + 1 - j0)
                for j in range(j0, j0 + nj):
                    k.tr(bb[:, (j - j0) * 128:(j - j0 + 1) * 128], m01[:, j * 128:(j + 1) * 128], ident_b,
                         [m01, cbf], [pb[7]])
                k.cp('scalar', maskT[:, j0:j0 + nj, :], bb[:, 0:nj * 128].rearrange("p (a b) -> p a b", a=nj),
                     [pb[7]], [maskT])
        st.append(s_mask)
        return st

    def steps_AT(i):
        maskT = maskT2[i % 2]
        qlt = ql_t[i % 2]
        sgt = sgb_t[i % 2]
        st = []

        def L(hg, j):
            for cc in range(4):
                k.mm(bank(0).rearrange("p (a b) -> p a b", a=4), ckvT[:, cc, j * 128:(j + 1) * 128],
                     qlt[:, hg * 4:(hg + 1) * 4, cc, :], cc == 0, False, [ckvT, qlt], [pb[0]])
            k.mm(bank(0).rearrange("p (a b) -> p a b", a=4), ident_b,
                 maskT[:, j:j + 1, :].broadcast_to([128, 4, 128]), False, True, [cbf, maskT], [pb[0]])

        def mk_j(hg, j):
            def f():
                if j == 0:
                    L(hg, 0)
                p_ = ptb[j % 2]
                k.act(p_[:, :], bank(0), AF.Exp, [pb[0]], [p_])
                if j + 1 <= i:
                    L(hg, j + 1)
                for cc in range(4):
                    k.mm(bank(2 + cc), ckv_tok[:, j, cc * 128:(cc + 1) * 128], p_[:, :],
                         j == 0, j == i, [ckv_tok, p_], [pb[2 + cc]])
                k.mm(bank(6), ones_b, p_[:, :], j == 0, j == i, [cbf, p_], [pb[6]])
            return f

        def mk_tail(hg):
            def f():
                k.act(rden[:, :], bank(6), AF.Ln, [pb[6]], [rden])
                k.act(rden[:, :], rden[:, :], AF.Exp, [rden], [rden], scale=-1.0)
                for cc in range(4):
                    k.cp('scalar', olb[:, cc, :], bank(2 + cc), [pb[2 + cc]], [olb])
                for hl in range(4):
                    h = hg * 4 + hl
                    for cc in range(4):
                        k.mm(bank(7)[:, hl * 128:(hl + 1) * 128], wuv[:, h, cc, :], olb[:, cc, hl * 128:(hl + 1) * 128],
                             cc == 0, cc == 3, [wuv, olb], [pb[7]])
                k.cp('scalar', yt1[:, :], bank(7), [pb[7]], [yt1])
                k.tt('gpsimd', yt1[:, :], yt1[:, :], rden[:, :], ALU.mult, [yt1, rden], [yt1])
                yo = yob[hg % 2]
                k.tt('gpsimd', yo[:, :, :], yt1[:, :].rearrange("p (a b) -> p a b", a=4), sgt[:, hg * 4:(hg + 1) * 4, :],
                     ALU.mult, [yt1, sgt], [yo])
                k.dma('sync', yT[2048 + hg * 512:2048 + (hg + 1) * 512, i * 128:(i + 1) * 128].rearrange("(a p) t -> p a t", p=128),
                      yo[:, :, :], [yo], [yT])
            return f
        for hg in range(4):
            for j in range(i + 1):
                st.append(mk_j(hg, j))
            st.append(mk_tail(hg))
        return st

    def merge(lists):
        items = []
        for li, l in enumerate(lists):
            n = len(l)
            for idx, f in enumerate(l):
                items.append(((idx + 0.5) / n, li, idx, f))
        items.sort(key=lambda t: (t[0], t[1], t[2]))
        return [t[3] for t in items]

    n_qt = 16
    for f in steps_SC(0) + steps_TK(0) + steps_SC(1):
        f()
    for s_ in range(n_qt):
        lists = [steps_AT(s_)]
        if s_ + 1 < n_qt:
            lists.append(steps_TK(s_ + 1))
        if s_ + 2 < n_qt:
            lists.append(steps_SC(s_ + 2))
        for f in merge(lists):
            f()
    P.barrier()
    sb.reset(base_mark)

    def out_proj(Wout, res_src, dst):
        m = sb.mark()
        yres = sb.alloc([128, 32, 1024], BF16, 'yres')
        wo = [sb.alloc([128, 32, 512], BF16, f'wo{i}') for i in range(2)]
        xr = [sb.alloc([128, 512], F32, f'xr{i}') for i in range(4)]
        n = 0
        for th in range(2):
            for q4 in range(4):
                k.dma('sync', yres[:, q4 * 8:(q4 + 1) * 8, :],
                      yT[q4 * 1024:(q4 + 1) * 1024, th * 1024:(th + 1) * 1024].rearrange("(fc p) t -> p fc t", p=128),
                      [yT], [yres])
            for cb in range(8):
                w_ = wo[n % 2]
                n += 1
                for hf in range(2):
                    k.dma('gpsimd', w_[:, hf * 16:(hf + 1) * 16, :],
                          Wout[hf * 2048:(hf + 1) * 2048, cb * 512:(cb + 1) * 512].rearrange("(fc p) c -> p fc c", p=128),
                          [], [w_])
                for t8 in range(8):
                    tt = th * 8 + t8
                    bi = (cb * 8 + t8) % 8
                    xr_ = xr[(cb * 8 + t8) % 4]
                    k.dma('sync', xr_[:, :], res_src[tt * 128:(tt + 1) * 128, cb * 512:(cb + 1) * 512], [res_src], [xr_])
                    for fc in range(32):
                        k.mm(bank(bi), yres[:, fc, t8 * 128:(t8 + 1) * 128], w_[:, fc, :],
                             fc == 0, fc == 31, [yres, w_], [pb[bi]])
                    k.tt('vector', xr_[:, :], bank(bi), xr_[:, :], ALU.add, [pb[bi], xr_], [xr_])
                    k.dma('sync', dst[tt * 128:(tt + 1) * 128, cb * 512:(cb + 1) * 512], xr_[:, :], [xr_], [dst])
        P.barrier()
        sb.reset(m)

    out_proj(ab_out_w, x, h1)

    if stop_after == 'l0':
        P.emit()
        return nc

    sb.reset(base_mark)
    hnT = sb.alloc([128, 32, T], BF16, 'hnT1')
    phase_norm_T(h1, norm_w[1, :], hnT)
    l1_mark = sb.mark()
    wslots = [sb.alloc([128, 32, 128], BF16, f'w{i}') for i in range(2)]
    m = sb.mark()
    evb = [sb.alloc([128, T], BF16, f'evb{i}') for i in range(3)]
    a1 = sb.alloc([128, T + 4], F32, 'a1')
    a2 = sb.alloc([128, T], F32, 'a2')
    a3 = sb.alloc([128, T], F32, 'a3')
    rstb = sb.alloc([128, T], BF16, 'rstb')
    sm = sb.alloc([128, 160], F32, 'sm')
    k.dma('gpsimd', rstb[:, :], rst_d[:, :], [], [rstb])
    k.memset('vector', a1[:, 0:3], 0.0, [a1])
    lb_ = sm[:, 0:16]
    oml = sm[:, 16:32]
    a_b = sm[:, 32:64]
    o_l, _ = PP['lbs']
    k.tt('vector', lb_, pp[:, o_l + 16:o_l + 32], pp[:, o_l:o_l + 16], ALU.subtract, [pp], [sm])
    k.act(lb_, lb_, AF.Sigmoid, [sm], [sm])
    k.ts('vector', oml, lb_, -1.0, 1.0, ALU.mult, ALU.add, [sm], [sm])
    k.dma('sync', a_b, a_log[:].partition_broadcast(128), [], [sm])
    k.act(a_b, a_b, AF.Exp, [sm], [sm])
    k.ts('vector', a_b, a_b, -1.0, None, ALU.mult, None, [sm], [sm])

    def evac_z(bi, grp, ncols):
        ob = evb[bi % 3]
        k.act(ob[:, :], grp.ap, AF.Silu, [grp.bufs], [ob])
        k.dma('sync', sz_s[bi * 128:(bi + 1) * 128, :], ob[:, :], [ob], [sz_s])
    linear_fm(hnT, cd_in_w, [[(0, i * 128, 128)] for i in range(16)], wslots, evac_z, 'z')

    def evac_xbc(bi, grp, ncols):
        conv_block(grp, a1, a2, 'cw1', 'cb1', bi)
        ob = evb[bi % 3]
        k.act(ob[:, :], a2[:, :], AF.Silu, [a2], [ob])
        if bi < 16:
            k.dma('sync', xc_s[bi * 128:(bi + 1) * 128, :], ob[:, :], [ob], [xc_s])
        elif bi < 20:
            k.dma('sync', BT_s[(bi - 16) * 128:(bi - 15) * 128, :], ob[:, :], [ob], [BT_s])
        else:
            k.dma('sync', CT_s[(bi - 20) * 128:(bi - 19) * 128, :], ob[:, :], [ob], [CT_s])
    linear_fm(hnT, cd_in_w, [[(0, 2048 + i * 128, 128)] for i in range(24)], wslots, evac_xbc, 'xbc')

    def evac_dt(bi, grp, ncols):
        k.act(a3[0:32, :], grp.ap[0:32, :], AF.Exp, [grp.bufs, pp], [a3], bias=pp[0:32, PP['dtb'][0]:PP['dtb'][0] + 1])
        k.act(a3[0:32, :], a3[0:32, :], AF.Ln, [a3], [a3], bias=1.0)
        for tt in range(16):
            bi2 = 4 + tt % 4
            k.tr(bank(bi2)[:, 0:32], a3[0:32, tt * 128:(tt + 1) * 128], ident_f[0:32, 0:32], [a3, cst], [pb[bi2]])
            k.cp('vector', dt_tok[:, tt, :], bank(bi2)[:, 0:32], [pb[bi2]], [dt_tok])
        k.tt('vector', dtA_tok[:, :, :], dt_tok[:, :, :], a_b.unsqueeze(1).broadcast_to([128, 16, 32]), ALU.mult,
             [dt_tok, sm], [dtA_tok])
    linear_fm(hnT, cd_in_w, [[(0, 5120, 32)]], wslots, evac_dt, 'dt')

    def evac_fq(bi, grp, ncols):
        h = bi // 2
        if bi % 2 == 0:
            k.act(a1[:, 0:T], grp.ap, AF.Sigmoid, [grp.bufs], [a1])
            k.ts('vector', a1[:, 0:T], a1[:, 0:T], oml[:, h:h + 1], lb_[:, h:h + 1], ALU.mult, ALU.add, [a1, sm], [a1])
            k.act(a2[:, :], a1[:, 0:T], AF.Ln, [a1], [a2])
            k.scan(a3[:, :], rstb[:, :], a2[:, :], 0.0, ALU.mult, ALU.add, [rstb, a2], [a3])
            k.ts('vector', a1[:, 0:T], a1[:, 0:T], -1.0, 1.0, ALU.mult, ALU.add, [a1], [a1])
            k.act(a2[:, :], a3[:, :], AF.Exp, [a3], [a2], scale=-1.0)
            ob = evb[0]
            k.tt('vector', ob[:, :], a1[:, 0:T], a2[:, :], ALU.mult, [a1, a2], [ob])
            k.dma('sync', kt_s[h * 128:(h + 1) * 128, :], ob[:, :], [ob], [kt_s])
            a3v = a3[:, :].rearrange("p (c l) -> p c l", c=16)
            a2v = a2[:, :].rearrange("p (c l) -> p c l", c=16)
            k.tt('vector', a2v, a3v[:, :, 127:128].broadcast_to([128, 16, 128]), a3v, ALU.subtract, [a3], [a2])
            k.act(a2[:, :], a2[:, :], AF.Exp, [a2], [a2])
            ob = evb[1]
            k.tt('vector', ob[:, :], a1[:, 0:T], a2[:, :], ALU.mult, [a1, a2], [ob])
            k.dma('sync', kd_s[h * 128:(h + 1) * 128, :], ob[:, :], [ob], [kd_s])
            k.act(ebl[:, h, :], a3v[:, :, 127], AF.Exp, [a3], [ebl])
        else:
            k.act(a1[:, 0:T], grp.ap, AF.Silu, [grp.bufs], [a1])
            k.act(a2[:, :], a3[:, :], AF.Exp, [a3], [a2])
            ob = evb[2]
            k.tt('vector', ob[:, :], a1[:, 0:T], a2[:, :], ALU.mult, [a1, a2], [ob])
            k.dma('sync', qt_s[h * 128:(h + 1) * 128, :], ob[:, :], [ob], [qt_s])
    blocks = []
    for h in range(16):
        blocks.append([(0, 7200 + h * 128, 128)])
        blocks.append([(0, 5152 + h * 128, 128)])
    linear_fm(hnT, cd_in_w, blocks, wslots, evac_fq, 'fq')

    def evac_gd(bi, grp, ncols):
        ob = evb[bi % 3]
        k.act(ob[:, :], grp.ap, AF.Silu, [grp.bufs], [ob])
        k.dma('sync', sgd_s[bi * 128:(bi + 1) * 128, :], ob[:, :], [ob], [sgd_s])
    linear_fm(hnT, cd_in_w, [[(0, 11296 + i * 128, 128)] for i in range(16)], wslots, evac_gd, 'gd')
    P.barrier()
    sb.reset(l1_mark)

    wo = [sb.alloc([128, 32, 256], BF16, f'wv{i}') for i in range(2)]
    vb = [sb.alloc([128, 256], BF16, f'vb{i}') for i in range(4)]
    n = 0
    for sl in range(8):
        w_ = wo[sl % 2]
        k.dma('gpsimd', w_[:, :, :], cd_in_w[:, 9248 + sl * 256:9248 + (sl + 1) * 256].rearrange("(kc p) c -> p kc c", p=128),
              [], [w_])
        for tt in range(16):
            bi = n % 8
            v_ = vb[n % 4]
            n += 1
            for kc in range(32):
                k.mm(bank(bi)[:, 0:256], hnT[:, kc, tt * 128:(tt + 1) * 128], w_[:, kc, :], kc == 0, kc == 31,
                     [hnT, w_], [pb[bi]])
            k.cpa(v_[:, :], bank(bi)[:, 0:256], [pb[bi]], [v_])
            k.dma('sync', v_s[tt * 128:(tt + 1) * 128, sl * 256:(sl + 1) * 256], v_[:, :], [v_], [v_s])
    P.barrier()
    sb.reset(base_mark)

    sprev = sb.alloc([128, 4, 512], F32, 'sprev')
    sprev_b = sb.alloc([128, 4, 512], BF16, 'sprev_b')
    hprev = sb.alloc([128, 16, 128], F32, 'hprev')
    hprev_b = sb.alloc([128, 16, 128], BF16, 'hprev_b')
    Db = sb.alloc([128, 32], F32, 'Db')
    k.memset('vector', sprev[:, :, :], 0.0, [sprev])
    k.memset('vector', sprev_b[:, :, :], 0.0, [sprev_b])
    k.memset('vector', hprev[:, :, :], 0.0, [hprev])
    k.memset('vector', hprev_b[:, :, :], 0.0, [hprev_b])
    k.dma('sync', Db[:, :], d_skip[:].partition_broadcast(128), [], [Db])

    def dbl(shape, dt, name):
        return [sb.alloc(shape, dt, f'{name}{i}') for i in range(2)]
    xcT_c = dbl([128, 16, 128], BF16, 'xcT_c')
    BT_c = dbl([128, 4, 128], BF16, 'BT_c')
    CT_c = dbl([128, 4, 128], BF16, 'CT_c')
    sz_c = dbl([128, 16, 128], BF16, 'sz_c')
    qt_c = dbl([128, 16, 128], BF16, 'qt_c')
    kt_c = dbl([128, 16, 128], BF16, 'kt_c')
    kd_c = dbl([128, 16, 128], BF16, 'kd_c')
    sgd_c = dbl([128, 16, 128], BF16, 'sgd_c')
    v_c = dbl([128, 2048], BF16, 'v_c')
    x_tok = sb.alloc([128, 32, 64], BF16, 'x_tok')
    B_tok = sb.alloc([128, 4, 128], BF16, 'B_tok')
    sml = sb.alloc([128, 256], F32, 'sml')
    xdt = sb.alloc([128, 32, 64], BF16, 'xdt')
    xdtd = sb.alloc([128, 32, 64], BF16, 'xdtd')
    cbm = sb.alloc([128, 4, 128], F32, 'cbm')
    yoff = sb.alloc([128, 32, 64], F32, 'yoff')
    Lm4 = [sb.alloc([128, 4, 128], F32, f'Lm{i}') for i in range(2)]
    seg4 = [sb.alloc([128, 512], F32, f'seg{i}') for i in range(2)]
    MT4 = [sb.alloc([128, 4, 128], BF16, f'MT{i}') for i in range(2)]
    yv = sb.alloc([128, 2048], F32, 'yv')
    gT = sb.alloc([128, 16, 128], F32, 'gT')
    sq = sb.alloc([128, 16, 128], F32, 'sq')
    t2v = sq[:, :, :].rearrange("p a b -> p (a b)")
    rs = sb.alloc([128, 4, 128], F32, 'rs')
    ycb = dbl([128, 16, 128], BF16, 'ycb')
    kd_tok = sb.alloc([128, 16, 128], BF16, 'kd_tok')
    attm4 = [sb.alloc([128, 4, 128], BF16, f'attm{i}') for i in range(2)]
    rsh = sb.alloc([128, 2048], F32, 'rsh')
    ydb = dbl([128, 16, 128], BF16, 'ydb')
    o_hg, _ = PP['hgn']

    osb = sb.alloc([128, 2048], F32, 'osb')
    sqh = sb.alloc([128, 2048], F32, 'sqh')

    def chunk_steps(c):
        cs = slice(c * 128, (c + 1) * 128)
        p2 = c % 2
        xT, Bc, Cc, szc = xcT_c[p2], BT_c[p2], CT_c[p2], sz_c[p2]
        qtc, ktc, kdc, sgc, vc = qt_c[p2], kt_c[p2], kd_c[p2], sgd_c[p2], v_c[p2]
        eacs = sml[:, 64:96]
        dte = sml[:, 96:128]
        cdb = sml[:, 128:160]
        x_tok2 = x_tok[:, :, :].rearrange("p a b -> p (a b)")
        ssd = []
        hg_ = []

        def ld(dst, src):
            k.dma('sync', dst[:, :, :], src[:, cs].rearrange("(b p) t -> p b t", p=128), [src], [dst])

        def s_load_ssd():
            ld(xT, xc_s)
            ld(Bc, BT_s)
            ld(Cc, CT_s)
            ld(szc, sz_s)
        ssd.append(s_load_ssd)

        def s_pro1():
            for half in range(2):
                bb = bank_bf(4)
                for j in range(8):
                    k.tr(bb[:, j * 128:(j + 1) * 128], xT[:, half * 8 + j, :], ident_b, [xT, cbf], [pb[4]])
                k.cpa(x_tok2[:, half * 1024:(half + 1) * 1024], bb[:, 0:1024], [pb[4]], [x_tok])
            bb = bank_bf(4)
            for g in range(4):
                k.tr(bb[:, g * 128:(g + 1) * 128], Bc[:, g, :], ident_b, [Bc, cbf], [pb[4]])
            k.cpa(B_tok[:, :, :].rearrange("p a b -> p (a b)"), bb[:, 0:512], [pb[4]], [B_tok])
            k.mm(bank(4)[:, 0:32], le_f, dtA_tok[:, c, :], True, True, [cst, dtA_tok], [pb[4]])
            k.mm(bank(4)[:, 32:64], ones_f, dtA_tok[:, c, :], True, True, [cst, dtA_tok], [pb[4]])
            k.cp('scalar', sml[:, 0:64], bank(4)[:, 0:64], [pb[4]], [sml])
            k.act(sml[:, 64:96], sml[:, 0:32], AF.Exp, [sml], [sml])
            k.tt('vector', sml[:, 96:128], sml[:, 32:64], sml[:, 0:32], ALU.subtract, [sml], [sml])
            k.act(sml[:, 96:128], sml[:, 96:128], AF.Exp, [sml], [sml])
            k.act(sml[:, 128:160], sml[:, 32:64], AF.Exp, [sml], [sml])
            k.tt('vector', xdt[:, :, :], x_tok[:, :, :], dt_tok[:, c, :].unsqueeze(2).broadcast_to([128, 32, 64]), ALU.mult,
                 [x_tok, dt_tok], [xdt])
            k.tt('vector', xdtd[:, :, :], xdt[:, :, :], dte.unsqueeze(2).broadcast_to([128, 32, 64]), ALU.mult,
                 [xdt, sml], [xdtd])
        ssd.append(s_pro1)

        def s_pro2():
            for g in range(4):
                k.mm(bank(4)[:, g * 128:(g + 1) * 128], Bc[:, g, :], Cc[:, g, :], True, True, [Bc, Cc], [pb[4]])
            k.tt('vector', cbm[:, :, :], bank(4).rearrange("p (a b) -> p a b", a=4),
                 le_f.unsqueeze(1).broadcast_to([128, 4, 128]), ALU.mult, [pb[4], cst], [cbm])
            for g in range(4):
                k.mm(bank(4), Cc[:, g, :], sprev_b[:, g, :], True, True, [Cc, sprev_b], [pb[4]])
                k.tt('vector', yoff[:, g * 8:(g + 1) * 8, :], bank(4).rearrange("p (a b) -> p a b", a=8),
                     eacs[:, g * 8:(g + 1) * 8].unsqueeze(2).broadcast_to([128, 8, 64]), ALU.mult, [pb[4], sml], [yoff])
        ssd.append(s_pro2)

        def mk_ssd_batch(bq):
            def f():
                hd0 = bq * 4
                g = bq // 2
                yb_ = g % 2
                L4 = Lm4[bq % 2]
                k.tt('gpsimd' if bq % 2 else 'vector', L4[:, :, :], gt_f.unsqueeze(1).broadcast_to([128, 4, 128]),
                     dtA_tok[:, c, hd0:hd0 + 4].unsqueeze(2).broadcast_to([128, 4, 128]), ALU.mult, [cst, dtA_tok], [L4])
                for q in range(4):
                    k.mm(bank(6)[:, q * 128:(q + 1) * 128], L4[:, q, :], le_f, True, True, [L4, cst], [pb[6]])
                s4 = seg4[bq % 2]
                k.act(s4[:, :], bank(6), AF.Exp, [pb[6]], [s4])
                M4 = MT4[bq % 2]
                k.tt('vector', M4[:, :, :], s4[:, :].rearrange("p (a b) -> p a b", a=4),
                     cbm[:, g:g + 1, :].broadcast_to([128, 4, 128]), ALU.mult, [s4, cbm], [M4])
                for q in range(4):
                    hd = hd0 + q
                    k.mm(bank(yb_)[:, (hd % 8) * 64:(hd % 8 + 1) * 64], M4[:, q, :], xdt[:, hd, :], True, True,
                         [M4, xdt], [pb[yb_]])
                if bq % 2 == 1:
                    k.tt('vector', yv[:, g * 512:(g + 1) * 512], bank(yb_),
                         yoff[:, g * 8:(g + 1) * 8, :].rearrange("p a b -> p (a b)"), ALU.add, [pb[yb_], yoff], [yv])
            return f
        for bq in range(8):
            ssd.append(mk_ssd_batch(bq))

        def s_states():
            for g in range(4):
                k.mm(bank(4), B_tok[:, g, :], xdtd[:, g * 8:(g + 1) * 8, :].rearrange("p a b -> p (a b)"), True, True,
                     [B_tok, xdtd], [pb[4]])
                spv = sprev[:, g, :].rearrange("p (a b) -> p a b", a=8)
                k.tt('vector', spv, spv, cdb[:, g * 8:(g + 1) * 8].unsqueeze(2).broadcast_to([128, 8, 64]), ALU.mult,
                     [sprev, sml], [sprev])
                k.tt('vector', sprev[:, g, :], sprev[:, g, :], bank(4), ALU.add, [sprev, pb[4]], [sprev])
                k.cp('scalar', sprev_b[:, g, :], sprev[:, g, :], [sprev], [sprev_b])
            k.tt('gpsimd', t2v.rearrange("p (a b) -> p a b", a=32), x_tok[:, :, :], Db[:, :].unsqueeze(2).broadcast_to([128, 32, 64]), ALU.mult,
                 [x_tok, Db], [sq])
            k.tt('vector', yv[:, :], yv[:, :], t2v, ALU.add, [yv, sq], [yv])
        ssd.append(s_states)

        def s_epi():
            for q4 in range(4):
                for j in range(4):
                    fc = q4 * 4 + j
                    k.tr(bank(4)[:, j * 128:(j + 1) * 128], yv[:, fc * 128:(fc + 1) * 128], ident_f, [yv, cst], [pb[4]])
                k.tt('vector', gT[:, q4 * 4:(q4 + 1) * 4, :], bank(4).rearrange("p (a b) -> p a b", a=4),
                     szc[:, q4 * 4:(q4 + 1) * 4, :], ALU.mult, [pb[4], szc], [gT])
            k.act(sq[:, :, :], gT[:, :, :], AF.Square, [gT], [sq])
            for g in range(4):
                for j in range(4):
                    k.mm(bank(4)[:, g * 128:(g + 1) * 128], ones_f, sq[:, g * 4 + j, :], j == 0, j == 3, [cst, sq], [pb[4]])
            rs2 = rs[:, :, :].rearrange("p a b -> p (a b)")
            k.ts('vector', rs2, bank(4), 1.0 / 512, 1e-6, ALU.mult, ALU.add, [pb[4]], [rs])
            k.act(rs2, rs2, AF.Sqrt, [rs], [rs])
            k.recip(rs2, rs2, [rs], [rs])
            yc_ = ycb[p2]
            for fc in range(16):
                k.stt(yc_[:, fc, :], gT[:, fc, :], ppc('ssdn', fc), rs[:, fc // 4, :], ALU.mult, ALU.mult,
                      [gT, pp, rs], [yc_])
            k.dma('sync', yT[0:2048, cs].rearrange("(b p) t -> p b t", p=128), yc_[:, :, :], [yc_], [yT])
        ssd.append(s_epi)

        def h_load():
            ld(qtc, qt_s)
            ld(ktc, kt_s)
            ld(kdc, kd_s)
            ld(sgc, sgd_s)
            k.dma('sync', vc[:, :], v_s[cs, :], [v_s], [vc])
        hg_.append(h_load)

        def h_pro():
            for half in range(2):
                bb = bank_bf(5)
                for j in range(8):
                    k.tr(bb[:, j * 128:(j + 1) * 128], kdc[:, half * 8 + j, :], ident_b, [kdc, cbf], [pb[5]])
                k.cpa(kd_tok[:, half * 8:(half + 1) * 8, :].rearrange("p a b -> p (a b)"), bb[:, 0:1024], [pb[5]], [kd_tok])
        hg_.append(h_pro)

        def mk_h_batch(hq):
            def f():
                h0 = hq * 4
                ob_ = 2 + hq % 2
                for q in range(4):
                    h = h0 + q
                    k.mm(bank(7)[:, q * 128:(q + 1) * 128], ktc[:, h, :], qtc[:, h, :], True, True, [ktc, qtc], [pb[7]])
                am4 = attm4[hq % 2]
                k.tt('vector', am4[:, :, :], bank(7).rearrange("p (a b) -> p a b", a=4),
                     le_f.unsqueeze(1).broadcast_to([128, 4, 128]), ALU.mult, [pb[7], cst], [am4])
                for q in range(4):
                    h = h0 + q
                    hs = slice(h * 128, (h + 1) * 128)
                    oc = slice(q * 128, (q + 1) * 128)
                    k.mm(bank(ob_)[:, oc], vc[:, hs], am4[:, q, :], True, False, [vc, am4], [pb[ob_]])
                    k.mm(bank(ob_)[:, oc], hprev_b[:, h, :], qtc[:, h, :], False, True, [hprev_b, qtc], [pb[ob_]])
                for q in range(4):
                    h = h0 + q
                    hs = slice(h * 128, (h + 1) * 128)
                    k.mm(bank(5)[:, q * 128:(q + 1) * 128], kd_tok[:, h, :], vc[:, hs], True, True, [kd_tok, vc], [pb[5]])
                hp4 = hprev[:, h0:h0 + 4, :]
                k.tt('vector', hp4, hp4, ebl[:, h0:h0 + 4, c:c + 1].broadcast_to([128, 4, 128]), ALU.mult,
                     [hprev, ebl], [hprev])
                k.tt('vector', hp4, hp4, bank(5).rearrange("p (a b) -> p a b", a=4), ALU.add, [hprev, pb[5]], [hprev])
                k.cp('scalar', hprev_b[:, h0:h0 + 4, :], hp4, [hprev], [hprev_b])
                k.cp('scalar', osb[:, hq * 512:(hq + 1) * 512], bank(ob_), [pb[ob_]], [osb])
                k.act(sqh[:, hq * 512:(hq + 1) * 512], bank(ob_), AF.Square, [pb[ob_]], [sqh])
            return f
        for hq in range(4):
            hg_.append(mk_h_batch(hq))

        def h_epi():
            for q4 in range(4):
                qs = slice(q4 * 512, (q4 + 1) * 512)
                k.mm(bank(5), ones_f, sqh[:, qs], True, True, [cst, sqh], [pb[5]])
                k.ts('vector', rsh[:, qs], bank(5), 1.0 / 128, 1e-6, ALU.mult, ALU.add, [pb[5]], [rsh])
            k.act(rsh[:, :], rsh[:, :], AF.Sqrt, [rsh], [rsh])
            k.recip(rsh[:, :], rsh[:, :], [rsh], [rsh])
            k.tt('vector', rsh[:, :], osb[:, :], rsh[:, :], ALU.mult, [osb, rsh], [rsh])
            rsh3 = rsh[:, :].rearrange("p (a b) -> p a b", a=16)
            k.tt('gpsimd', rsh3, rsh3, pp[:, o_hg:o_hg + 16].unsqueeze(2).broadcast_to([128, 16, 128]), ALU.mult,
                 [rsh, pp], [rsh])
            yd_ = ydb[p2]
            k.tt('vector', yd_[:, :, :], rsh3, sgc[:, :, :], ALU.mult, [rsh, sgc], [yd_])
            k.dma('sync', yT[2048:4096, cs].rearrange("(b p) t -> p b t", p=128), yd_[:, :, :], [yd_], [yT])
        hg_.append(h_epi)
        return ssd, hg_

    def merge2(lists):
        items = []
        for li, l in enumerate(lists):
            n = len(l)
            for idx, f in enumerate(l):
                items.append(((idx + 0.5) / n, li, idx, f))
        items.sort(key=lambda t: (t[0], t[1], t[2]))
        return [t[3] for t in items]

    for c in range(16):
        a_, b_ = chunk_steps(c)
        for f in merge2([a_, b_]):
            f()
    P.barrier()
    sb.reset(base_mark)

    out_proj(cd_out_w, h1, h2)

    nwb = sb.alloc([128, D], F32, 'fnw')
    xt2 = [sb.alloc([128, D], F32, f'fx{i}') for i in range(2)]
    xo2 = [sb.alloc([128, D], F32, f'fo{i}') for i in range(2)]
    junk = sb.alloc([128, D], BF16, 'fjunk')
    st = sb.alloc([128, 4], F32, 'fst')
    k.dma('sync', nwb[:, :], final_norm[:].partition_broadcast(128), [], [nwb])
    for tt in range(16):
        xt = xt2[tt % 2]
        xo = xo2[tt % 2]
        k.dma('sync', xt[:, :], h2[tt * 128:(tt + 1) * 128, :], [h2], [xt])
        k.act(junk[:, :], xt[:, :], AF.Square, [xt], [junk, st], accum=st[:, 0:1])
        k.ts('vector', st[:, 1:2], st[:, 0:1], 1.0 / D, 1e-6, ALU.mult, ALU.add, [st], [st])
        k.act(st[:, 2:3], st[:, 1:2], AF.Sqrt, [st], [st])
        k.recip(st[:, 3:4], st[:, 2:3], [st], [st])
        k.stt(xo[:, :], xt[:, :], st[:, 3:4], nwb[:, :], ALU.mult, ALU.mult, [xt, st, nwb], [xo])
        k.dma('sync', out_d[tt * 128:(tt + 1) * 128, :], xo[:, :], [xo], [out_d])
    P.emit()
    return nc


_W_KEYS = (("norm_w", None), ("final_norm", None), ("ab_in_w", 0), ("ab_rg_wa", 0), ("ab_rg_wx", 0),
           ("ab_w_uk", 0), ("ab_w_uv", 0), ("ab_out_w", 0), ("cd_in_w", 0), ("cd_a_log", 0),
           ("cd_d_skip", 0), ("cd_out_w", 0))


def make_in_maps(inp, batches):
    common = {}
    for name, idx in _W_KEYS:
        a = np.asarray(inp[name], dtype=np.float32)
        common[name] = np.ascontiguousarray(a if idx is None else a[idx])
    common["pp"] = pack_params(inp)
    cst, rst = make_consts()
    common["cst"] = cst
    common["rst"] = rst
    maps = []
    for b in batches:
        d = dict(common)
        d["x"] = np.ascontiguousarray(np.asarray(inp["x"][b], dtype=np.float32))
        maps.append(d)
    return maps


def kernel(**inputs):
    nc = build()
    in_maps = make_in_maps(inputs, range(8))
    res = run_bass_kernel_spmd(nc, in_maps, core_ids=list(range(8)))
    return np.stack([np.asarray(res.results[b]["out"], dtype=np.float32) for b in range(8)], axis=0)
```

```python
import numpy as np
import concourse.bass as bass
import concourse.mybir as mybir
from concourse.bass_utils import run_bass_kernel_spmd

F32 = mybir.dt.float32
BF16 = mybir.dt.bfloat16
AF = mybir.ActivationFunctionType
ALU = mybir.AluOpType
T = 2048
D = 4096
AB_COLS = 10848
CD_COLS = 13344
EPOCH = 30000
SB_BASE = 16640
SB_CAP = SB_BASE + 207 * 1024
NEG = -1.0e30


class Ins:
    __slots__ = ('eng', 'fn', 'deps', 'signal', 'seq', 'dkey', 'dcnt', 'order', 'isdma')


class Buf:
    __slots__ = ('name', 'w', 'r')

    def __init__(self, name=''):
        self.name = name
        self.w = {}
        self.r = {}


class Tn:
    def __init__(self, h, name=''):
        self.h = h
        self.b = Buf(name)

    def __getitem__(self, k):
        return self.h[k]


class Prog:
    ENGS = ('tensor', 'vector', 'scalar', 'gpsimd', 'sync')

    def __init__(self, nc, ndma=12):
        self.nc = nc
        self.streams = {e: [] for e in self.ENGS}
        self.order = 0
        self.ndma = ndma
        self.rr = {}
        self.dlast = {}
        self.dcount = {}
        self.lastc = {}
        self.barrier_deps = {e: None for e in self.ENGS}

    def add(self, eng, fn, reads=(), writes=(), dma=False):
        ins = Ins()
        ins.eng = eng
        ins.fn = fn
        ins.signal = False
        ins.seq = None
        ins.isdma = dma
        ins.dcnt = 0
        ins.order = self.order
        self.order += 1
        deps = {}

        def need(d):
            k = d.dkey
            if k not in deps or deps[k].order < d.order:
                deps[k] = d
        for b in reads:
            for d in b.w.values():
                need(d)
        for b in writes:
            for d in b.w.values():
                need(d)
            for d in b.r.values():
                need(d)
        bd = self.barrier_deps[eng]
        if bd is not None:
            for d in bd:
                need(d)
            self.barrier_deps[eng] = None
        if dma:
            k = self.rr.get(eng, 0)
            self.rr[eng] = (k + 1) % self.ndma
            ins.dkey = ('d', eng, k)
            prev = self.dlast.get(ins.dkey)
            if prev is not None:
                need(prev)
            self.dlast[ins.dkey] = ins
            ins.dcnt = self.dcount.get(ins.dkey, 0) + 16
            self.dcount[ins.dkey] = ins.dcnt
        else:
            ins.dkey = eng
            if eng == 'tensor':
                deps.pop('tensor', None)
            self.lastc[eng] = ins
        for d in deps.values():
            if not d.isdma:
                d.signal = True
        ins.deps = list(deps.values())
        for b in writes:
            if b.r:
                b.w = {}
                b.r = {}
            b.w[ins.dkey] = ins
        for b in reads:
            b.r[ins.dkey] = ins
        self.streams[eng].append(ins)
        return ins

    def barrier(self):
        deps = list(self.lastc.values()) + list(self.dlast.values())
        for e in self.ENGS:
            self.barrier_deps[e] = list(deps)

    def emit(self):
        nc = self.nc
        self.barrier()
        for e in self.ENGS:
            self.add(e, None)
        nsig = {}
        for e, st in self.streams.items():
            n = 0
            for ins in st:
                if (not ins.isdma) and ins.signal:
                    n += 1
                    ins.seq = n
            nsig[e] = n
        csem = {e: [nc.alloc_semaphore(name=f"c_{e}_{i}") for i in range((nsig[e] + EPOCH - 1) // EPOCH)]
                for e in self.ENGS}
        dsem = {k: nc.alloc_semaphore(name=f"d_{k[1]}_{k[2]}") for k in self.dcount}
        streams = self.streams

        def run(ename, eng):
            waited = {}
            maxep = {}
            for ins in streams[ename]:
                for d in ins.deps:
                    if d.isdma:
                        key = d.dkey
                        sem = dsem[key]
                        val = d.dcnt
                    else:
                        ep = (d.seq - 1) // EPOCH
                        if maxep.get(d.eng, -1) > ep:
                            continue
                        key = (d.eng, ep)
                        sem = csem[d.eng][ep]
                        val = (d.seq - 1) % EPOCH + 1
                    if waited.get(key, 0) >= val:
                        continue
                    eng.wait_ge(sem, val)
                    waited[key] = val
                    if not d.isdma:
                        maxep[d.eng] = max(maxep.get(d.eng, -1), ep)
                if ins.fn is None:
                    continue
                r = ins.fn(eng)
                if ins.isdma:
                    r.then_inc(dsem[ins.dkey], 16)
                elif ins.signal:
                    ep = (ins.seq - 1) // EPOCH
                    r.then_inc(csem[ename][ep], 1)

        with nc.Block() as block:
            @block.tensor
            def _(e):
                run('tensor', e)

            @block.vector
            def _(e):
                run('vector', e)

            @block.scalar
            def _(e):
                run('scalar', e)

            @block.gpsimd
            def _(e):
                run('gpsimd', e)

            @block.sync
            def _(e):
                run('sync', e)


def _size(dt):
    return 2 if dt == BF16 else 4


class SBAlloc:
    def __init__(self, nc):
        self.nc = nc
        self.ptr = SB_BASE
        self.n = 0

    def alloc(self, shape, dtype, name='t'):
        nb = int(np.prod(shape[1:])) * _size(dtype)
        off = (self.ptr + 63) // 64 * 64
        self.ptr = off + nb
        assert self.ptr <= SB_CAP, f"SBUF overflow {self.ptr} at {name}"
        self.n += 1
        h = self.nc.alloc_sbuf_tensor_at(f"{name}_{self.n}", list(shape), dtype, offset=off)
        return Tn(h, name)

    def mark(self):
        return self.ptr

    def reset(self, m):
        self.ptr = m


def bl(x):
    out = []
    for t in x:
        if isinstance(t, Tn):
            out.append(t.b)
        elif isinstance(t, Buf):
            out.append(t)
        else:
            out.extend(t)
    return out


class K:
    def __init__(self, P):
        self.P = P
        self.alt = 0

    def mm(self, out, lhsT, rhs, start, stop, R, W):
        self.P.add('tensor', lambda e: e.matmul(out, lhsT, rhs, start=start, stop=stop), bl(R), bl(W))

    def tr(self, out, in_, ident, R, W):
        self.P.add('tensor', lambda e: e.transpose(out, in_, ident), bl(R), bl(W))

    def act(self, out, in_, func, R, W, bias=None, scale=None, accum=None):
        kw = {}
        if bias is not None:
            kw['bias'] = bias
        if scale is not None:
            kw['scale'] = scale
        if accum is not None:
            kw['accum_out'] = accum
        self.P.add('scalar', lambda e: e.activation(out, in_, func, **kw), bl(R), bl(W))

    def tt(self, eng, out, in0, in1, op, R, W):
        self.P.add(eng, lambda e: e.tensor_tensor(out, in0, in1, op), bl(R), bl(W))

    def ts(self, eng, out, in0, s1, s2, op0, op1, R, W):
        if s2 is None:
            self.P.add(eng, lambda e: e.tensor_scalar(out, in0, s1, None, op0), bl(R), bl(W))
        else:
            self.P.add(eng, lambda e: e.tensor_scalar(out, in0, s1, s2, op0, op1), bl(R), bl(W))

    def stt(self, out, in0, scalar, in1, op0, op1, R, W):
        self.P.add('vector', lambda e: e.scalar_tensor_tensor(out, in0, scalar, in1, op0, op1), bl(R), bl(W))

    def cp(self, eng, out, in_, R, W):
        if eng == 'scalar':
            self.P.add('scalar', lambda e: e.activation(out, in_, AF.Copy), bl(R), bl(W))
        else:
            self.P.add(eng, lambda e: e.tensor_copy(out, in_), bl(R), bl(W))

    def cpa(self, out, in_, R, W):
        self.alt ^= 1
        self.cp('scalar' if self.alt else 'vector', out, in_, R, W)

    def recip(self, out, in_, R, W):
        self.P.add('vector', lambda e: e.reciprocal(out, in_), bl(R), bl(W))

    def scan(self, out, d0, d1, init, op0, op1, R, W):
        self.P.add('vector', lambda e: e.tensor_tensor_scan(out, d0, d1, init, op0, op1), bl(R), bl(W))

    def memset(self, eng, out, val, W):
        self.P.add(eng, lambda e: e.memset(out, val), [], bl(W))

    def dma(self, q, out, in_, R, W):
        self.P.add(q, lambda e: e.dma_start(out=out, in_=in_), bl(R), bl(W), dma=True)


PP = {}
_o = 0
for _n, _w in (('cw0', 64), ('cb0', 16), ('ba', 16), ('bx', 16), ('lam', 16), ('kvn', 4),
               ('cw1', 96), ('cb1', 24), ('dtb', 1), ('ssdn', 16), ('hgn', 16), ('lbs', 32)):
    PP[_n] = (_o, _w)
    _o += _w
NPP = _o
NCST = 640


def pack_params(inp):
    pp = np.zeros((128, NPP), np.float32)

    def put(name, arr):
        o, w = PP[name]
        pp[:arr.shape[0], o:o + w] = arr.reshape(arr.shape[0], -1)
    put('cw0', inp['ab_conv_w'][0].reshape(4, 16, 128).transpose(2, 1, 0))
    put('cb0', inp['ab_conv_b'][0].reshape(16, 128).T)
    put('ba', inp['ab_rg_ba'][0].reshape(16, 128).T)
    put('bx', inp['ab_rg_bx'][0].reshape(16, 128).T)
    put('lam', inp['ab_rg_lambda'][0].reshape(16, 128).T)
    put('kvn', inp['ab_kv_norm'][0].reshape(4, 128).T)
    put('cw1', inp['cd_conv_w'][0].reshape(4, 24, 128).transpose(2, 1, 0))
    put('cb1', inp['cd_conv_b'][0].reshape(24, 128).T)
    put('dtb', inp['cd_dt_bias'][0].reshape(32, 1))
    put('ssdn', inp['cd_ssd_norm'][0].reshape(16, 128).T)
    put('hgn', inp['cd_hgrn_norm'][0].reshape(16, 128).T)
    put('lbs', inp['hgrn_lower_bounds'].reshape(2, 16, 128).transpose(2, 0, 1))
    return pp


def make_consts():
    c = np.zeros((128, NCST), np.float32)
    p = np.arange(128)[:, None]
    j = np.arange(128)[None, :]
    c[:, 0:128] = (p == j)
    c[:, 128:256] = (p <= j)
    c[:, 256:384] = (p > j)
    c[:, 384:512] = np.where(j <= p, 0.0, NEG)
    c[:, 512:640] = 1.0
    rst = np.ones((128, T), np.float32)
    rst[:, ::128] = 0.0
    return c, rst


def build(debug=False, stop_after=None):
    nc = bass.Bass("TRN2", target_bir_lowering=False)
    P = Prog(nc)
    k = K(P)
    sb = SBAlloc(nc)

    def din(name, shape, dt=F32):
        return Tn(nc.dram_tensor(name, list(shape), dt, kind="ExternalInput").ap(), name)

    def dscr(name, shape, dt, out=False):
        kind = "ExternalOutput" if (out or debug) else "Internal"
        return Tn(nc.dram_tensor(name, list(shape), dt, kind=kind).ap(), name)

    x = din("x", [T, D])
    norm_w = din("norm_w", [2, D])
    final_norm = din("final_norm", [D])
    ab_in_w = din("ab_in_w", [D, AB_COLS])
    wa = din("ab_rg_wa", [16, 128, 128])
    wx = din("ab_rg_wx", [16, 128, 128])
    w_uk = din("ab_w_uk", [16, 512, 128])
    w_uv = din("ab_w_uv", [16, 512, 128])
    ab_out_w = din("ab_out_w", [D, D])
    cd_in_w = din("cd_in_w", [D, CD_COLS])
    a_log = din("cd_a_log", [32])
    d_skip = din("cd_d_skip", [32])
    cd_out_w = din("cd_out_w", [D, D])
    pp_d = din("pp", [128, NPP])
    cst_d = din("cst", [128, NCST])
    rst_d = din("rst", [128, T])
    out_d = dscr("out", [T, D], F32, out=True)

    yT = dscr("s_yT", [D, T], BF16)
    h1 = dscr("s_h1", [T, D], F32)
    h2 = dscr("s_h2", [T, D], F32)
    ckv_raw = dscr("s_ckv", [512, T], F32)
    qlat_s = dscr("s_qlat", [16, 4, 128, T], BF16)
    qiT_s = dscr("s_qi", [2048, T], BF16)
    sgb_s = dscr("s_sgb", [2048, T], BF16)
    sz_s = dscr("s_sz", [2048, T], BF16)
    xc_s = dscr("s_xc", [2048, T], BF16)
    BT_s = dscr("s_B", [512, T], BF16)
    CT_s = dscr("s_C", [512, T], BF16)
    qt_s = dscr("s_qt", [2048, T], BF16)
    kt_s = dscr("s_kt", [2048, T], BF16)
    kd_s = dscr("s_kd", [2048, T], BF16)
    sgd_s = dscr("s_sgd", [2048, T], BF16)
    v_s = dscr("s_v", [T, 2048], BF16)

    ps = nc.alloc_psum_tensor("ps", [128, 4096], F32)
    pb = [Buf(f"bank{i}") for i in range(8)]

    def bank(i):
        return ps[:, i * 512:(i + 1) * 512]

    def bank_bf(i):
        return ps[:, i * 512:(i + 1) * 512].bitcast(BF16)

    class Grp:
        def __init__(self, g):
            self.ap = ps[:, g * 2048:(g + 1) * 2048]
            self.bufs = pb[g * 4:(g + 1) * 4]
    G = [Grp(0), Grp(1)]

    cst = sb.alloc([128, NCST], F32, 'cst')
    pp = sb.alloc([128, NPP], F32, 'pp')
    cbf = sb.alloc([128, 384], BF16, 'cbf')
    k.dma('sync', cst[:, :], cst_d[:, :], [], [cst])
    k.dma('sync', pp[:, :], pp_d[:, :], [], [pp])
    k.cp('vector', cbf[:, 0:256], cst[:, 0:256], [cst], [cbf])
    k.cp('vector', cbf[:, 256:384], cst[:, 512:640], [cst], [cbf])
    ident_f = cst[:, 0:128]
    le_f = cst[:, 128:256]
    gt_f = cst[:, 256:384]
    cbias_f = cst[:, 384:512]
    ones_f = cst[:, 512:640]
    ident_b = cbf[:, 0:128]
    le_b = cbf[:, 128:256]
    ones_b = cbf[:, 256:384]

    def ppc(name, j=0, n=1):
        o, w = PP[name]
        return pp[:, o + j:o + j + n]

    kiT2 = sb.alloc([128, T], BF16, 'kiT2')
    wi_tok = sb.alloc([128, 16, 32], F32, 'wi_tok')
    dt_tok = sb.alloc([128, 16, 32], F32, 'dt_tok')
    dtA_tok = sb.alloc([128, 16, 32], F32, 'dtA_tok')
    ebl = sb.alloc([128, 16, 16], F32, 'ebl')
    base_mark = sb.mark()

    def phase_norm_T(src, nw_ap, hnT):
        m = sb.mark()
        nwb = sb.alloc([128, D], F32, 'nwb')
        xt = sb.alloc([128, D], F32, 'xt')
        junk = sb.alloc([128, D], BF16, 'junk')
        xn = sb.alloc([128, D], BF16, 'xn')
        st = sb.alloc([128, 4], F32, 'st')
        k.dma('sync', nwb[:, :], nw_ap.partition_broadcast(128), [], [nwb])
        for tt in range(16):
            k.dma('sync', xt[:, :], src[tt * 128:(tt + 1) * 128, :], [src], [xt])
            k.act(junk[:, :], xt[:, :], AF.Square, [xt], [junk, st], accum=st[:, 0:1])
            k.ts('vector', st[:, 1:2], st[:, 0:1], 1.0 / D, 1e-6, ALU.mult, ALU.add, [st], [st])
            k.act(st[:, 2:3], st[:, 1:2], AF.Sqrt, [st], [st])
            k.recip(st[:, 3:4], st[:, 2:3], [st], [st])
            k.stt(xn[:, :], xt[:, :], st[:, 3:4], nwb[:, :], ALU.mult, ALU.mult, [xt, st, nwb], [xn])
            for g in range(4):
                bi = (tt * 4 + g) % 8
                bb = bank_bf(bi)
                for j in range(8):
                    c = g * 8 + j
                    k.tr(bb[:, j * 128:(j + 1) * 128], xn[:, c * 128:(c + 1) * 128], ident_b,
                         [xn, cbf], [pb[bi]])
                k.cpa(hnT[:, g * 8:(g + 1) * 8, tt * 128:(tt + 1) * 128],
                      bb.rearrange("p (a b) -> p a b", a=8), [pb[bi]], [hnT])
        P.barrier()
        sb.reset(m)

    def linear_fm(hnT, W, blocks, wslots, evac, wtag):
        for bi, segs in enumerate(blocks):
            ws = wslots[bi % len(wslots)]
            ncols = 0
            for (d0, s0, n) in segs:
                k.dma('gpsimd', ws[:, :, d0:d0 + n],
                      W[:, s0:s0 + n].rearrange("(kc p) c -> p kc c", p=128), [], [ws])
                ncols = max(ncols, d0 + n)
            grp = G[bi % 2]
            for kc in range(32):
                for tc in range(4):
                    k.mm(grp.ap[0:ncols, tc * 512:(tc + 1) * 512], ws[:, kc, 0:ncols],
                         hnT[:, kc, tc * 512:(tc + 1) * 512], kc == 0, kc == 31, [ws, hnT], [grp.bufs])
            evac(bi, grp, ncols)

    def conv_block(grp, a1, a2, cwname, cbname, blk):
        k.cp('scalar', a1[:, 3:3 + T], grp.ap, [grp.bufs], [a1])
        k.act(a2[:, :], a1[:, 3:3 + T], AF.Identity, [a1, pp], [a2],
              bias=ppc(cbname, blk), scale=ppc(cwname, blk * 4 + 3))
        for kk in range(3):
            k.stt(a2[:, :], a1[:, kk:kk + T], ppc(cwname, blk * 4 + kk), a2[:, :], ALU.mult, ALU.add,
                  [a1, pp, a2], [a2])

    hnT = sb.alloc([128, 32, T], BF16, 'hnT')
    phase_norm_T(x, norm_w[0, :], hnT)
    l0_mark = sb.mark()
    wslots = [sb.alloc([128, 32, 128], BF16, f'w{i}') for i in range(2)]

    m = sb.mark()
    a1 = sb.alloc([128, T + 4], F32, 'a1')
    a2 = sb.alloc([128, T], F32, 'a2')
    a3 = sb.alloc([128, T], F32, 'a3')
    xcb = sb.alloc([128, T], BF16, 'xcb')
    sga = sb.alloc([128, T], BF16, 'sga')
    yab = sb.alloc([128, T], BF16, 'yab')
    gw = sb.alloc([128, 2, 128], BF16, 'gw')
    s1 = sb.alloc([128, 48], F32, 's1')
    k.memset('vector', a1[:, 0:3], 0.0, [a1])
    k.act(s1[:, 0:16], ppc('lam', 0, 16), AF.Exp, [pp], [s1], scale=-1.0)
    k.act(s1[:, 0:16], s1[:, 0:16], AF.Ln, [s1], [s1], bias=1.0)
    k.ts('vector', s1[:, 16:32], s1[:, 0:16], -8.0, None, ALU.mult, None, [s1], [s1])
    k.ts('vector', s1[:, 32:48], s1[:, 0:16], -16.0, None, ALU.mult, None, [s1], [s1])

    evq = [sb.alloc([128, T], BF16, f'evq{i}') for i in range(2)]

    def rg_gates(g):
        g0 = G[0]
        for tc in range(4):
            k.mm(g0.ap[:, tc * 512:(tc + 1) * 512], gw[:, 0, :], xcb[:, tc * 512:(tc + 1) * 512],
                 True, True, [gw, xcb], [g0.bufs])
        k.act(a1[:, 0:T], g0.ap, AF.Sigmoid, [g0.bufs, pp], [a1], bias=ppc('ba', g))
        g1 = G[1]
        for tc in range(4):
            k.mm(g1.ap[:, tc * 512:(tc + 1) * 512], gw[:, 1, :], xcb[:, tc * 512:(tc + 1) * 512],
                 True, True, [gw, xcb], [g1.bufs])
        k.act(a3[:, :], a1[:, 0:T], AF.Exp, [a1, s1], [a3], scale=s1[:, 32 + g:33 + g])
        k.act(a1[:, 0:T], a1[:, 0:T], AF.Exp, [a1, s1], [a1], scale=s1[:, 16 + g:17 + g])
        k.act(a3[:, :], a3[:, :], AF.Sqrt, [a3], [a3], bias=1.0, scale=-1.0)
        k.tt('vector', a3[:, :], a3[:, :], a2[:, :], ALU.mult, [a3, a2], [a3])
        k.act(a2[:, :], g1.ap, AF.Sigmoid, [g1.bufs, pp], [a2], bias=ppc('bx', g))
        k.tt('vector', a3[:, :], a3[:, :], a2[:, :], ALU.mult, [a3, a2], [a3])
        k.scan(a2[:, :], a1[:, 0:T], a3[:, :], 0.0, ALU.mult, ALU.add, [a1, a3], [a2])
        k.tt('vector', yab[:, :], a2[:, :], sga[:, :], ALU.mult, [a2, sga], [yab])
        k.dma('sync', yT[g * 128:(g + 1) * 128, :], yab[:, :], [yab], [yT])
        k.memset('vector', a1[:, 0:3], 0.0, [a1])

    def evac_rg(bi, grp, ncols):
        g = bi // 4
        kind = bi % 4
        if kind == 0:
            conv_block(grp, a1, a2, 'cw0', 'cb0', g)
            k.cp('gpsimd', xcb[:, :], a2[:, :], [a2], [xcb])
            k.dma('gpsimd', gw[:, 0, :], wa[g, :, :], [], [gw])
            k.dma('gpsimd', gw[:, 1, :], wx[g, :, :], [], [gw])
        elif kind == 1:
            k.act(sga[:, :], grp.ap, AF.Silu, [grp.bufs], [sga])
        elif kind == 2:
            ob = evq[0]
            k.act(ob[:, :], grp.ap, AF.Silu, [grp.bufs], [ob])
            k.dma('sync', sgb_s[g * 128:(g + 1) * 128, :], ob[:, :], [ob], [sgb_s])
        else:
            ob = evq[1]
            k.cp('vector', ob[:, :], grp.ap, [grp.bufs], [ob])
            k.dma('sync', qiT_s[g * 128:(g + 1) * 128, :], ob[:, :], [ob], [qiT_s])
            rg_gates(g)

    blocks = []
    for g in range(16):
        blocks.append([(0, g * 128, 128)])
        blocks.append([(0, 2048 + g * 128, 128)])
        blocks.append([(0, 8800 + g * 128, 128)])
        blocks.append([(0, 6656 + g * 128, 128)])
    linear_fm(hnT, ab_in_w, blocks, wslots, evac_rg, 'rg')
    P.barrier()
    sb.reset(m)

    sb.reset(l0_mark)
    wslots = [sb.alloc([128, 32, 128], BF16, f'w{i}') for i in range(2)]
    m = sb.mark()
    ev = [sb.alloc([128, T], F32, f'ev{i}') for i in range(1)]
    evb = [sb.alloc([128, T], BF16, f'evb{i}') for i in range(3)]
    wukT = sb.alloc([128, 16, 512], BF16, 'wukT')
    wuk_ld = sb.alloc([128, 4, 128], BF16, 'wuk_ld')
    cnt = [0]

    for h in range(16):
        k.dma('gpsimd', wuk_ld[:, :, :], w_uk[h, :, :].rearrange("(cc p) d -> p cc d", p=128), [], [wuk_ld])
        bi = h % 8
        bb = bank_bf(bi)
        for cc in range(4):
            k.tr(bb[:, cc * 128:(cc + 1) * 128], wuk_ld[:, cc, :], ident_b, [wuk_ld, cbf], [pb[bi]])
        k.cpa(wukT[:, h, :], bb[:, 0:512], [pb[bi]], [wukT])

    def evac_ckv(bi, grp, ncols):
        t_ = ev[0]
        k.cpa(t_[:, :], grp.ap, [grp.bufs], [t_])
        k.dma('sync', ckv_raw[bi * 128:(bi + 1) * 128, :], t_[:, :], [t_], [ckv_raw])
    linear_fm(hnT, ab_in_w, [[(0, 6144 + i * 128, 128)] for i in range(4)], wslots, evac_ckv, 'ckv')

    def evac_q(h, grp, ncols):
        qb = evb[2]
        k.cp('scalar', qb[:, :], grp.ap, [grp.bufs], [qb])
        for cc in range(4):
            g2 = G[(h + 1 + cc) % 2]
            for tc in range(4):
                k.mm(g2.ap[:, tc * 512:(tc + 1) * 512], wukT[:, h, cc * 128:(cc + 1) * 128],
                     qb[:, tc * 512:(tc + 1) * 512], True, True, [wukT, qb], [g2.bufs])
            ob = evb[cc % 2]
            if cc % 2 == 0:
                k.act(ob[:, :], g2.ap, AF.Copy, [g2.bufs], [ob], scale=float(128 ** -0.5))
            else:
                k.ts('vector', ob[:, :], g2.ap, float(128 ** -0.5), None, ALU.mult, None, [g2.bufs], [ob])
            k.dma('sync', qlat_s[h, cc, :, :], ob[:, :], [ob], [qlat_s])
    linear_fm(hnT, ab_in_w, [[(0, 4096 + h * 128, 128)] for h in range(16)], wslots, evac_q, 'q')

    def evac_ki(bi, grp, ncols):
        k.cp('scalar', kiT2[:, :], grp.ap, [grp.bufs], [kiT2])
    linear_fm(hnT, ab_in_w, [[(0, 8704, 64), (64, 8704, 64)]], wslots, evac_ki, 'ki')

    def evac_wi(bi, grp, ncols):
        t_ = ev[0]
        k.cp('scalar', t_[0:32, :], grp.ap[0:32, :], [grp.bufs], [t_])
        for tt in range(16):
            bi2 = 4 + tt % 4
            k.tr(bank(bi2)[:, 0:32], t_[0:32, tt * 128:(tt + 1) * 128], ident_f[0:32, 0:32], [t_, cst], [pb[bi2]])
            k.cp('vector', wi_tok[:, tt, :], bank(bi2)[:, 0:32], [pb[bi2]], [wi_tok])
    linear_fm(hnT, ab_in_w, [[(0, 8768, 32)]], wslots, evac_wi, 'wi')

    P.barrier()
    sb.reset(m)

    sb.reset(base_mark)
    ckvT = sb.alloc([128, 4, T], BF16, 'ckvT')
    ckv_tok = sb.alloc([128, 16, 512], BF16, 'ckv_tok')
    wuv = sb.alloc([128, 16, 4, 128], BF16, 'wuv')
    thr_c = sb.alloc([128, 1], F32, 'thr_c')
    k.memset('vector', thr_c[:, :], -1.0e29, [thr_c])
    for h in range(16):
        k.dma('gpsimd', wuv[:, h, :, :], w_uv[h, :, :].rearrange("(cc p) d -> p cc d", p=128), [], [wuv])
    m2 = sb.mark()
    craw = sb.alloc([128, 4, T], F32, 'craw')
    csq = sb.alloc([128, 4, T], F32, 'csq')
    rstd = sb.alloc([128, T], F32, 'rstd')
    k.dma('sync', craw[:, :, :], ckv_raw[:, :].rearrange("(cc p) t -> p cc t", p=128), [ckv_raw], [craw])
    k.act(csq[:, :, :], craw[:, :, :], AF.Square, [craw], [csq])
    for tc in range(4):
        for cc in range(4):
            k.mm(G[0].ap[:, tc * 512:(tc + 1) * 512], ones_f, csq[:, cc, tc * 512:(tc + 1) * 512],
                 cc == 0, cc == 3, [cst, csq], [G[0].bufs])
    k.ts('vector', rstd[:, :], G[0].ap, 1.0 / 512, 1e-6, ALU.mult, ALU.add, [G[0].bufs], [rstd])
    k.act(rstd[:, :], rstd[:, :], AF.Sqrt, [rstd], [rstd])
    k.recip(rstd[:, :], rstd[:, :], [rstd], [rstd])
    for cc in range(4):
        k.stt(ckvT[:, cc, :], craw[:, cc, :], ppc('kvn', cc), rstd[:, :], ALU.mult, ALU.mult,
              [craw, pp, rstd], [ckvT])
    for tt in range(16):
        bi = tt % 8
        bb = bank_bf(bi)
        for cc in range(4):
            k.tr(bb[:, cc * 128:(cc + 1) * 128], ckvT[:, cc, tt * 128:(tt + 1) * 128], ident_b,
                 [ckvT, cbf], [pb[bi]])
        k.cpa(ckv_tok[:, tt, :], bb[:, 0:512], [pb[bi]], [ckv_tok])
    P.barrier()
    sb.reset(m2)

    acc2 = [sb.alloc([128, T], F32, f'acc{i}') for i in range(2)]
    wk = [sb.alloc([128, T], F32, f'wk{i}') for i in range(2)]
    rt = [sb.alloc([128, T], F32, f'rt{i}') for i in range(2)]
    m8 = sb.alloc([128, 8], F32, 'm8')
    m01 = sb.alloc([128, T], BF16, 'm01')
    maskT2 = [sb.alloc([128, 16, 128], BF16, f'maskT{i}') for i in range(2)]
    qi_t = [sb.alloc([128, 16, 128], BF16, f'qi_t{i}') for i in range(2)]
    ql_t = [sb.alloc([128, 16, 4, 128], BF16, f'ql_t{i}') for i in range(2)]
    sgb_t = [sb.alloc([128, 16, 128], BF16, f'sgb_t{i}') for i in range(2)]
    eb = [sb.alloc([128, 512], BF16, f'eb{i}') for i in range(2)]
    ptb = [sb.alloc([128, 512], BF16, f'ptb{i}') for i in range(2)]
    rden = sb.alloc([128, 512], F32, 'rden')
    olb = sb.alloc([128, 4, 512], BF16, 'olb')
    yt1 = sb.alloc([128, 512], F32, 'yt1')
    yob = [sb.alloc([128, 4, 128], BF16, f'yob{i}') for i in range(2)]

    def steps_SC(i):
        S = (i + 1) * 128
        qit = qi_t[i % 2]
        acc = acc2[i % 2]
        nsc = (S + 511) // 512
        st = []

        def s_load():
            k.dma('sync', qit[:, :, :], qiT_s[:, i * 128:(i + 1) * 128].rearrange("(b p) t -> p b t", p=128),
                  [qiT_s], [qit])
        st.append(s_load)

        def mk_head(hh):
            def f():
                blk, half = hh // 2, hh % 2
                r_ = rt[hh % 2]
                for sc in range(nsc):
                    w_ = min(512, S - sc * 512)
                    k.mm(bank(7)[:, 0:w_], qit[half * 64:(half + 1) * 64, blk, :],
                         kiT2[half * 64:(half + 1) * 64, sc * 512:sc * 512 + w_], True, True, [qit, kiT2], [pb[7]])
                    k.act(r_[:, sc * 512:sc * 512 + w_], bank(7)[:, 0:w_], AF.Relu, [pb[7]], [r_])
                if hh == 0:
                    k.ts('vector', acc[:, 0:S], r_[:, 0:S], wi_tok[:, i, 0:1], None, ALU.mult, None,
                         [r_, wi_tok], [acc])
                else:
                    k.stt(acc[:, 0:S], r_[:, 0:S], wi_tok[:, i, hh:hh + 1], acc[:, 0:S], ALU.mult, ALU.add,
                          [r_, wi_tok, acc], [acc])
            return f
        for hh in range(32):
            st.append(mk_head(hh))

        def s_bias():
            k.tt('vector', acc[:, i * 128:S], acc[:, i * 128:S], cbias_f, ALU.add, [acc, cst], [acc])
        st.append(s_bias)
        return st

    def steps_TK(i):
        S = (i + 1) * 128
        acc = acc2[i % 2]
        maskT = maskT2[i % 2]
        qlt = ql_t[i % 2]
        sgt = sgb_t[i % 2]
        st = []

        def s_load():
            k.dma('sync', qlt[:, :, :, :], qlat_s[:, :, :, i * 128:(i + 1) * 128].rearrange("h c p t -> p h c t"),
                  [qlat_s], [qlt])
            k.dma('sync', sgt[:, :, :], sgb_s[:, i * 128:(i + 1) * 128].rearrange("(b p) t -> p b t", p=128),
                  [sgb_s], [sgt])
        st.append(s_load)
        if i >= 2:
            def mk_round(r):
                def f():
                    cur = acc if r == 0 else wk[(r - 1) % 2]
                    k.P.add('vector', (lambda c, S_: (lambda e: e.max(m8[:, :], c[:, 0:S_])))(cur, S), bl([cur]), bl([m8]))
                    if r < 31:
                        nxt = wk[r % 2]
                        k.P.add('vector', (lambda c, n, S_: (lambda e: e.match_replace(n[:, 0:S_], m8[:, :], c[:, 0:S_], -3.0e38)))(cur, nxt, S),
                                bl([cur, m8]), bl([nxt]))
                return f
            for r in range(32):
                st.append(mk_round(r))

        def s_mask():
            if i >= 2:
                thr, thr_b = m8[:, 7:8], [m8]
            else:
                thr, thr_b = thr_c[:, 0:1], [thr_c]
            k.ts('vector', m01[:, 0:S], acc[:, 0:S], thr, None, ALU.is_lt, None, [acc] + thr_b, [m01])
            k.ts('vector', m01[:, 0:S], m01[:, 0:S], -30000.0, None, ALU.mult, None, [m01], [m01])
            for j0 in range(0, i + 1, 8):
                bb = bank_bf(7)
                nj = min(8, i + 1 - j0)
                for j in range(j0, j0 + nj):
                    k.tr(bb[:, (j - j0) * 128:(j - j0 + 1) * 128], m01[:, j * 128:(j + 1) * 128], ident_b,
                         [m01, cbf], [pb[7]])
                k.cp('scalar', maskT[:, j0:j0 + nj, :], bb[:, 0:nj * 128].rearrange("p (a b) -> p a b", a=nj),
                     [pb[7]], [maskT])
        st.append(s_mask)
        return st

    def steps_AT(i):
        maskT = maskT2[i % 2]
        qlt = ql_t[i % 2]
        sgt = sgb_t[i % 2]
        st = []

        def L(hg, j):
            lb_ = j % 2
            for cc in range(4):
                k.mm(bank(lb_).rearrange("p (a b) -> p a b", a=4), ckvT[:, cc, j * 128:(j + 1) * 128],
                     qlt[:, hg * 4:(hg + 1) * 4, cc, :], cc == 0, False, [ckvT, qlt], [pb[lb_]])
            k.mm(bank(lb_).rearrange("p (a b) -> p a b", a=4), ident_b,
                 maskT[:, j:j + 1, :].broadcast_to([128, 4, 128]), False, True, [cbf, maskT], [pb[lb_]])

        def mk_j(hg, j):
            def f():
                if j == 0:
                    L(hg, 0)
                p_ = ptb[j % 2]
                k.act(p_[:, :], bank(j % 2), AF.Exp, [pb[j % 2]], [p_])
                if j + 1 <= i:
                    L(hg, j + 1)
                for cc in range(4):
                    k.mm(bank(2 + cc), ckv_tok[:, j, cc * 128:(cc + 1) * 128], p_[:, :],
                         j == 0, j == i, [ckv_tok, p_], [pb[2 + cc]])
                k.mm(bank(6), ones_b, p_[:, :], j == 0, j == i, [cbf, p_], [pb[6]])
            return f

        def mk_tail(hg):
            def f():
                k.act(rden[:, :], bank(6), AF.Ln, [pb[6]], [rden])
                k.act(rden[:, :], rden[:, :], AF.Exp, [rden], [rden], scale=-1.0)
                for cc in range(4):
                    k.cp('scalar', olb[:, cc, :], bank(2 + cc), [pb[2 + cc]], [olb])
                for hl in range(4):
                    h = hg * 4 + hl
                    for cc in range(4):
                        k.mm(bank(7)[:, hl * 128:(hl + 1) * 128], wuv[:, h, cc, :], olb[:, cc, hl * 128:(hl + 1) * 128],
                             cc == 0, cc == 3, [wuv, olb], [pb[7]])
                k.cp('scalar', yt1[:, :], bank(7), [pb[7]], [yt1])
                k.tt('gpsimd', yt1[:, :], yt1[:, :], rden[:, :], ALU.mult, [yt1, rden], [yt1])
                yo = yob[hg % 2]
                k.tt('gpsimd', yo[:, :, :], yt1[:, :].rearrange("p (a b) -> p a b", a=4), sgt[:, hg * 4:(hg + 1) * 4, :],
                     ALU.mult, [yt1, sgt], [yo])
                k.dma('sync', yT[2048 + hg * 512:2048 + (hg + 1) * 512, i * 128:(i + 1) * 128].rearrange("(a p) t -> p a t", p=128),
                      yo[:, :, :], [yo], [yT])
            return f
        for hg in range(4):
            for j in range(i + 1):
                st.append(mk_j(hg, j))
            st.append(mk_tail(hg))
        return st

    def merge(lists):
        items = []
        for li, l in enumerate(lists):
            n = len(l)
            for idx, f in enumerate(l):
                items.append(((idx + 0.5) / n, li, idx, f))
        items.sort(key=lambda t: (t[0], t[1], t[2]))
        return [t[3] for t in items]

    n_qt = 16
    for f in steps_SC(0) + steps_TK(0) + steps_SC(1):
        f()
    for s_ in range(n_qt):
        lists = [steps_AT(s_)]
        if s_ + 1 < n_qt:
            lists.append(steps_TK(s_ + 1))
        if s_ + 2 < n_qt:
            lists.append(steps_SC(s_ + 2))
        for f in merge(lists):
            f()
    P.barrier()
    sb.reset(base_mark)

    def out_proj(Wout, res_src, dst):
        m = sb.mark()
        yres = sb.alloc([128, 32, 1024], BF16, 'yres')
        wo = [sb.alloc([128, 32, 512], BF16, f'wo{i}') for i in range(2)]
        xr = [sb.alloc([128, 512], F32, f'xr{i}') for i in range(4)]
        n = 0
        for th in range(2):
            for q4 in range(4):
                k.dma('sync', yres[:, q4 * 8:(q4 + 1) * 8, :],
                      yT[q4 * 1024:(q4 + 1) * 1024, th * 1024:(th + 1) * 1024].rearrange("(fc p) t -> p fc t", p=128),
                      [yT], [yres])
            for cb in range(8):
                w_ = wo[n % 2]
                n += 1
                for hf in range(2):
                    k.dma('gpsimd', w_[:, hf * 16:(hf + 1) * 16, :],
                          Wout[hf * 2048:(hf + 1) * 2048, cb * 512:(cb + 1) * 512].rearrange("(fc p) c -> p fc c", p=128),
                          [], [w_])
                for t8 in range(8):
                    tt = th * 8 + t8
                    bi = (cb * 8 + t8) % 8
                    xr_ = xr[(cb * 8 + t8) % 4]
                    k.dma('sync', xr_[:, :], res_src[tt * 128:(tt + 1) * 128, cb * 512:(cb + 1) * 512], [res_src], [xr_])
                    for fc in range(32):
                        k.mm(bank(bi), yres[:, fc, t8 * 128:(t8 + 1) * 128], w_[:, fc, :],
                             fc == 0, fc == 31, [yres, w_], [pb[bi]])
                    k.tt('vector', xr_[:, :], bank(bi), xr_[:, :], ALU.add, [pb[bi], xr_], [xr_])
                    k.dma('sync', dst[tt * 128:(tt + 1) * 128, cb * 512:(cb + 1) * 512], xr_[:, :], [xr_], [dst])
        P.barrier()
        sb.reset(m)

    out_proj(ab_out_w, x, h1)

    if stop_after == 'l0':
        P.emit()
        return nc

    sb.reset(base_mark)
    hnT = sb.alloc([128, 32, T], BF16, 'hnT1')
    phase_norm_T(h1, norm_w[1, :], hnT)
    l1_mark = sb.mark()
    wslots = [sb.alloc([128, 32, 128], BF16, f'w{i}') for i in range(2)]
    m = sb.mark()
    evb = [sb.alloc([128, T], BF16, f'evb{i}') for i in range(3)]
    a1 = sb.alloc([128, T + 4], F32, 'a1')
    a2 = sb.alloc([128, T], F32, 'a2')
    a3 = sb.alloc([128, T], F32, 'a3')
    rstb = sb.alloc([128, T], BF16, 'rstb')
    sm = sb.alloc([128, 160], F32, 'sm')
    k.dma('gpsimd', rstb[:, :], rst_d[:, :], [], [rstb])
    k.memset('vector', a1[:, 0:3], 0.0, [a1])
    lb_ = sm[:, 0:16]
    oml = sm[:, 16:32]
    a_b = sm[:, 32:64]
    o_l, _ = PP['lbs']
    k.tt('vector', lb_, pp[:, o_l + 16:o_l + 32], pp[:, o_l:o_l + 16], ALU.subtract, [pp], [sm])
    k.act(lb_, lb_, AF.Sigmoid, [sm], [sm])
    k.ts('vector', oml, lb_, -1.0, 1.0, ALU.mult, ALU.add, [sm], [sm])
    k.dma('sync', a_b, a_log[:].partition_broadcast(128), [], [sm])
    k.act(a_b, a_b, AF.Exp, [sm], [sm])
    k.ts('vector', a_b, a_b, -1.0, None, ALU.mult, None, [sm], [sm])

    def evac_z(bi, grp, ncols):
        ob = evb[bi % 3]
        k.act(ob[:, :], grp.ap, AF.Silu, [grp.bufs], [ob])
        k.dma('sync', sz_s[bi * 128:(bi + 1) * 128, :], ob[:, :], [ob], [sz_s])
    linear_fm(hnT, cd_in_w, [[(0, i * 128, 128)] for i in range(16)], wslots, evac_z, 'z')

    def evac_xbc(bi, grp, ncols):
        conv_block(grp, a1, a2, 'cw1', 'cb1', bi)
        ob = evb[bi % 3]
        k.act(ob[:, :], a2[:, :], AF.Silu, [a2], [ob])
        if bi < 16:
            k.dma('sync', xc_s[bi * 128:(bi + 1) * 128, :], ob[:, :], [ob], [xc_s])
        elif bi < 20:
            k.dma('sync', BT_s[(bi - 16) * 128:(bi - 15) * 128, :], ob[:, :], [ob], [BT_s])
        else:
            k.dma('sync', CT_s[(bi - 20) * 128:(bi - 19) * 128, :], ob[:, :], [ob], [CT_s])
    linear_fm(hnT, cd_in_w, [[(0, 2048 + i * 128, 128)] for i in range(24)], wslots, evac_xbc, 'xbc')

    def evac_dt(bi, grp, ncols):
        k.act(a3[0:32, :], grp.ap[0:32, :], AF.Exp, [grp.bufs, pp], [a3], bias=pp[0:32, PP['dtb'][0]:PP['dtb'][0] + 1])
        k.act(a3[0:32, :], a3[0:32, :], AF.Ln, [a3], [a3], bias=1.0)
        for tt in range(16):
            bi2 = 4 + tt % 4
            k.tr(bank(bi2)[:, 0:32], a3[0:32, tt * 128:(tt + 1) * 128], ident_f[0:32, 0:32], [a3, cst], [pb[bi2]])
            k.cp('vector', dt_tok[:, tt, :], bank(bi2)[:, 0:32], [pb[bi2]], [dt_tok])
        k.tt('vector', dtA_tok[:, :, :], dt_tok[:, :, :], a_b.unsqueeze(1).broadcast_to([128, 16, 32]), ALU.mult,
             [dt_tok, sm], [dtA_tok])
    linear_fm(hnT, cd_in_w, [[(0, 5120, 32)]], wslots, evac_dt, 'dt')

    def evac_fq(bi, grp, ncols):
        h = bi // 2
        if bi % 2 == 0:
            k.act(a1[:, 0:T], grp.ap, AF.Sigmoid, [grp.bufs], [a1])
            k.ts('vector', a1[:, 0:T], a1[:, 0:T], oml[:, h:h + 1], lb_[:, h:h + 1], ALU.mult, ALU.add, [a1, sm], [a1])
            k.act(a2[:, :], a1[:, 0:T], AF.Ln, [a1], [a2])
            k.scan(a3[:, :], rstb[:, :], a2[:, :], 0.0, ALU.mult, ALU.add, [rstb, a2], [a3])
            k.ts('vector', a1[:, 0:T], a1[:, 0:T], -1.0, 1.0, ALU.mult, ALU.add, [a1], [a1])
            k.act(a2[:, :], a3[:, :], AF.Exp, [a3], [a2], scale=-1.0)
            ob = evb[0]
            k.tt('vector', ob[:, :], a1[:, 0:T], a2[:, :], ALU.mult, [a1, a2], [ob])
            k.dma('sync', kt_s[h * 128:(h + 1) * 128, :], ob[:, :], [ob], [kt_s])
            a3v = a3[:, :].rearrange("p (c l) -> p c l", c=16)
            a2v = a2[:, :].rearrange("p (c l) -> p c l", c=16)
            k.tt('vector', a2v, a3v[:, :, 127:128].broadcast_to([128, 16, 128]), a3v, ALU.subtract, [a3], [a2])
            k.act(a2[:, :], a2[:, :], AF.Exp, [a2], [a2])
            ob = evb[1]
            k.tt('vector', ob[:, :], a1[:, 0:T], a2[:, :], ALU.mult, [a1, a2], [ob])
            k.dma('sync', kd_s[h * 128:(h + 1) * 128, :], ob[:, :], [ob], [kd_s])
            k.act(ebl[:, h, :], a3v[:, :, 127], AF.Exp, [a3], [ebl])
        else:
            k.act(a1[:, 0:T], grp.ap, AF.Silu, [grp.bufs], [a1])
            k.act(a2[:, :], a3[:, :], AF.Exp, [a3], [a2])
            ob = evb[2]
            k.tt('vector', ob[:, :], a1[:, 0:T], a2[:, :], ALU.mult, [a1, a2], [ob])
            k.dma('sync', qt_s[h * 128:(h + 1) * 128, :], ob[:, :], [ob], [qt_s])
    blocks = []
    for h in range(16):
        blocks.append([(0, 7200 + h * 128, 128)])
        blocks.append([(0, 5152 + h * 128, 128)])
    linear_fm(hnT, cd_in_w, blocks, wslots, evac_fq, 'fq')

    def evac_gd(bi, grp, ncols):
        ob = evb[bi % 3]
        k.act(ob[:, :], grp.ap, AF.Silu, [grp.bufs], [ob])
        k.dma('sync', sgd_s[bi * 128:(bi + 1) * 128, :], ob[:, :], [ob], [sgd_s])
    linear_fm(hnT, cd_in_w, [[(0, 11296 + i * 128, 128)] for i in range(16)], wslots, evac_gd, 'gd')
    P.barrier()
    sb.reset(l1_mark)

    wo = [sb.alloc([128, 32, 256], BF16, f'wv{i}') for i in range(2)]
    vb = [sb.alloc([128, 256], BF16, f'vb{i}') for i in range(4)]
    n = 0
    for sl in range(8):
        w_ = wo[sl % 2]
        k.dma('gpsimd', w_[:, :, :], cd_in_w[:, 9248 + sl * 256:9248 + (sl + 1) * 256].rearrange("(kc p) c -> p kc c", p=128),
              [], [w_])
        for tt in range(16):
            bi = n % 8
            v_ = vb[n % 4]
            n += 1
            for kc in range(32):
                k.mm(bank(bi)[:, 0:256], hnT[:, kc, tt * 128:(tt + 1) * 128], w_[:, kc, :], kc == 0, kc == 31,
                     [hnT, w_], [pb[bi]])
            k.cpa(v_[:, :], bank(bi)[:, 0:256], [pb[bi]], [v_])
            k.dma('sync', v_s[tt * 128:(tt + 1) * 128, sl * 256:(sl + 1) * 256], v_[:, :], [v_], [v_s])
    P.barrier()
    sb.reset(base_mark)

    sprev = sb.alloc([128, 4, 512], F32, 'sprev')
    sprev_b = sb.alloc([128, 4, 512], BF16, 'sprev_b')
    hprev = sb.alloc([128, 16, 128], F32, 'hprev')
    hprev_b = sb.alloc([128, 16, 128], BF16, 'hprev_b')
    Db = sb.alloc([128, 32], F32, 'Db')
    k.memset('vector', sprev[:, :, :], 0.0, [sprev])
    k.memset('vector', sprev_b[:, :, :], 0.0, [sprev_b])
    k.memset('vector', hprev[:, :, :], 0.0, [hprev])
    k.memset('vector', hprev_b[:, :, :], 0.0, [hprev_b])
    k.dma('sync', Db[:, :], d_skip[:].partition_broadcast(128), [], [Db])

    def dbl(shape, dt, name):
        return [sb.alloc(shape, dt, f'{name}{i}') for i in range(2)]
    xcT_c = dbl([128, 16, 128], BF16, 'xcT_c')
    BT_c = dbl([128, 4, 128], BF16, 'BT_c')
    CT_c = dbl([128, 4, 128], BF16, 'CT_c')
    sz_c = dbl([128, 16, 128], BF16, 'sz_c')
    qt_c = dbl([128, 16, 128], BF16, 'qt_c')
    kt_c = dbl([128, 16, 128], BF16, 'kt_c')
    kd_c = dbl([128, 16, 128], BF16, 'kd_c')
    sgd_c = dbl([128, 16, 128], BF16, 'sgd_c')
    v_c = dbl([128, 2048], BF16, 'v_c')
    x_tok = sb.alloc([128, 32, 64], BF16, 'x_tok')
    B_tok = sb.alloc([128, 4, 128], BF16, 'B_tok')
    sml = sb.alloc([128, 256], F32, 'sml')
    xdt = sb.alloc([128, 32, 64], BF16, 'xdt')
    xdtd = sb.alloc([128, 32, 64], BF16, 'xdtd')
    cbm = sb.alloc([128, 4, 128], F32, 'cbm')
    yoff = sb.alloc([128, 32, 64], F32, 'yoff')
    Lm4 = [sb.alloc([128, 4, 128], F32, f'Lm{i}') for i in range(2)]
    seg4 = [sb.alloc([128, 512], F32, f'seg{i}') for i in range(2)]
    MT4 = [sb.alloc([128, 4, 128], BF16, f'MT{i}') for i in range(2)]
    yv = sb.alloc([128, 2048], F32, 'yv')
    gT = sb.alloc([128, 16, 128], F32, 'gT')
    sq = sb.alloc([128, 16, 128], F32, 'sq')
    t2v = sq[:, :, :].rearrange("p a b -> p (a b)")
    rs = sb.alloc([128, 4, 128], F32, 'rs')
    ycb = dbl([128, 16, 128], BF16, 'ycb')
    kd_tok = sb.alloc([128, 16, 128], BF16, 'kd_tok')
    attm4 = [sb.alloc([128, 4, 128], BF16, f'attm{i}') for i in range(2)]
    rsh = sb.alloc([128, 2048], F32, 'rsh')
    ydb = dbl([128, 16, 128], BF16, 'ydb')
    o_hg, _ = PP['hgn']

    osb = sb.alloc([128, 2048], F32, 'osb')
    sqh = sb.alloc([128, 2048], F32, 'sqh')

    def chunk_steps(c):
        cs = slice(c * 128, (c + 1) * 128)
        p2 = c % 2
        xT, Bc, Cc, szc = xcT_c[p2], BT_c[p2], CT_c[p2], sz_c[p2]
        qtc, ktc, kdc, sgc, vc = qt_c[p2], kt_c[p2], kd_c[p2], sgd_c[p2], v_c[p2]
        eacs = sml[:, 64:96]
        dte = sml[:, 96:128]
        cdb = sml[:, 128:160]
        x_tok2 = x_tok[:, :, :].rearrange("p a b -> p (a b)")
        ssd = []
        hg_ = []

        def ld(dst, src):
            k.dma('sync', dst[:, :, :], src[:, cs].rearrange("(b p) t -> p b t", p=128), [src], [dst])

        def s_load_ssd():
            ld(xT, xc_s)
            ld(Bc, BT_s)
            ld(Cc, CT_s)
            ld(szc, sz_s)
        ssd.append(s_load_ssd)

        def s_pro1():
            for half in range(2):
                bb = bank_bf(4)
                for j in range(8):
                    k.tr(bb[:, j * 128:(j + 1) * 128], xT[:, half * 8 + j, :], ident_b, [xT, cbf], [pb[4]])
                k.cpa(x_tok2[:, half * 1024:(half + 1) * 1024], bb[:, 0:1024], [pb[4]], [x_tok])
            bb = bank_bf(4)
            for g in range(4):
                k.tr(bb[:, g * 128:(g + 1) * 128], Bc[:, g, :], ident_b, [Bc, cbf], [pb[4]])
            k.cpa(B_tok[:, :, :].rearrange("p a b -> p (a b)"), bb[:, 0:512], [pb[4]], [B_tok])
            k.mm(bank(4)[:, 0:32], le_f, dtA_tok[:, c, :], True, True, [cst, dtA_tok], [pb[4]])
            k.mm(bank(4)[:, 32:64], ones_f, dtA_tok[:, c, :], True, True, [cst, dtA_tok], [pb[4]])
            k.cp('scalar', sml[:, 0:64], bank(4)[:, 0:64], [pb[4]], [sml])
            k.act(sml[:, 64:96], sml[:, 0:32], AF.Exp, [sml], [sml])
            k.tt('vector', sml[:, 96:128], sml[:, 32:64], sml[:, 0:32], ALU.subtract, [sml], [sml])
            k.act(sml[:, 96:128], sml[:, 96:128], AF.Exp, [sml], [sml])
            k.act(sml[:, 128:160], sml[:, 32:64], AF.Exp, [sml], [sml])
            k.tt('vector', xdt[:, :, :], x_tok[:, :, :], dt_tok[:, c, :].unsqueeze(2).broadcast_to([128, 32, 64]), ALU.mult,
                 [x_tok, dt_tok], [xdt])
            k.tt('vector', xdtd[:, :, :], xdt[:, :, :], dte.unsqueeze(2).broadcast_to([128, 32, 64]), ALU.mult,
                 [xdt, sml], [xdtd])
        ssd.append(s_pro1)

        def s_pro2():
            for g in range(4):
                k.mm(bank(4)[:, g * 128:(g + 1) * 128], Bc[:, g, :], Cc[:, g, :], True, True, [Bc, Cc], [pb[4]])
            k.tt('vector', cbm[:, :, :], bank(4).rearrange("p (a b) -> p a b", a=4),
                 le_f.unsqueeze(1).broadcast_to([128, 4, 128]), ALU.mult, [pb[4], cst], [cbm])
            for g in range(4):
                k.mm(bank(4), Cc[:, g, :], sprev_b[:, g, :], True, True, [Cc, sprev_b], [pb[4]])
                k.tt('vector', yoff[:, g * 8:(g + 1) * 8, :], bank(4).rearrange("p (a b) -> p a b", a=8),
                     eacs[:, g * 8:(g + 1) * 8].unsqueeze(2).broadcast_to([128, 8, 64]), ALU.mult, [pb[4], sml], [yoff])
        ssd.append(s_pro2)

        def mk_ssd_batch(bq):
            def f():
                hd0 = bq * 4
                g = bq // 2
                yb_ = g % 2
                L4 = Lm4[bq % 2]
                k.tt('gpsimd' if bq % 2 else 'vector', L4[:, :, :], gt_f.unsqueeze(1).broadcast_to([128, 4, 128]),
                     dtA_tok[:, c, hd0:hd0 + 4].unsqueeze(2).broadcast_to([128, 4, 128]), ALU.mult, [cst, dtA_tok], [L4])
                for q in range(4):
                    k.mm(bank(6)[:, q * 128:(q + 1) * 128], L4[:, q, :], le_f, True, True, [L4, cst], [pb[6]])
                s4 = seg4[bq % 2]
                k.act(s4[:, :], bank(6), AF.Exp, [pb[6]], [s4])
                M4 = MT4[bq % 2]
                k.tt('vector', M4[:, :, :], s4[:, :].rearrange("p (a b) -> p a b", a=4),
                     cbm[:, g:g + 1, :].broadcast_to([128, 4, 128]), ALU.mult, [s4, cbm], [M4])
                for q in range(4):
                    hd = hd0 + q
                    k.mm(bank(yb_)[:, (hd % 8) * 64:(hd % 8 + 1) * 64], M4[:, q, :], xdt[:, hd, :], True, True,
                         [M4, xdt], [pb[yb_]])
                if bq % 2 == 1:
                    k.tt('vector', yv[:, g * 512:(g + 1) * 512], bank(yb_),
                         yoff[:, g * 8:(g + 1) * 8, :].rearrange("p a b -> p (a b)"), ALU.add, [pb[yb_], yoff], [yv])
            return f
        for bq in range(8):
            ssd.append(mk_ssd_batch(bq))

        def s_states():
            for g in range(4):
                k.mm(bank(4), B_tok[:, g, :], xdtd[:, g * 8:(g + 1) * 8, :].rearrange("p a b -> p (a b)"), True, True,
                     [B_tok, xdtd], [pb[4]])
                spv = sprev[:, g, :].rearrange("p (a b) -> p a b", a=8)
                k.tt('vector', spv, spv, cdb[:, g * 8:(g + 1) * 8].unsqueeze(2).broadcast_to([128, 8, 64]), ALU.mult,
                     [sprev, sml], [sprev])
                k.tt('vector', sprev[:, g, :], sprev[:, g, :], bank(4), ALU.add, [sprev, pb[4]], [sprev])
                k.cp('scalar', sprev_b[:, g, :], sprev[:, g, :], [sprev], [sprev_b])
            k.tt('gpsimd', t2v.rearrange("p (a b) -> p a b", a=32), x_tok[:, :, :], Db[:, :].unsqueeze(2).broadcast_to([128, 32, 64]), ALU.mult,
                 [x_tok, Db], [sq])
            k.tt('vector', yv[:, :], yv[:, :], t2v, ALU.add, [yv, sq], [yv])
        ssd.append(s_states)

        def s_epi():
            for q4 in range(4):
                for j in range(4):
                    fc = q4 * 4 + j
                    k.tr(bank(4)[:, j * 128:(j + 1) * 128], yv[:, fc * 128:(fc + 1) * 128], ident_f, [yv, cst], [pb[4]])
                k.tt('vector', gT[:, q4 * 4:(q4 + 1) * 4, :], bank(4).rearrange("p (a b) -> p a b", a=4),
                     szc[:, q4 * 4:(q4 + 1) * 4, :], ALU.mult, [pb[4], szc], [gT])
            k.act(sq[:, :, :], gT[:, :, :], AF.Square, [gT], [sq])
            for g in range(4):
                for j in range(4):
                    k.mm(bank(4)[:, g * 128:(g + 1) * 128], ones_f, sq[:, g * 4 + j, :], j == 0, j == 3, [cst, sq], [pb[4]])
            rs2 = rs[:, :, :].rearrange("p a b -> p (a b)")
            k.ts('vector', rs2, bank(4), 1.0 / 512, 1e-6, ALU.mult, ALU.add, [pb[4]], [rs])
            k.act(rs2, rs2, AF.Sqrt, [rs], [rs])
            k.recip(rs2, rs2, [rs], [rs])
            yc_ = ycb[p2]
            for fc in range(16):
                k.stt(yc_[:, fc, :], gT[:, fc, :], ppc('ssdn', fc), rs[:, fc // 4, :], ALU.mult, ALU.mult,
                      [gT, pp, rs], [yc_])
            k.dma('sync', yT[0:2048, cs].rearrange("(b p) t -> p b t", p=128), yc_[:, :, :], [yc_], [yT])
        ssd.append(s_epi)

        def h_load():
            ld(qtc, qt_s)
            ld(ktc, kt_s)
            ld(kdc, kd_s)
            ld(sgc, sgd_s)
            k.dma('sync', vc[:, :], v_s[cs, :], [v_s], [vc])
        hg_.append(h_load)

        def h_pro():
            for half in range(2):
                bb = bank_bf(5)
                for j in range(8):
                    k.tr(bb[:, j * 128:(j + 1) * 128], kdc[:, half * 8 + j, :], ident_b, [kdc, cbf], [pb[5]])
                k.cpa(kd_tok[:, half * 8:(half + 1) * 8, :].rearrange("p a b -> p (a b)"), bb[:, 0:1024], [pb[5]], [kd_tok])
        hg_.append(h_pro)

        def mk_h_batch(hq):
            def f():
                h0 = hq * 4
                ob_ = 2 + hq % 2
                for q in range(4):
                    h = h0 + q
                    k.mm(bank(7)[:, q * 128:(q + 1) * 128], ktc[:, h, :], qtc[:, h, :], True, True, [ktc, qtc], [pb[7]])
                am4 = attm4[hq % 2]
                k.tt('vector', am4[:, :, :], bank(7).rearrange("p (a b) -> p a b", a=4),
                     le_f.unsqueeze(1).broadcast_to([128, 4, 128]), ALU.mult, [pb[7], cst], [am4])
                for q in range(4):
                    h = h0 + q
                    hs = slice(h * 128, (h + 1) * 128)
                    oc = slice(q * 128, (q + 1) * 128)
                    k.mm(bank(ob_)[:, oc], vc[:, hs], am4[:, q, :], True, False, [vc, am4], [pb[ob_]])
                    k.mm(bank(ob_)[:, oc], hprev_b[:, h, :], qtc[:, h, :], False, True, [hprev_b, qtc], [pb[ob_]])
                for q in range(4):
                    h = h0 + q
                    hs = slice(h * 128, (h + 1) * 128)
                    k.mm(bank(5)[:, q * 128:(q + 1) * 128], kd_tok[:, h, :], vc[:, hs], True, True, [kd_tok, vc], [pb[5]])
                hp4 = hprev[:, h0:h0 + 4, :]
                k.tt('vector', hp4, hp4, ebl[:, h0:h0 + 4, c:c + 1].broadcast_to([128, 4, 128]), ALU.mult,
                     [hprev, ebl], [hprev])
                k.tt('vector', hp4, hp4, bank(5).rearrange("p (a b) -> p a b", a=4), ALU.add, [hprev, pb[5]], [hprev])
                k.cp('scalar', hprev_b[:, h0:h0 + 4, :], hp4, [hprev], [hprev_b])
                k.cp('scalar', osb[:, hq * 512:(hq + 1) * 512], bank(ob_), [pb[ob_]], [osb])
                k.act(sqh[:, hq * 512:(hq + 1) * 512], bank(ob_), AF.Square, [pb[ob_]], [sqh])
            return f
        for hq in range(4):
            hg_.append(mk_h_batch(hq))

        def h_epi():
            for q4 in range(4):
                qs = slice(q4 * 512, (q4 + 1) * 512)
                k.mm(bank(5), ones_f, sqh[:, qs], True, True, [cst, sqh], [pb[5]])
                k.ts('vector', rsh[:, qs], bank(5), 1.0 / 128, 1e-6, ALU.mult, ALU.add, [pb[5]], [rsh])
            k.act(rsh[:, :], rsh[:, :], AF.Sqrt, [rsh], [rsh])
            k.recip(rsh[:, :], rsh[:, :], [rsh], [rsh])
            k.tt('vector', rsh[:, :], osb[:, :], rsh[:, :], ALU.mult, [osb, rsh], [rsh])
            rsh3 = rsh[:, :].rearrange("p (a b) -> p a b", a=16)
            k.tt('gpsimd', rsh3, rsh3, pp[:, o_hg:o_hg + 16].unsqueeze(2).broadcast_to([128, 16, 128]), ALU.mult,
                 [rsh, pp], [rsh])
            yd_ = ydb[p2]
            k.tt('vector', yd_[:, :, :], rsh3, sgc[:, :, :], ALU.mult, [rsh, sgc], [yd_])
            k.dma('sync', yT[2048:4096, cs].rearrange("(b p) t -> p b t", p=128), yd_[:, :, :], [yd_], [yT])
        hg_.append(h_epi)
        return ssd, hg_

    def merge2(lists):
        items = []
        for li, l in enumerate(lists):
            n = len(l)
            for idx, f in enumerate(l):
                items.append(((idx + 0.5) / n, li, idx, f))
        items.sort(key=lambda t: (t[0], t[1], t[2]))
        return [t[3] for t in items]

    for c in range(16):
        a_, b_ = chunk_steps(c)
        for f in merge2([a_, b_]):
            f()
    P.barrier()
    sb.reset(base_mark)

    out_proj(cd_out_w, h1, h2)

    nwb = sb.alloc([128, D], F32, 'fnw')
    xt2 = [sb.alloc([128, D], F32, f'fx{i}') for i in range(2)]
    xo2 = [sb.alloc([128, D], F32, f'fo{i}') for i in range(2)]
    junk = sb.alloc([128, D], BF16, 'fjunk')
    st = sb.alloc([128, 4], F32, 'fst')
    k.dma('sync', nwb[:, :], final_norm[:].partition_broadcast(128), [], [nwb])
    for tt in range(16):
        xt = xt2[tt % 2]
        xo = xo2[tt % 2]
        k.dma('sync', xt[:, :], h2[tt * 128:(tt + 1) * 128, :], [h2], [xt])
        k.act(junk[:, :], xt[:, :], AF.Square, [xt], [junk, st], accum=st[:, 0:1])
        k.ts('vector', st[:, 1:2], st[:, 0:1], 1.0 / D, 1e-6, ALU.mult, ALU.add, [st], [st])
        k.act(st[:, 2:3], st[:, 1:2], AF.Sqrt, [st], [st])
        k.recip(st[:, 3:4], st[:, 2:3], [st], [st])
        k.stt(xo[:, :], xt[:, :], st[:, 3:4], nwb[:, :], ALU.mult, ALU.mult, [xt, st, nwb], [xo])
        k.dma('sync', out_d[tt * 128:(tt + 1) * 128, :], xo[:, :], [xo], [out_d])
    P.emit()
    return nc


_W_KEYS = (("norm_w", None), ("final_norm", None), ("ab_in_w", 0), ("ab_rg_wa", 0), ("ab_rg_wx", 0),
           ("ab_w_uk", 0), ("ab_w_uv", 0), ("ab_out_w", 0), ("cd_in_w", 0), ("cd_a_log", 0),
           ("cd_d_skip", 0), ("cd_out_w", 0))


def make_in_maps(inp, batches):
    common = {}
    for name, idx in _W_KEYS:
        a = np.asarray(inp[name], dtype=np.float32)
        common[name] = np.ascontiguousarray(a if idx is None else a[idx])
    common["pp"] = pack_params(inp)
    cst, rst = make_consts()
    common["cst"] = cst
    common["rst"] = rst
    maps = []
    for b in batches:
        d = dict(common)
        d["x"] = np.ascontiguousarray(np.asarray(inp["x"][b], dtype=np.float32))
        maps.append(d)
    return maps


def kernel(**inputs):
    nc = build()
    in_maps = make_in_maps(inputs, range(8))
    res = run_bass_kernel_spmd(nc, in_maps, core_ids=list(range(8)))
    return np.stack([np.asarray(res.results[b]["out"], dtype=np.float32) for b in range(8)], axis=0)
```

```python
import numpy as np
import concourse.bass as bass
import concourse.mybir as mybir
from concourse.bass_utils import run_bass_kernel_spmd

F32 = mybir.dt.float32
BF16 = mybir.dt.bfloat16
AF = mybir.ActivationFunctionType
ALU = mybir.AluOpType
T = 2048
D = 4096
AB_COLS = 10848
CD_COLS = 13344
EPOCH = 30000
SB_BASE = 16640
SB_CAP = SB_BASE + 207 * 1024
NEG = -1.0e30


class Ins:
    __slots__ = ('eng', 'fn', 'deps', 'signal', 'seq', 'dkey', 'dcnt', 'order', 'isdma')


class Buf:
    __slots__ = ('name', 'w', 'r')

    def __init__(self, name=''):
        self.name = name
        self.w = {}
        self.r = {}


class Tn:
    def __init__(self, h, name=''):
        self.h = h
        self.b = Buf(name)

    def __getitem__(self, k):
        return self.h[k]


class Prog:
    ENGS = ('tensor', 'vector', 'scalar', 'gpsimd', 'sync')

    def __init__(self, nc, ndma=12):
        self.nc = nc
        self.streams = {e: [] for e in self.ENGS}
        self.order = 0
        self.ndma = ndma
        self.rr = {}
        self.dlast = {}
        self.dcount = {}
        self.lastc = {}
        self.barrier_deps = {e: None for e in self.ENGS}

    def add(self, eng, fn, reads=(), writes=(), dma=False):
        ins = Ins()
        ins.eng = eng
        ins.fn = fn
        ins.signal = False
        ins.seq = None
        ins.isdma = dma
        ins.dcnt = 0
        ins.order = self.order
        self.order += 1
        deps = {}

        def need(d):
            k = d.dkey
            if k not in deps or deps[k].order < d.order:
                deps[k] = d
        for b in reads:
            for d in b.w.values():
                need(d)
        for b in writes:
            for d in b.w.values():
                need(d)
            for d in b.r.values():
                need(d)
        bd = self.barrier_deps[eng]
        if bd is not None:
            for d in bd:
                need(d)
            self.barrier_deps[eng] = None
        if dma:
            k = self.rr.get(eng, 0)
            self.rr[eng] = (k + 1) % self.ndma
            ins.dkey = ('d', eng, k)
            prev = self.dlast.get(ins.dkey)
            if prev is not None:
                need(prev)
            self.dlast[ins.dkey] = ins
            ins.dcnt = self.dcount.get(ins.dkey, 0) + 16
            self.dcount[ins.dkey] = ins.dcnt
        else:
            ins.dkey = eng
            if eng == 'tensor':
                deps.pop('tensor', None)
            self.lastc[eng] = ins
        for d in deps.values():
            if not d.isdma:
                d.signal = True
        ins.deps = list(deps.values())
        for b in writes:
            if b.r:
                b.w = {}
                b.r = {}
            b.w[ins.dkey] = ins
        for b in reads:
            b.r[ins.dkey] = ins
        self.streams[eng].append(ins)
        return ins

    def barrier(self):
        deps = list(self.lastc.values()) + list(self.dlast.values())
        for e in self.ENGS:
            self.barrier_deps[e] = list(deps)

    def emit(self):
        nc = self.nc
        self.barrier()
        for e in self.ENGS:
            self.add(e, None)
        nsig = {}
        for e, st in self.streams.items():
            n = 0
            for ins in st:
                if (not ins.isdma) and ins.signal:
                    n += 1
                    ins.seq = n
            nsig[e] = n
        csem = {e: [nc.alloc_semaphore(name=f"c_{e}_{i}") for i in range((nsig[e] + EPOCH - 1) // EPOCH)]
                for e in self.ENGS}
        dsem = {k: nc.alloc_semaphore(name=f"d_{k[1]}_{k[2]}") for k in self.dcount}
        streams = self.streams

        def run(ename, eng):
            waited = {}
            maxep = {}
            for ins in streams[ename]:
                for d in ins.deps:
                    if d.isdma:
                        key = d.dkey
                        sem = dsem[key]
                        val = d.dcnt
                    else:
                        ep = (d.seq - 1) // EPOCH
                        if maxep.get(d.eng, -1) > ep:
                            continue
                        key = (d.eng, ep)
                        sem = csem[d.eng][ep]
                        val = (d.seq - 1) % EPOCH + 1
                    if waited.get(key, 0) >= val:
                        continue
                    eng.wait_ge(sem, val)
                    waited[key] = val
                    if not d.isdma:
                        maxep[d.eng] = max(maxep.get(d.eng, -1), ep)
                if ins.fn is None:
                    continue
                r = ins.fn(eng)
                if ins.isdma:
                    r.then_inc(dsem[ins.dkey], 16)
                elif ins.signal:
                    ep = (ins.seq - 1) // EPOCH
                    r.then_inc(csem[ename][ep], 1)

        with nc.Block() as block:
            @block.tensor
            def _(e):
                run('tensor', e)

            @block.vector
            def _(e):
                run('vector', e)

            @block.scalar
            def _(e):
                run('scalar', e)

            @block.gpsimd
            def _(e):
                run('gpsimd', e)

            @block.sync
            def _(e):
                run('sync', e)


def _size(dt):
    return 2 if dt == BF16 else 4


class SBAlloc:
    def __init__(self, nc):
        self.nc = nc
        self.ptr = SB_BASE
        self.n = 0

    def alloc(self, shape, dtype, name='t'):
        nb = int(np.prod(shape[1:])) * _size(dtype)
        off = (self.ptr + 63) // 64 * 64
        self.ptr = off + nb
        assert self.ptr <= SB_CAP, f"SBUF overflow {self.ptr} at {name}"
        self.n += 1
        h = self.nc.alloc_sbuf_tensor_at(f"{name}_{self.n}", list(shape), dtype, offset=off)
        return Tn(h, name)

    def mark(self):
        return self.ptr

    def reset(self, m):
        self.ptr = m


def bl(x):
    out = []
    for t in x:
        if isinstance(t, Tn):
            out.append(t.b)
        elif isinstance(t, Buf):
            out.append(t)
        else:
            out.extend(t)
    return out


class K:
    def __init__(self, P):
        self.P = P
        self.alt = 0

    def mm(self, out, lhsT, rhs, start, stop, R, W):
        self.P.add('tensor', lambda e: e.matmul(out, lhsT, rhs, start=start, stop=stop), bl(R), bl(W))

    def tr(self, out, in_, ident, R, W):
        self.P.add('tensor', lambda e: e.transpose(out, in_, ident), bl(R), bl(W))

    def act(self, out, in_, func, R, W, bias=None, scale=None, accum=None):
        kw = {}
        if bias is not None:
            kw['bias'] = bias
        if scale is not None:
            kw['scale'] = scale
        if accum is not None:
            kw['accum_out'] = accum
        self.P.add('scalar', lambda e: e.activation(out, in_, func, **kw), bl(R), bl(W))

    def tt(self, eng, out, in0, in1, op, R, W):
        self.P.add(eng, lambda e: e.tensor_tensor(out, in0, in1, op), bl(R), bl(W))

    def ts(self, eng, out, in0, s1, s2, op0, op1, R, W):
        if s2 is None:
            self.P.add(eng, lambda e: e.tensor_scalar(out, in0, s1, None, op0), bl(R), bl(W))
        else:
            self.P.add(eng, lambda e: e.tensor_scalar(out, in0, s1, s2, op0, op1), bl(R), bl(W))

    def stt(self, out, in0, scalar, in1, op0, op1, R, W):
        self.P.add('vector', lambda e: e.scalar_tensor_tensor(out, in0, scalar, in1, op0, op1), bl(R), bl(W))

    def cp(self, eng, out, in_, R, W):
        if eng == 'scalar':
            self.P.add('scalar', lambda e: e.activation(out, in_, AF.Copy), bl(R), bl(W))
        else:
            self.P.add(eng, lambda e: e.tensor_copy(out, in_), bl(R), bl(W))

    def cpa(self, out, in_, R, W):
        self.alt ^= 1
        self.cp('scalar' if self.alt else 'vector', out, in_, R, W)

    def recip(self, out, in_, R, W):
        self.P.add('vector', lambda e: e.reciprocal(out, in_), bl(R), bl(W))

    def scan(self, out, d0, d1, init, op0, op1, R, W):
        self.P.add('vector', lambda e: e.tensor_tensor_scan(out, d0, d1, init, op0, op1), bl(R), bl(W))

    def memset(self, eng, out, val, W):
        self.P.add(eng, lambda e: e.memset(out, val), [], bl(W))

    def dma(self, q, out, in_, R, W):
        self.P.add(q, lambda e: e.dma_start(out=out, in_=in_), bl(R), bl(W), dma=True)


PP = {}
_o = 0
for _n, _w in (('cw0', 64), ('cb0', 16), ('ba', 16), ('bx', 16), ('lam', 16), ('kvn', 4),
               ('cw1', 96), ('cb1', 24), ('dtb', 1), ('ssdn', 16), ('hgn', 16), ('lbs', 32)):
    PP[_n] = (_o, _w)
    _o += _w
NPP = _o
NCST = 640


def pack_params(inp):
    pp = np.zeros((128, NPP), np.float32)

    def put(name, arr):
        o, w = PP[name]
        pp[:arr.shape[0], o:o + w] = arr.reshape(arr.shape[0], -1)
    put('cw0', inp['ab_conv_w'][0].reshape(4, 16, 128).transpose(2, 1, 0))
    put('cb0', inp['ab_conv_b'][0].reshape(16, 128).T)
    put('ba', inp['ab_rg_ba'][0].reshape(16, 128).T)
    put('bx', inp['ab_rg_bx'][0].reshape(16, 128).T)
    put('lam', inp['ab_rg_lambda'][0].reshape(16, 128).T)
    put('kvn', inp['ab_kv_norm'][0].reshape(4, 128).T)
    put('cw1', inp['cd_conv_w'][0].reshape(4, 24, 128).transpose(2, 1, 0))
    put('cb1', inp['cd_conv_b'][0].reshape(24, 128).T)
    put('dtb', inp['cd_dt_bias'][0].reshape(32, 1))
    put('ssdn', inp['cd_ssd_norm'][0].reshape(16, 128).T)
    put('hgn', inp['cd_hgrn_norm'][0].reshape(16, 128).T)
    put('lbs', inp['hgrn_lower_bounds'].reshape(2, 16, 128).transpose(2, 0, 1))
    return pp


def make_consts():
    c = np.zeros((128, NCST), np.float32)
    p = np.arange(128)[:, None]
    j = np.arange(128)[None, :]
    c[:, 0:128] = (p == j)
    c[:, 128:256] = (p <= j)
    c[:, 256:384] = (p > j)
    c[:, 384:512] = np.where(j <= p, 0.0, NEG)
    c[:, 512:640] = 1.0
    rst = np.ones((128, T), np.float32)
    rst[:, ::128] = 0.0
    return c, rst


def build(debug=False, stop_after=None):
    nc = bass.Bass("TRN2", target_bir_lowering=False)
    P = Prog(nc)
    k = K(P)
    sb = SBAlloc(nc)

    def din(name, shape, dt=F32):
        return Tn(nc.dram_tensor(name, list(shape), dt, kind="ExternalInput").ap(), name)

    def dscr(name, shape, dt, out=False):
        kind = "ExternalOutput" if (out or debug) else "Internal"
        return Tn(nc.dram_tensor(name, list(shape), dt, kind=kind).ap(), name)

    x = din("x", [T, D])
    norm_w = din("norm_w", [2, D])
    final_norm = din("final_norm", [D])
    ab_in_w = din("ab_in_w", [D, AB_COLS])
    wa = din("ab_rg_wa", [16, 128, 128])
    wx = din("ab_rg_wx", [16, 128, 128])
    w_uk = din("ab_w_uk", [16, 512, 128])
    w_uv = din("ab_w_uv", [16, 512, 128])
    ab_out_w = din("ab_out_w", [D, D])
    cd_in_w = din("cd_in_w", [D, CD_COLS])
    a_log = din("cd_a_log", [32])
    d_skip = din("cd_d_skip", [32])
    cd_out_w = din("cd_out_w", [D, D])
    pp_d = din("pp", [128, NPP])
    cst_d = din("cst", [128, NCST])
    rst_d = din("rst", [128, T])
    out_d = dscr("out", [T, D], F32, out=True)

    yT = dscr("s_yT", [D, T], BF16)
    h1 = dscr("s_h1", [T, D], F32)
    h2 = dscr("s_h2", [T, D], F32)
    ckv_raw = dscr("s_ckv", [512, T], F32)
    qlat_s = dscr("s_qlat", [16, 4, 128, T], BF16)
    qiT_s = dscr("s_qi", [2048, T], BF16)
    sgb_s = dscr("s_sgb", [2048, T], BF16)
    sz_s = dscr("s_sz", [2048, T], BF16)
    xc_s = dscr("s_xc", [2048, T], BF16)
    BT_s = dscr("s_B", [512, T], BF16)
    CT_s = dscr("s_C", [512, T], BF16)
    qt_s = dscr("s_qt", [2048, T], BF16)
    kt_s = dscr("s_kt", [2048, T], BF16)
    kd_s = dscr("s_kd", [2048, T], BF16)
    sgd_s = dscr("s_sgd", [2048, T], BF16)
    v_s = dscr("s_v", [T, 2048], BF16)

    ps = nc.alloc_psum_tensor("ps", [128, 4096], F32)
    pb = [Buf(f"bank{i}") for i in range(8)]

    def bank(i):
        return ps[:, i * 512:(i + 1) * 512]

    def bank_bf(i):
        return ps[:, i * 512:(i + 1) * 512].bitcast(BF16)

    class Grp:
        def __init__(self, g):
            self.ap = ps[:, g * 2048:(g + 1) * 2048]
            self.bufs = pb[g * 4:(g + 1) * 4]
    G = [Grp(0), Grp(1)]

    cst = sb.alloc([128, NCST], F32, 'cst')
    pp = sb.alloc([128, NPP], F32, 'pp')
    cbf = sb.alloc([128, 384], BF16, 'cbf')
    k.dma('sync', cst[:, :], cst_d[:, :], [], [cst])
    k.dma('sync', pp[:, :], pp_d[:, :], [], [pp])
    k.cp('vector', cbf[:, 0:256], cst[:, 0:256], [cst], [cbf])
    k.cp('vector', cbf[:, 256:384], cst[:, 512:640], [cst], [cbf])
    ident_f = cst[:, 0:128]
    le_f = cst[:, 128:256]
    gt_f = cst[:, 256:384]
    cbias_f = cst[:, 384:512]
    ones_f = cst[:, 512:640]
    ident_b = cbf[:, 0:128]
    le_b = cbf[:, 128:256]
    ones_b = cbf[:, 256:384]

    def ppc(name, j=0, n=1):
        o, w = PP[name]
        return pp[:, o + j:o + j + n]

    kiT2 = sb.alloc([128, T], BF16, 'kiT2')
    wi_tok = sb.alloc([128, 16, 32], F32, 'wi_tok')
    dt_tok = sb.alloc([128, 16, 32], F32, 'dt_tok')
    dtA_tok = sb.alloc([128, 16, 32], F32, 'dtA_tok')
    ebl = sb.alloc([128, 16, 16], F32, 'ebl')
    base_mark = sb.mark()

    def phase_norm_T(src, nw_ap, hnT):
        m = sb.mark()
        nwb = sb.alloc([128, D], F32, 'nwb')
        xt_2 = [sb.alloc([128, D], F32, f'xt{i}') for i in range(2)]
        xn = sb.alloc([128, D], BF16, 'xn')
        junk = xn
        st = sb.alloc([128, 4], F32, 'st')
        k.dma('sync', nwb[:, :], nw_ap.partition_broadcast(128), [], [nwb])
        for tt in range(16):
            xt = xt_2[tt % 2]
            k.dma('sync', xt[:, :], src[tt * 128:(tt + 1) * 128, :], [src], [xt])
            k.act(junk[:, :], xt[:, :], AF.Square, [xt], [junk, st], accum=st[:, 0:1])
            k.ts('vector', st[:, 1:2], st[:, 0:1], 1.0 / D, 1e-6, ALU.mult, ALU.add, [st], [st])
            k.act(st[:, 2:3], st[:, 1:2], AF.Sqrt, [st], [st])
            k.recip(st[:, 3:4], st[:, 2:3], [st], [st])
            k.stt(xn[:, :], xt[:, :], st[:, 3:4], nwb[:, :], ALU.mult, ALU.mult, [xt, st, nwb], [xn])
            for g in range(4):
                bi = (tt * 4 + g) % 8
                bb = bank_bf(bi)
                for j in range(8):
                    c = g * 8 + j
                    k.tr(bb[:, j * 128:(j + 1) * 128], xn[:, c * 128:(c + 1) * 128], ident_b,
                         [xn, cbf], [pb[bi]])
                k.cpa(hnT[:, g * 8:(g + 1) * 8, tt * 128:(tt + 1) * 128],
                      bb.rearrange("p (a b) -> p a b", a=8), [pb[bi]], [hnT])
        P.barrier()
        sb.reset(m)

    def linear_fm(hnT, W, blocks, wslots, evac, wtag):
        for bi, segs in enumerate(blocks):
            ws = wslots[bi % len(wslots)]
            ncols = 0
            for (d0, s0, n) in segs:
                k.dma('gpsimd', ws[:, :, d0:d0 + n],
                      W[:, s0:s0 + n].rearrange("(kc p) c -> p kc c", p=128), [], [ws])
                ncols = max(ncols, d0 + n)
            grp = G[bi % 2]
            for kc in range(32):
                for tc in range(4):
                    k.mm(grp.ap[0:ncols, tc * 512:(tc + 1) * 512], ws[:, kc, 0:ncols],
                         hnT[:, kc, tc * 512:(tc + 1) * 512], kc == 0, kc == 31, [ws, hnT], [grp.bufs])
            evac(bi, grp, ncols)

    def conv_block(grp, a1, a2, cwname, cbname, blk):
        k.cp('scalar', a1[:, 3:3 + T], grp.ap, [grp.bufs], [a1])
        k.act(a2[:, :], a1[:, 3:3 + T], AF.Identity, [a1, pp], [a2],
              bias=ppc(cbname, blk), scale=ppc(cwname, blk * 4 + 3))
        for kk in range(3):
            k.stt(a2[:, :], a1[:, kk:kk + T], ppc(cwname, blk * 4 + kk), a2[:, :], ALU.mult, ALU.add,
                  [a1, pp, a2], [a2])

    hnT = sb.alloc([128, 32, T], BF16, 'hnT')
    phase_norm_T(x, norm_w[0, :], hnT)
    l0_mark = sb.mark()
    wslots = [sb.alloc([128, 32, 128], BF16, f'w{i}') for i in range(2)]

    m = sb.mark()
    a1 = sb.alloc([128, T + 4], F32, 'a1')
    a2 = sb.alloc([128, T], F32, 'a2')
    a3 = sb.alloc([128, T], F32, 'a3')
    xcb = sb.alloc([128, T], BF16, 'xcb')
    sga = sb.alloc([128, T], BF16, 'sga')
    yab = sb.alloc([128, T], BF16, 'yab')
    gw2 = [sb.alloc([128, 2, 128], BF16, f'gw{i}') for i in range(2)]
    s1 = sb.alloc([128, 48], F32, 's1')
    k.memset('vector', a1[:, 0:3], 0.0, [a1])
    k.act(s1[:, 0:16], ppc('lam', 0, 16), AF.Exp, [pp], [s1], scale=-1.0)
    k.act(s1[:, 0:16], s1[:, 0:16], AF.Ln, [s1], [s1], bias=1.0)
    k.ts('vector', s1[:, 16:32], s1[:, 0:16], -8.0, None, ALU.mult, None, [s1], [s1])
    k.ts('vector', s1[:, 32:48], s1[:, 0:16], -16.0, None, ALU.mult, None, [s1], [s1])

    evq = [sb.alloc([128, T], BF16, f'evq{i}') for i in range(2)]

    def rg_gates(g):
        gw = gw2[g % 2]
        g0 = G[0]
        for tc in range(4):
            k.mm(g0.ap[:, tc * 512:(tc + 1) * 512], gw[:, 0, :], xcb[:, tc * 512:(tc + 1) * 512],
                 True, True, [gw, xcb], [g0.bufs])
        k.act(a1[:, 0:T], g0.ap, AF.Sigmoid, [g0.bufs, pp], [a1], bias=ppc('ba', g))
        g1 = G[1]
        for tc in range(4):
            k.mm(g1.ap[:, tc * 512:(tc + 1) * 512], gw[:, 1, :], xcb[:, tc * 512:(tc + 1) * 512],
                 True, True, [gw, xcb], [g1.bufs])
        k.act(a3[:, :], a1[:, 0:T], AF.Exp, [a1, s1], [a3], scale=s1[:, 32 + g:33 + g])
        k.act(a1[:, 0:T], a1[:, 0:T], AF.Exp, [a1, s1], [a1], scale=s1[:, 16 + g:17 + g])
        k.act(a3[:, :], a3[:, :], AF.Sqrt, [a3], [a3], bias=1.0, scale=-1.0)
        k.tt('vector', a3[:, :], a3[:, :], a2[:, :], ALU.mult, [a3, a2], [a3])
        k.act(a2[:, :], g1.ap, AF.Sigmoid, [g1.bufs, pp], [a2], bias=ppc('bx', g))
        k.tt('vector', a3[:, :], a3[:, :], a2[:, :], ALU.mult, [a3, a2], [a3])
        k.scan(a2[:, :], a1[:, 0:T], a3[:, :], 0.0, ALU.mult, ALU.add, [a1, a3], [a2])
        k.tt('vector', yab[:, :], a2[:, :], sga[:, :], ALU.mult, [a2, sga], [yab])
        k.dma('sync', yT[g * 128:(g + 1) * 128, :], yab[:, :], [yab], [yT])
        k.memset('vector', a1[:, 0:3], 0.0, [a1])

    def evac_rg(bi, grp, ncols):
        g = bi // 4
        kind = bi % 4
        if kind == 0:
            conv_block(grp, a1, a2, 'cw0', 'cb0', g)
            k.cp('scalar', xcb[:, :], a2[:, :], [a2], [xcb])
            gw = gw2[g % 2]
            k.dma('gpsimd', gw[:, 0, :], wa[g, :, :], [], [gw])
            k.dma('gpsimd', gw[:, 1, :], wx[g, :, :], [], [gw])
        elif kind == 1:
            k.act(sga[:, :], grp.ap, AF.Silu, [grp.bufs], [sga])
        elif kind == 2:
            ob = evq[0]
            k.act(ob[:, :], grp.ap, AF.Silu, [grp.bufs], [ob])
            k.dma('sync', sgb_s[g * 128:(g + 1) * 128, :], ob[:, :], [ob], [sgb_s])
        else:
            ob = evq[1]
            k.cp('vector', ob[:, :], grp.ap, [grp.bufs], [ob])
            k.dma('sync', qiT_s[g * 128:(g + 1) * 128, :], ob[:, :], [ob], [qiT_s])
            rg_gates(g)

    blocks = []
    for g in range(16):
        blocks.append([(0, g * 128, 128)])
        blocks.append([(0, 2048 + g * 128, 128)])
        blocks.append([(0, 8800 + g * 128, 128)])
        blocks.append([(0, 6656 + g * 128, 128)])
    linear_fm(hnT, ab_in_w, blocks, wslots, evac_rg, 'rg')
    P.barrier()
    sb.reset(m)

    sb.reset(l0_mark)
    wslots = [sb.alloc([128, 32, 128], BF16, f'w{i}') for i in range(2)]
    m = sb.mark()
    ev = [sb.alloc([128, T], F32, f'ev{i}') for i in range(1)]
    evb = [sb.alloc([128, T], BF16, f'evb{i}') for i in range(3)]
    wukT = sb.alloc([128, 16, 512], BF16, 'wukT')
    wuk_ld2 = [sb.alloc([128, 4, 128], BF16, f'wuk_ld{i}') for i in range(2)]
    cnt = [0]

    for h in range(16):
        wuk_ld = wuk_ld2[h % 2]
        k.dma('gpsimd', wuk_ld[:, :, :], w_uk[h, :, :].rearrange("(cc p) d -> p cc d", p=128), [], [wuk_ld])
        bi = h % 8
        bb = bank_bf(bi)
        for cc in range(4):
            k.tr(bb[:, cc * 128:(cc + 1) * 128], wuk_ld[:, cc, :], ident_b, [wuk_ld, cbf], [pb[bi]])
        k.cpa(wukT[:, h, :], bb[:, 0:512], [pb[bi]], [wukT])

    def evac_ckv(bi, grp, ncols):
        t_ = ev[0]
        k.cpa(t_[:, :], grp.ap, [grp.bufs], [t_])
        k.dma('sync', ckv_raw[bi * 128:(bi + 1) * 128, :], t_[:, :], [t_], [ckv_raw])
    linear_fm(hnT, ab_in_w, [[(0, 6144 + i * 128, 128)] for i in range(4)], wslots, evac_ckv, 'ckv')

    def evac_q(h, grp, ncols):
        qb = evb[2]
        k.cp('scalar', qb[:, :], grp.ap, [grp.bufs], [qb])
        for cc in range(4):
            g2 = G[(h + 1 + cc) % 2]
            for tc in range(4):
                k.mm(g2.ap[:, tc * 512:(tc + 1) * 512], wukT[:, h, cc * 128:(cc + 1) * 128],
                     qb[:, tc * 512:(tc + 1) * 512], True, True, [wukT, qb], [g2.bufs])
            ob = evb[cc % 2]
            if cc % 2 == 0:
                k.act(ob[:, :], g2.ap, AF.Copy, [g2.bufs], [ob], scale=float(128 ** -0.5))
            else:
                k.ts('vector', ob[:, :], g2.ap, float(128 ** -0.5), None, ALU.mult, None, [g2.bufs], [ob])
            k.dma('sync', qlat_s[h, cc, :, :], ob[:, :], [ob], [qlat_s])
    linear_fm(hnT, ab_in_w, [[(0, 4096 + h * 128, 128)] for h in range(16)], wslots, evac_q, 'q')

    def evac_ki(bi, grp, ncols):
        k.cp('scalar', kiT2[:, :], grp.ap, [grp.bufs], [kiT2])
    linear_fm(hnT, ab_in_w, [[(0, 8704, 64), (64, 8704, 64)]], wslots, evac_ki, 'ki')

    def evac_wi(bi, grp, ncols):
        t_ = ev[0]
        k.cp('scalar', t_[0:32, :], grp.ap[0:32, :], [grp.bufs], [t_])
        for tt in range(16):
            bi2 = 4 + tt % 4
            k.tr(bank(bi2)[:, 0:32], t_[0:32, tt * 128:(tt + 1) * 128], ident_f[0:32, 0:32], [t_, cst], [pb[bi2]])
            k.cp('vector', wi_tok[:, tt, :], bank(bi2)[:, 0:32], [pb[bi2]], [wi_tok])
    linear_fm(hnT, ab_in_w, [[(0, 8768, 32)]], wslots, evac_wi, 'wi')

    P.barrier()
    sb.reset(m)

    sb.reset(base_mark)
    ckvT = sb.alloc([128, 4, T], BF16, 'ckvT')
    ckv_tok = sb.alloc([128, 16, 512], BF16, 'ckv_tok')
    wuv = sb.alloc([128, 16, 4, 128], BF16, 'wuv')
    thr_c = sb.alloc([128, 1], F32, 'thr_c')
    k.memset('vector', thr_c[:, :], -1.0e29, [thr_c])
    for h in range(16):
        k.dma('gpsimd', wuv[:, h, :, :], w_uv[h, :, :].rearrange("(cc p) d -> p cc d", p=128), [], [wuv])
    m2 = sb.mark()
    craw = sb.alloc([128, 4, T], F32, 'craw')
    csq = sb.alloc([128, 4, T], F32, 'csq')
    rstd = sb.alloc([128, T], F32, 'rstd')
    k.dma('sync', craw[:, :, :], ckv_raw[:, :].rearrange("(cc p) t -> p cc t", p=128), [ckv_raw], [craw])
    k.act(csq[:, :, :], craw[:, :, :], AF.Square, [craw], [csq])
    for tc in range(4):
        for cc in range(4):
            k.mm(G[0].ap[:, tc * 512:(tc + 1) * 512], ones_f, csq[:, cc, tc * 512:(tc + 1) * 512],
                 cc == 0, cc == 3, [cst, csq], [G[0].bufs])
    k.ts('vector', rstd[:, :], G[0].ap, 1.0 / 512, 1e-6, ALU.mult, ALU.add, [G[0].bufs], [rstd])
    k.act(rstd[:, :], rstd[:, :], AF.Sqrt, [rstd], [rstd])
    k.recip(rstd[:, :], rstd[:, :], [rstd], [rstd])
    for cc in range(4):
        k.stt(ckvT[:, cc, :], craw[:, cc, :], ppc('kvn', cc), rstd[:, :], ALU.mult, ALU.mult,
              [craw, pp, rstd], [ckvT])
    for tt in range(16):
        bi = tt % 8
        bb = bank_bf(bi)
        for cc in range(4):
            k.tr(bb[:, cc * 128:(cc + 1) * 128], ckvT[:, cc, tt * 128:(tt + 1) * 128], ident_b,
                 [ckvT, cbf], [pb[bi]])
        k.cpa(ckv_tok[:, tt, :], bb[:, 0:512], [pb[bi]], [ckv_tok])
    P.barrier()
    sb.reset(m2)

    acc2 = [sb.alloc([128, T], F32, f'acc{i}') for i in range(2)]
    wk = [sb.alloc([128, T], F32, f'wk{i}') for i in range(2)]
    rt = [sb.alloc([128, T], F32, f'rt{i}') for i in range(2)]
    m8 = sb.alloc([128, 8], F32, 'm8')
    m01 = sb.alloc([128, T], BF16, 'm01')
    maskT2 = [sb.alloc([128, 16, 128], BF16, f'maskT{i}') for i in range(2)]
    qi_t = [sb.alloc([128, 16, 128], BF16, f'qi_t{i}') for i in range(2)]
    ql_t = [sb.alloc([128, 16, 4, 128], BF16, f'ql_t{i}') for i in range(2)]
    sgb_t = [sb.alloc([128, 16, 128], BF16, f'sgb_t{i}') for i in range(2)]
    eb = [sb.alloc([128, 512], BF16, f'eb{i}') for i in range(2)]
    ptb = [sb.alloc([128, 512], BF16, f'ptb{i}') for i in range(2)]
    rden = sb.alloc([128, 512], F32, 'rden')
    olb = sb.alloc([128, 4, 512], BF16, 'olb')
    yt1 = sb.alloc([128, 512], F32, 'yt1')
    yob = [sb.alloc([128, 4, 128], BF16, f'yob{i}') for i in range(2)]

    def steps_SC(i):
        S = (i + 1) * 128
        qit = qi_t[i % 2]
        acc = acc2[i % 2]
        nsc = (S + 511) // 512
        st = []

        def s_load():
            k.dma('sync', qit[:, :, :], qiT_s[:, i * 128:(i + 1) * 128].rearrange("(b p) t -> p b t", p=128),
                  [qiT_s], [qit])
        st.append(s_load)

        def mk_head(hh):
            def f():
                blk, half = hh // 2, hh % 2
                r_ = rt[hh % 2]
                for sc in range(nsc):
                    w_ = min(512, S - sc * 512)
                    k.mm(bank(7)[:, 0:w_], qit[half * 64:(half + 1) * 64, blk, :],
                         kiT2[half * 64:(half + 1) * 64, sc * 512:sc * 512 + w_], True, True, [qit, kiT2], [pb[7]])
                    k.act(r_[:, sc * 512:sc * 512 + w_], bank(7)[:, 0:w_], AF.Relu, [pb[7]], [r_])
                if hh == 0:
                    k.ts('vector', acc[:, 0:S], r_[:, 0:S], wi_tok[:, i, 0:1], None, ALU.mult, None,
                         [r_, wi_tok], [acc])
                else:
                    k.stt(acc[:, 0:S], r_[:, 0:S], wi_tok[:, i, hh:hh + 1], acc[:, 0:S], ALU.mult, ALU.add,
                          [r_, wi_tok, acc], [acc])
            return f
        for hh in range(32):
            st.append(mk_head(hh))

        def s_bias():
            k.tt('vector', acc[:, i * 128:S], acc[:, i * 128:S], cbias_f, ALU.add, [acc, cst], [acc])
        st.append(s_bias)
        return st

    def steps_TK(i):
        S = (i + 1) * 128
        acc = acc2[i % 2]
        maskT = maskT2[i % 2]
        qlt = ql_t[i % 2]
        sgt = sgb_t[i % 2]
        st = []

        def s_load():
            k.dma('sync', qlt[:, :, :, :], qlat_s[:, :, :, i * 128:(i + 1) * 128].rearrange("h c p t -> p h c t"),
                  [qlat_s], [qlt])
            k.dma('sync', sgt[:, :, :], sgb_s[:, i * 128:(i + 1) * 128].rearrange("(b p) t -> p b t", p=128),
                  [sgb_s], [sgt])
        st.append(s_load)
        if i >= 2:
            def mk_round(r):
                def f():
                    cur = acc if r == 0 else wk[(r - 1) % 2]
                    k.P.add('vector', (lambda c, S_: (lambda e: e.max(m8[:, :], c[:, 0:S_])))(cur, S), bl([cur]), bl([m8]))
                    if r < 31:
                        nxt = wk[r % 2]
                        k.P.add('vector', (lambda c, n, S_: (lambda e: e.match_replace(n[:, 0:S_], m8[:, :], c[:, 0:S_], -3.0e38)))(cur, nxt, S),
                                bl([cur, m8]), bl([nxt]))
                return f
            for r in range(32):
                st.append(mk_round(r))

        def s_mask():
            if i >= 2:
                thr, thr_b = m8[:, 7:8], [m8]
            else:
                thr, thr_b = thr_c[:, 0:1], [thr_c]
            k.ts('vector', m01[:, 0:S], acc[:, 0:S], thr, None, ALU.is_lt, None, [acc] + thr_b, [m01])
            k.ts('vector', m01[:, 0:S], m01[:, 0:S], -30000.0, None, ALU.mult, None, [m01], [m01])
            for j0 in range(0, i + 1, 8):
                bb = bank_bf(7)
                nj = min(8, i + 1 - j0)
                for j in range(j0, j0 + nj):
                    k.tr(bb[:, (j - j0) * 128:(j - j0 + 1) * 128], m01[:, j * 128:(j + 1) * 128], ident_b,
                         [m01, cbf], [pb[7]])
                k.cp('scalar', maskT[:, j0:j0 + nj, :], bb[:, 0:nj * 128].rearrange("p (a b) -> p a b", a=nj),
                     [pb[7]], [maskT])
        st.append(s_mask)
        return st

    def steps_AT(i):
        maskT = maskT2[i % 2]
        qlt = ql_t[i % 2]
        sgt = sgb_t[i % 2]
        st = []

        def L(hg, j):
            lb_ = j % 2
            for cc in range(4):
                k.mm(bank(lb_).rearrange("p (a b) -> p a b", a=4), ckvT[:, cc, j * 128:(j + 1) * 128],
                     qlt[:, hg * 4:(hg + 1) * 4, cc, :], cc == 0, False, [ckvT, qlt], [pb[lb_]])
            k.mm(bank(lb_).rearrange("p (a b) -> p a b", a=4), ident_b,
                 maskT[:, j:j + 1, :].broadcast_to([128, 4, 128]), False, True, [cbf, maskT], [pb[lb_]])

        def mk_j(hg, j):
            def f():
                if j == 0:
                    L(hg, 0)
                p_ = ptb[j % 2]
                k.act(p_[:, :], bank(j % 2), AF.Exp, [pb[j % 2]], [p_])
                if j + 1 <= i:
                    L(hg, j + 1)
                for cc in range(4):
                    k.mm(bank(2 + cc), ckv_tok[:, j, cc * 128:(cc + 1) * 128], p_[:, :],
                         j == 0, j == i, [ckv_tok, p_], [pb[2 + cc]])
                k.mm(bank(6), ones_b, p_[:, :], j == 0, j == i, [cbf, p_], [pb[6]])
            return f

        def mk_tail(hg):
            def f():
                k.act(rden[:, :], bank(6), AF.Ln, [pb[6]], [rden])
                k.act(rden[:, :], rden[:, :], AF.Exp, [rden], [rden], scale=-1.0)
                for cc in range(4):
                    k.cp('scalar', olb[:, cc, :], bank(2 + cc), [pb[2 + cc]], [olb])
                for hl in range(4):
                    h = hg * 4 + hl
                    for cc in range(4):
                        k.mm(bank(7)[:, hl * 128:(hl + 1) * 128], wuv[:, h, cc, :], olb[:, cc, hl * 128:(hl + 1) * 128],
                             cc == 0, cc == 3, [wuv, olb], [pb[7]])
                k.cp('scalar', yt1[:, :], bank(7), [pb[7]], [yt1])
                k.tt('gpsimd', yt1[:, :], yt1[:, :], rden[:, :], ALU.mult, [yt1, rden], [yt1])
                yo = yob[hg % 2]
                k.tt('gpsimd', yo[:, :, :], yt1[:, :].rearrange("p (a b) -> p a b", a=4), sgt[:, hg * 4:(hg + 1) * 4, :],
                     ALU.mult, [yt1, sgt], [yo])
                k.dma('sync', yT[2048 + hg * 512:2048 + (hg + 1) * 512, i * 128:(i + 1) * 128].rearrange("(a p) t -> p a t", p=128),
                      yo[:, :, :], [yo], [yT])
            return f
        for hg in range(4):
            for j in range(i + 1):
                st.append(mk_j(hg, j))
            st.append(mk_tail(hg))
        return st

    def merge(lists):
        items = []
        for li, l in enumerate(lists):
            n = len(l)
            for idx, f in enumerate(l):
                items.append(((idx + 0.5) / n, li, idx, f))
        items.sort(key=lambda t: (t[0], t[1], t[2]))
        return [t[3] for t in items]

    n_qt = 16
    for f in steps_SC(0) + steps_TK(0) + steps_SC(1):
        f()
    for s_ in range(n_qt):
        lists = [steps_AT(s_)]
        if s_ + 1 < n_qt:
            lists.append(steps_TK(s_ + 1))
        if s_ + 2 < n_qt:
            lists.append(steps_SC(s_ + 2))
        for f in merge(lists):
            f()
    P.barrier()
    sb.reset(base_mark)

    def out_proj(Wout, res_src, dst):
        m = sb.mark()
        yres = sb.alloc([128, 32, 1024], BF16, 'yres')
        wo = [sb.alloc([128, 32, 512], BF16, f'wo{i}') for i in range(2)]
        xr = [sb.alloc([128, 512], F32, f'xr{i}') for i in range(4)]
        n = 0
        for th in range(2):
            for q4 in range(4):
                k.dma('sync', yres[:, q4 * 8:(q4 + 1) * 8, :],
                      yT[q4 * 1024:(q4 + 1) * 1024, th * 1024:(th + 1) * 1024].rearrange("(fc p) t -> p fc t", p=128),
                      [yT], [yres])
            for cb in range(8):
                w_ = wo[n % 2]
                n += 1
                for hf in range(2):
                    k.dma('gpsimd', w_[:, hf * 16:(hf + 1) * 16, :],
                          Wout[hf * 2048:(hf + 1) * 2048, cb * 512:(cb + 1) * 512].rearrange("(fc p) c -> p fc c", p=128),
                          [], [w_])
                for t8 in range(8):
                    tt = th * 8 + t8
                    bi = (cb * 8 + t8) % 8
                    xr_ = xr[(cb * 8 + t8) % 4]
                    k.dma('sync', xr_[:, :], res_src[tt * 128:(tt + 1) * 128, cb * 512:(cb + 1) * 512], [res_src], [xr_])
                    for fc in range(32):
                        k.mm(bank(bi), yres[:, fc, t8 * 128:(t8 + 1) * 128], w_[:, fc, :],
                             fc == 0, fc == 31, [yres, w_], [pb[bi]])
                    k.tt('vector', xr_[:, :], bank(bi), xr_[:, :], ALU.add, [pb[bi], xr_], [xr_])
                    k.dma('sync', dst[tt * 128:(tt + 1) * 128, cb * 512:(cb + 1) * 512], xr_[:, :], [xr_], [dst])
        P.barrier()
        sb.reset(m)

    out_proj(ab_out_w, x, h1)

    if stop_after == 'l0':
        P.emit()
        return nc

    sb.reset(base_mark)
    hnT = sb.alloc([128, 32, T], BF16, 'hnT1')
    phase_norm_T(h1, norm_w[1, :], hnT)
    l1_mark = sb.mark()
    wslots = [sb.alloc([128, 32, 128], BF16, f'w{i}') for i in range(2)]
    m = sb.mark()
    evb = [sb.alloc([128, T], BF16, f'evb{i}') for i in range(3)]
    a1 = sb.alloc([128, T + 4], F32, 'a1')
    a2 = sb.alloc([128, T], F32, 'a2')
    a3 = sb.alloc([128, T], F32, 'a3')
    rstb = sb.alloc([128, T], BF16, 'rstb')
    sm = sb.alloc([128, 160], F32, 'sm')
    k.dma('gpsimd', rstb[:, :], rst_d[:, :], [], [rstb])
    k.memset('vector', a1[:, 0:3], 0.0, [a1])
    lb_ = sm[:, 0:16]
    oml = sm[:, 16:32]
    a_b = sm[:, 32:64]
    o_l, _ = PP['lbs']
    k.tt('vector', lb_, pp[:, o_l + 16:o_l + 32], pp[:, o_l:o_l + 16], ALU.subtract, [pp], [sm])
    k.act(lb_, lb_, AF.Sigmoid, [sm], [sm])
    k.ts('vector', oml, lb_, -1.0, 1.0, ALU.mult, ALU.add, [sm], [sm])
    k.dma('sync', a_b, a_log[:].partition_broadcast(128), [], [sm])
    k.act(a_b, a_b, AF.Exp, [sm], [sm])
    k.ts('vector', a_b, a_b, -1.0, None, ALU.mult, None, [sm], [sm])

    def evac_z(bi, grp, ncols):
        ob = evb[bi % 3]
        k.act(ob[:, :], grp.ap, AF.Silu, [grp.bufs], [ob])
        k.dma('sync', sz_s[bi * 128:(bi + 1) * 128, :], ob[:, :], [ob], [sz_s])
    linear_fm(hnT, cd_in_w, [[(0, i * 128, 128)] for i in range(16)], wslots, evac_z, 'z')

    def evac_xbc(bi, grp, ncols):
        conv_block(grp, a1, a2, 'cw1', 'cb1', bi)
        ob = evb[bi % 3]
        k.act(ob[:, :], a2[:, :], AF.Silu, [a2], [ob])
        if bi < 16:
            k.dma('sync', xc_s[bi * 128:(bi + 1) * 128, :], ob[:, :], [ob], [xc_s])
        elif bi < 20:
            k.dma('sync', BT_s[(bi - 16) * 128:(bi - 15) * 128, :], ob[:, :], [ob], [BT_s])
        else:
            k.dma('sync', CT_s[(bi - 20) * 128:(bi - 19) * 128, :], ob[:, :], [ob], [CT_s])
    linear_fm(hnT, cd_in_w, [[(0, 2048 + i * 128, 128)] for i in range(24)], wslots, evac_xbc, 'xbc')

    def evac_dt(bi, grp, ncols):
        k.act(a3[0:32, :], grp.ap[0:32, :], AF.Exp, [grp.bufs, pp], [a3], bias=pp[0:32, PP['dtb'][0]:PP['dtb'][0] + 1])
        k.act(a3[0:32, :], a3[0:32, :], AF.Ln, [a3], [a3], bias=1.0)
        for tt in range(16):
            bi2 = 4 + tt % 4
            k.tr(bank(bi2)[:, 0:32], a3[0:32, tt * 128:(tt + 1) * 128], ident_f[0:32, 0:32], [a3, cst], [pb[bi2]])
            k.cp('vector', dt_tok[:, tt, :], bank(bi2)[:, 0:32], [pb[bi2]], [dt_tok])
        k.tt('vector', dtA_tok[:, :, :], dt_tok[:, :, :], a_b.unsqueeze(1).broadcast_to([128, 16, 32]), ALU.mult,
             [dt_tok, sm], [dtA_tok])
    linear_fm(hnT, cd_in_w, [[(0, 5120, 32)]], wslots, evac_dt, 'dt')

    def evac_fq(bi, grp, ncols):
        h = bi // 2
        if bi % 2 == 0:
            k.act(a1[:, 0:T], grp.ap, AF.Sigmoid, [grp.bufs], [a1])
            k.ts('vector', a1[:, 0:T], a1[:, 0:T], oml[:, h:h + 1], lb_[:, h:h + 1], ALU.mult, ALU.add, [a1, sm], [a1])
            k.act(a2[:, :], a1[:, 0:T], AF.Ln, [a1], [a2])
            k.scan(a3[:, :], rstb[:, :], a2[:, :], 0.0, ALU.mult, ALU.add, [rstb, a2], [a3])
            k.ts('vector', a1[:, 0:T], a1[:, 0:T], -1.0, 1.0, ALU.mult, ALU.add, [a1], [a1])
            k.act(a2[:, :], a3[:, :], AF.Exp, [a3], [a2], scale=-1.0)
            ob = evb[0]
            k.tt('vector', ob[:, :], a1[:, 0:T], a2[:, :], ALU.mult, [a1, a2], [ob])
            k.dma('sync', kt_s[h * 128:(h + 1) * 128, :], ob[:, :], [ob], [kt_s])
            a3v = a3[:, :].rearrange("p (c l) -> p c l", c=16)
            a2v = a2[:, :].rearrange("p (c l) -> p c l", c=16)
            k.tt('vector', a2v, a3v[:, :, 127:128].broadcast_to([128, 16, 128]), a3v, ALU.subtract, [a3], [a2])
            k.act(a2[:, :], a2[:, :], AF.Exp, [a2], [a2])
            ob = evb[1]
            k.tt('vector', ob[:, :], a1[:, 0:T], a2[:, :], ALU.mult, [a1, a2], [ob])
            k.dma('sync', kd_s[h * 128:(h + 1) * 128, :], ob[:, :], [ob], [kd_s])
            k.act(ebl[:, h, :], a3v[:, :, 127], AF.Exp, [a3], [ebl])
        else:
            k.act(a1[:, 0:T], grp.ap, AF.Silu, [grp.bufs], [a1])
            k.act(a2[:, :], a3[:, :], AF.Exp, [a3], [a2])
            ob = evb[2]
            k.tt('vector', ob[:, :], a1[:, 0:T], a2[:, :], ALU.mult, [a1, a2], [ob])
            k.dma('sync', qt_s[h * 128:(h + 1) * 128, :], ob[:, :], [ob], [qt_s])
    blocks = []
    for h in range(16):
        blocks.append([(0, 7200 + h * 128, 128)])
        blocks.append([(0, 5152 + h * 128, 128)])
    linear_fm(hnT, cd_in_w, blocks, wslots, evac_fq, 'fq')

    def evac_gd(bi, grp, ncols):
        ob = evb[bi % 3]
        k.act(ob[:, :], grp.ap, AF.Silu, [grp.bufs], [ob])
        k.dma('sync', sgd_s[bi * 128:(bi + 1) * 128, :], ob[:, :], [ob], [sgd_s])
    linear_fm(hnT, cd_in_w, [[(0, 11296 + i * 128, 128)] for i in range(16)], wslots, evac_gd, 'gd')
    P.barrier()
    sb.reset(l1_mark)

    wo = [sb.alloc([128, 32, 256], BF16, f'wv{i}') for i in range(2)]
    vb = [sb.alloc([128, 256], BF16, f'vb{i}') for i in range(4)]
    n = 0
    for sl in range(8):
        w_ = wo[sl % 2]
        k.dma('gpsimd', w_[:, :, :], cd_in_w[:, 9248 + sl * 256:9248 + (sl + 1) * 256].rearrange("(kc p) c -> p kc c", p=128),
              [], [w_])
        for tt in range(16):
            bi = n % 8
            v_ = vb[n % 4]
            n += 1
            for kc in range(32):
                k.mm(bank(bi)[:, 0:256], hnT[:, kc, tt * 128:(tt + 1) * 128], w_[:, kc, :], kc == 0, kc == 31,
                     [hnT, w_], [pb[bi]])
            k.cpa(v_[:, :], bank(bi)[:, 0:256], [pb[bi]], [v_])
            k.dma('sync', v_s[tt * 128:(tt + 1) * 128, sl * 256:(sl + 1) * 256], v_[:, :], [v_], [v_s])
    P.barrier()
    sb.reset(base_mark)

    sprev = sb.alloc([128, 4, 512], F32, 'sprev')
    sprev_b = sb.alloc([128, 4, 512], BF16, 'sprev_b')
    hprev = sb.alloc([128, 16, 128], F32, 'hprev')
    hprev_b = sb.alloc([128, 16, 128], BF16, 'hprev_b')
    Db = sb.alloc([128, 32], F32, 'Db')
    k.memset('vector', sprev[:, :, :], 0.0, [sprev])
    k.memset('vector', sprev_b[:, :, :], 0.0, [sprev_b])
    k.memset('vector', hprev[:, :, :], 0.0, [hprev])
    k.memset('vector', hprev_b[:, :, :], 0.0, [hprev_b])
    k.dma('sync', Db[:, :], d_skip[:].partition_broadcast(128), [], [Db])

    def dbl(shape, dt, name):
        return [sb.alloc(shape, dt, f'{name}{i}') for i in range(2)]
    xcT_c = dbl([128, 16, 128], BF16, 'xcT_c')
    BT_c = dbl([128, 4, 128], BF16, 'BT_c')
    CT_c = dbl([128, 4, 128], BF16, 'CT_c')
    sz_c = dbl([128, 16, 128], BF16, 'sz_c')
    qt_c = dbl([128, 16, 128], BF16, 'qt_c')
    kt_c = dbl([128, 16, 128], BF16, 'kt_c')
    kd_c = dbl([128, 16, 128], BF16, 'kd_c')
    sgd_c = dbl([128, 16, 128], BF16, 'sgd_c')
    v_c = dbl([128, 2048], BF16, 'v_c')
    x_tok = sb.alloc([128, 32, 64], BF16, 'x_tok')
    B_tok = sb.alloc([128, 4, 128], BF16, 'B_tok')
    sml = sb.alloc([128, 256], F32, 'sml')
    xdt = sb.alloc([128, 32, 64], BF16, 'xdt')
    xdtd = sb.alloc([128, 32, 64], BF16, 'xdtd')
    cbm = sb.alloc([128, 4, 128], F32, 'cbm')
    yoff = sb.alloc([128, 32, 64], F32, 'yoff')
    Lm4 = [sb.alloc([128, 4, 128], F32, f'Lm{i}') for i in range(2)]
    seg4 = [sb.alloc([128, 512], F32, f'seg{i}') for i in range(2)]
    MT4 = [sb.alloc([128, 4, 128], BF16, f'MT{i}') for i in range(2)]
    yv = sb.alloc([128, 2048], F32, 'yv')
    gT = sb.alloc([128, 16, 128], F32, 'gT')
    sq = sb.alloc([128, 16, 128], F32, 'sq')
    t2v = sq[:, :, :].rearrange("p a b -> p (a b)")
    rs = sb.alloc([128, 4, 128], F32, 'rs')
    ycb = dbl([128, 16, 128], BF16, 'ycb')
    kd_tok = sb.alloc([128, 16, 128], BF16, 'kd_tok')
    attm4 = [sb.alloc([128, 4, 128], BF16, f'attm{i}') for i in range(2)]
    rsh = sb.alloc([128, 2048], F32, 'rsh')
    ydb = dbl([128, 16, 128], BF16, 'ydb')
    o_hg, _ = PP['hgn']

    osb = sb.alloc([128, 2048], F32, 'osb')
    sqh = sb.alloc([128, 2048], F32, 'sqh')

    def chunk_steps(c):
        cs = slice(c * 128, (c + 1) * 128)
        p2 = c % 2
        xT, Bc, Cc, szc = xcT_c[p2], BT_c[p2], CT_c[p2], sz_c[p2]
        qtc, ktc, kdc, sgc, vc = qt_c[p2], kt_c[p2], kd_c[p2], sgd_c[p2], v_c[p2]
        eacs = sml[:, 64:96]
        dte = sml[:, 96:128]
        cdb = sml[:, 128:160]
        x_tok2 = x_tok[:, :, :].rearrange("p a b -> p (a b)")
        ssd = []
        hg_ = []

        def ld(dst, src):
            k.dma('sync', dst[:, :, :], src[:, cs].rearrange("(b p) t -> p b t", p=128), [src], [dst])

        def s_load_all():
            ld(xT, xc_s)
            ld(Bc, BT_s)
            ld(Cc, CT_s)
            ld(szc, sz_s)
            ld(qtc, qt_s)
            ld(ktc, kt_s)
            ld(kdc, kd_s)
            ld(sgc, sgd_s)
            k.dma('sync', vc[:, :], v_s[cs, :], [v_s], [vc])

        def s_pro1():
            for half in range(2):
                bb = bank_bf(4)
                for j in range(8):
                    k.tr(bb[:, j * 128:(j + 1) * 128], xT[:, half * 8 + j, :], ident_b, [xT, cbf], [pb[4]])
                k.cpa(x_tok2[:, half * 1024:(half + 1) * 1024], bb[:, 0:1024], [pb[4]], [x_tok])
            bb = bank_bf(4)
            for g in range(4):
                k.tr(bb[:, g * 128:(g + 1) * 128], Bc[:, g, :], ident_b, [Bc, cbf], [pb[4]])
            k.cpa(B_tok[:, :, :].rearrange("p a b -> p (a b)"), bb[:, 0:512], [pb[4]], [B_tok])
            k.mm(bank(4)[:, 0:32], le_f, dtA_tok[:, c, :], True, True, [cst, dtA_tok], [pb[4]])
            k.mm(bank(4)[:, 32:64], ones_f, dtA_tok[:, c, :], True, True, [cst, dtA_tok], [pb[4]])
            k.cp('scalar', sml[:, 0:64], bank(4)[:, 0:64], [pb[4]], [sml])
            k.act(sml[:, 64:96], sml[:, 0:32], AF.Exp, [sml], [sml])
            k.tt('vector', sml[:, 96:128], sml[:, 32:64], sml[:, 0:32], ALU.subtract, [sml], [sml])
            k.act(sml[:, 96:128], sml[:, 96:128], AF.Exp, [sml], [sml])
            k.act(sml[:, 128:160], sml[:, 32:64], AF.Exp, [sml], [sml])
            k.tt('vector', xdt[:, :, :], x_tok[:, :, :], dt_tok[:, c, :].unsqueeze(2).broadcast_to([128, 32, 64]), ALU.mult,
                 [x_tok, dt_tok], [xdt])
            k.tt('vector', xdtd[:, :, :], xdt[:, :, :], dte.unsqueeze(2).broadcast_to([128, 32, 64]), ALU.mult,
                 [xdt, sml], [xdtd])
        ssd.append(s_pro1)

        def s_pro2():
            for g in range(4):
                k.mm(bank(4)[:, g * 128:(g + 1) * 128], Bc[:, g, :], Cc[:, g, :], True, True, [Bc, Cc], [pb[4]])
            k.tt('vector', cbm[:, :, :], bank(4).rearrange("p (a b) -> p a b", a=4),
                 le_f.unsqueeze(1).broadcast_to([128, 4, 128]), ALU.mult, [pb[4], cst], [cbm])
            for g in range(4):
                k.mm(bank(4), Cc[:, g, :], sprev_b[:, g, :], True, True, [Cc, sprev_b], [pb[4]])
                k.tt('vector', yoff[:, g * 8:(g + 1) * 8, :], bank(4).rearrange("p (a b) -> p a b", a=8),
                     eacs[:, g * 8:(g + 1) * 8].unsqueeze(2).broadcast_to([128, 8, 64]), ALU.mult, [pb[4], sml], [yoff])
        ssd.append(s_pro2)

        def mk_ssd_batch(bq):
            def f():
                hd0 = bq * 4
                g = bq // 2
                yb_ = g % 2
                L4 = Lm4[bq % 2]
                k.tt('gpsimd' if bq % 2 else 'vector', L4[:, :, :], gt_f.unsqueeze(1).broadcast_to([128, 4, 128]),
                     dtA_tok[:, c, hd0:hd0 + 4].unsqueeze(2).broadcast_to([128, 4, 128]), ALU.mult, [cst, dtA_tok], [L4])
                for q in range(4):
                    k.mm(bank(6)[:, q * 128:(q + 1) * 128], L4[:, q, :], le_f, True, True, [L4, cst], [pb[6]])
                s4 = seg4[bq % 2]
                k.act(s4[:, :], bank(6), AF.Exp, [pb[6]], [s4])
                M4 = MT4[bq % 2]
                k.tt('vector', M4[:, :, :], s4[:, :].rearrange("p (a b) -> p a b", a=4),
                     cbm[:, g:g + 1, :].broadcast_to([128, 4, 128]), ALU.mult, [s4, cbm], [M4])
                for q in range(4):
                    hd = hd0 + q
                    k.mm(bank(yb_)[:, (hd % 8) * 64:(hd % 8 + 1) * 64], M4[:, q, :], xdt[:, hd, :], True, True,
                         [M4, xdt], [pb[yb_]])
                if bq % 2 == 1:
                    k.tt('vector', yv[:, g * 512:(g + 1) * 512], bank(yb_),
                         yoff[:, g * 8:(g + 1) * 8, :].rearrange("p a b -> p (a b)"), ALU.add, [pb[yb_], yoff], [yv])
            return f
        for bq in range(8):
            ssd.append(mk_ssd_batch(bq))

        def s_states():
            for g in range(4):
                k.mm(bank(4), B_tok[:, g, :], xdtd[:, g * 8:(g + 1) * 8, :].rearrange("p a b -> p (a b)"), True, True,
                     [B_tok, xdtd], [pb[4]])
                spv = sprev[:, g, :].rearrange("p (a b) -> p a b", a=8)
                k.tt('vector', spv, spv, cdb[:, g * 8:(g + 1) * 8].unsqueeze(2).broadcast_to([128, 8, 64]), ALU.mult,
                     [sprev, sml], [sprev])
                k.tt('vector', sprev[:, g, :], sprev[:, g, :], bank(4), ALU.add, [sprev, pb[4]], [sprev])
                k.cp('scalar', sprev_b[:, g, :], sprev[:, g, :], [sprev], [sprev_b])
            k.tt('gpsimd', t2v.rearrange("p (a b) -> p a b", a=32), x_tok[:, :, :], Db[:, :].unsqueeze(2).broadcast_to([128, 32, 64]), ALU.mult,
                 [x_tok, Db], [sq])
            k.tt('vector', yv[:, :], yv[:, :], t2v, ALU.add, [yv, sq], [yv])
        ssd.append(s_states)

        def s_epi():
            for q4 in range(4):
                for j in range(4):
                    fc = q4 * 4 + j
                    k.tr(bank(4)[:, j * 128:(j + 1) * 128], yv[:, fc * 128:(fc + 1) * 128], ident_f, [yv, cst], [pb[4]])
                k.tt('vector', gT[:, q4 * 4:(q4 + 1) * 4, :], bank(4).rearrange("p (a b) -> p a b", a=4),
                     szc[:, q4 * 4:(q4 + 1) * 4, :], ALU.mult, [pb[4], szc], [gT])
            k.act(sq[:, :, :], gT[:, :, :], AF.Square, [gT], [sq])
            for g in range(4):
                for j in range(4):
                    k.mm(bank(4)[:, g * 128:(g + 1) * 128], ones_f, sq[:, g * 4 + j, :], j == 0, j == 3, [cst, sq], [pb[4]])
            rs2 = rs[:, :, :].rearrange("p a b -> p (a b)")
            k.ts('vector', rs2, bank(4), 1.0 / 512, 1e-6, ALU.mult, ALU.add, [pb[4]], [rs])
            k.act(rs2, rs2, AF.Sqrt, [rs], [rs])
            k.recip(rs2, rs2, [rs], [rs])
            yc_ = ycb[p2]
            for fc in range(16):
                k.stt(yc_[:, fc, :], gT[:, fc, :], ppc('ssdn', fc), rs[:, fc // 4, :], ALU.mult, ALU.mult,
                      [gT, pp, rs], [yc_])
            k.dma('sync', yT[0:2048, cs].rearrange("(b p) t -> p b t", p=128), yc_[:, :, :], [yc_], [yT])
        ssd.append(s_epi)


        def h_pro():
            for half in range(2):
                bb = bank_bf(5)
                for j in range(8):
                    k.tr(bb[:, j * 128:(j + 1) * 128], kdc[:, half * 8 + j, :], ident_b, [kdc, cbf], [pb[5]])
                k.cpa(kd_tok[:, half * 8:(half + 1) * 8, :].rearrange("p a b -> p (a b)"), bb[:, 0:1024], [pb[5]], [kd_tok])
        hg_.append(h_pro)

        def mk_h_batch(hq):
            def f():
                h0 = hq * 4
                ob_ = 2 + hq % 2
                for q in range(4):
                    h = h0 + q
                    k.mm(bank(7)[:, q * 128:(q + 1) * 128], ktc[:, h, :], qtc[:, h, :], True, True, [ktc, qtc], [pb[7]])
                am4 = attm4[hq % 2]
                k.tt('vector', am4[:, :, :], bank(7).rearrange("p (a b) -> p a b", a=4),
                     le_f.unsqueeze(1).broadcast_to([128, 4, 128]), ALU.mult, [pb[7], cst], [am4])
                for q in range(4):
                    h = h0 + q
                    hs = slice(h * 128, (h + 1) * 128)
                    oc = slice(q * 128, (q + 1) * 128)
                    k.mm(bank(ob_)[:, oc], vc[:, hs], am4[:, q, :], True, False, [vc, am4], [pb[ob_]])
                    k.mm(bank(ob_)[:, oc], hprev_b[:, h, :], qtc[:, h, :], False, True, [hprev_b, qtc], [pb[ob_]])
                for q in range(4):
                    h = h0 + q
                    hs = slice(h * 128, (h + 1) * 128)
                    k.mm(bank(5)[:, q * 128:(q + 1) * 128], kd_tok[:, h, :], vc[:, hs], True, True, [kd_tok, vc], [pb[5]])
                hp4 = hprev[:, h0:h0 + 4, :]
                k.tt('vector', hp4, hp4, ebl[:, h0:h0 + 4, c:c + 1].broadcast_to([128, 4, 128]), ALU.mult,
                     [hprev, ebl], [hprev])
                k.tt('vector', hp4, hp4, bank(5).rearrange("p (a b) -> p a b", a=4), ALU.add, [hprev, pb[5]], [hprev])
                k.cp('scalar', hprev_b[:, h0:h0 + 4, :], hp4, [hprev], [hprev_b])
                k.cp('scalar', osb[:, hq * 512:(hq + 1) * 512], bank(ob_), [pb[ob_]], [osb])
                k.act(sqh[:, hq * 512:(hq + 1) * 512], bank(ob_), AF.Square, [pb[ob_]], [sqh])
            return f
        for hq in range(4):
            hg_.append(mk_h_batch(hq))

        def h_epi():
            for q4 in range(4):
                qs = slice(q4 * 512, (q4 + 1) * 512)
                k.mm(bank(5), ones_f, sqh[:, qs], True, True, [cst, sqh], [pb[5]])
                k.ts('vector', rsh[:, qs], bank(5), 1.0 / 128, 1e-6, ALU.mult, ALU.add, [pb[5]], [rsh])
            k.act(rsh[:, :], rsh[:, :], AF.Sqrt, [rsh], [rsh])
            k.recip(rsh[:, :], rsh[:, :], [rsh], [rsh])
            k.tt('vector', rsh[:, :], osb[:, :], rsh[:, :], ALU.mult, [osb, rsh], [rsh])
            rsh3 = rsh[:, :].rearrange("p (a b) -> p a b", a=16)
            k.tt('gpsimd', rsh3, rsh3, pp[:, o_hg:o_hg + 16].unsqueeze(2).broadcast_to([128, 16, 128]), ALU.mult,
                 [rsh, pp], [rsh])
            yd_ = ydb[p2]
            k.tt('vector', yd_[:, :, :], rsh3, sgc[:, :, :], ALU.mult, [rsh, sgc], [yd_])
            k.dma('sync', yT[2048:4096, cs].rearrange("(b p) t -> p b t", p=128), yd_[:, :, :], [yd_], [yT])
        hg_.append(h_epi)
        return ssd, hg_, s_load_all

    def merge2(lists):
        items = []
        for li, l in enumerate(lists):
            n = len(l)
            for idx, f in enumerate(l):
                items.append(((idx + 0.5) / n, li, idx, f))
        items.sort(key=lambda t: (t[0], t[1], t[2]))
        return [t[3] for t in items]

    cks = [chunk_steps(c) for c in range(16)]
    cks[0][2]()
    for c in range(16):
        a_, b_, _ = cks[c]
        if c + 1 < 16:
            cks[c + 1][2]()
        for f in merge2([a_, b_]):
            f()
    P.barrier()
    sb.reset(base_mark)

    out_proj(cd_out_w, h1, h2)

    nwb = sb.alloc([128, D], F32, 'fnw')
    xt2 = [sb.alloc([128, D], F32, f'fx{i}') for i in range(2)]
    xo2 = [sb.alloc([128, D], F32, f'fo{i}') for i in range(2)]
    junk = sb.alloc([128, D], BF16, 'fjunk')
    st = sb.alloc([128, 4], F32, 'fst')
    k.dma('sync', nwb[:, :], final_norm[:].partition_broadcast(128), [], [nwb])
    for tt in range(16):
        xt = xt2[tt % 2]
        xo = xo2[tt % 2]
        k.dma('sync', xt[:, :], h2[tt * 128:(tt + 1) * 128, :], [h2], [xt])
        k.act(junk[:, :], xt[:, :], AF.Square, [xt], [junk, st], accum=st[:, 0:1])
        k.ts('vector', st[:, 1:2], st[:, 0:1], 1.0 / D, 1e-6, ALU.mult, ALU.add, [st], [st])
        k.act(st[:, 2:3], st[:, 1:2], AF.Sqrt, [st], [st])
        k.recip(st[:, 3:4], st[:, 2:3], [st], [st])
        k.stt(xo[:, :], xt[:, :], st[:, 3:4], nwb[:, :], ALU.mult, ALU.mult, [xt, st, nwb], [xo])
        k.dma('sync', out_d[tt * 128:(tt + 1) * 128, :], xo[:, :], [xo], [out_d])
    P.emit()
    return nc


_W_KEYS = (("norm_w", None), ("final_norm", None), ("ab_in_w", 0), ("ab_rg_wa", 0), ("ab_rg_wx", 0),
           ("ab_w_uk", 0), ("ab_w_uv", 0), ("ab_out_w", 0), ("cd_in_w", 0), ("cd_a_log", 0),
           ("cd_d_skip", 0), ("cd_out_w", 0))


def make_in_maps(inp, batches):
    common = {}
    for name, idx in _W_KEYS:
        a = np.asarray(inp[name], dtype=np.float32)
        common[name] = np.ascontiguousarray(a if idx is None else a[idx])
    common["pp"] = pack_params(inp)
    cst, rst = make_consts()
    common["cst"] = cst
    common["rst"] = rst
    maps = []
    for b in batches:
        d = dict(common)
        d["x"] = np.ascontiguousarray(np.asarray(inp["x"][b], dtype=np.float32))
        maps.append(d)
    return maps


def kernel(**inputs):
    nc = build()
    in_maps = make_in_maps(inputs, range(8))
    res = run_bass_kernel_spmd(nc, in_maps, core_ids=list(range(8)))
    return np.stack([np.asarray(res.results[b]["out"], dtype=np.float32) for b in range(8)], axis=0)
```

```python
import numpy as np
import concourse.bass as bass
import concourse.mybir as mybir
from concourse.bass_utils import run_bass_kernel_spmd

F32 = mybir.dt.float32
BF16 = mybir.dt.bfloat16
AF = mybir.ActivationFunctionType
ALU = mybir.AluOpType
T = 2048
D = 4096
AB_COLS = 10848
CD_COLS = 13344
EPOCH = 30000
SB_BASE = 16640
SB_CAP = SB_BASE + 207 * 1024
NEG = -1.0e30


class Ins:
    __slots__ = ('eng', 'fn', 'deps', 'signal', 'seq', 'dkey', 'dcnt', 'order', 'isdma')


class Buf:
    __slots__ = ('name', 'w', 'r')

    def __init__(self, name=''):
        self.name = name
        self.w = {}
        self.r = {}


class Tn:
    def __init__(self, h, name=''):
        self.h = h
        self.b = Buf(name)

    def __getitem__(self, k):
        return self.h[k]


class Prog:
    ENGS = ('tensor', 'vector', 'scalar', 'gpsimd', 'sync')

    def __init__(self, nc, ndma=12):
        self.nc = nc
        self.streams = {e: [] for e in self.ENGS}
        self.order = 0
        self.ndma = ndma
        self.rr = {}
        self.dlast = {}
        self.dcount = {}
        self.lastc = {}
        self.barrier_deps = {e: None for e in self.ENGS}

    def add(self, eng, fn, reads=(), writes=(), dma=False):
        ins = Ins()
        ins.eng = eng
        ins.fn = fn
        ins.signal = False
        ins.seq = None
        ins.isdma = dma
        ins.dcnt = 0
        ins.order = self.order
        self.order += 1
        deps = {}

        def need(d):
            k = d.dkey
            if k not in deps or deps[k].order < d.order:
                deps[k] = d
        for b in reads:
            for d in b.w.values():
                need(d)
        for b in writes:
            for d in b.w.values():
                need(d)
            for d in b.r.values():
                need(d)
        bd = self.barrier_deps[eng]
        if bd is not None:
            for d in bd:
                need(d)
            self.barrier_deps[eng] = None
        if dma:
            k = self.rr.get(eng, 0)
            self.rr[eng] = (k + 1) % self.ndma
            ins.dkey = ('d', eng, k)
            prev = self.dlast.get(ins.dkey)
            if prev is not None:
                need(prev)
            self.dlast[ins.dkey] = ins
            ins.dcnt = self.dcount.get(ins.dkey, 0) + 16
            self.dcount[ins.dkey] = ins.dcnt
        else:
            ins.dkey = eng
            if eng == 'tensor':
                deps.pop('tensor', None)
            self.lastc[eng] = ins
        for d in deps.values():
            if not d.isdma:
                d.signal = True
        ins.deps = list(deps.values())
        for b in writes:
            if b.r:
                b.w = {}
                b.r = {}
            b.w[ins.dkey] = ins
        for b in reads:
            b.r[ins.dkey] = ins
        self.streams[eng].append(ins)
        return ins

    def barrier(self):
        deps = list(self.lastc.values()) + list(self.dlast.values())
        for e in self.ENGS:
            self.barrier_deps[e] = list(deps)

    def emit(self):
        nc = self.nc
        self.barrier()
        for e in self.ENGS:
            self.add(e, None)
        nsig = {}
        for e, st in self.streams.items():
            n = 0
            for ins in st:
                if (not ins.isdma) and ins.signal:
                    n += 1
                    ins.seq = n
            nsig[e] = n
        csem = {e: [nc.alloc_semaphore(name=f"c_{e}_{i}") for i in range((nsig[e] + EPOCH - 1) // EPOCH)]
                for e in self.ENGS}
        dsem = {k: nc.alloc_semaphore(name=f"d_{k[1]}_{k[2]}") for k in self.dcount}
        streams = self.streams

        def run(ename, eng):
            waited = {}
            maxep = {}
            for ins in streams[ename]:
                for d in ins.deps:
                    if d.isdma:
                        key = d.dkey
                        sem = dsem[key]
                        val = d.dcnt
                    else:
                        ep = (d.seq - 1) // EPOCH
                        if maxep.get(d.eng, -1) > ep:
                            continue
                        key = (d.eng, ep)
                        sem = csem[d.eng][ep]
                        val = (d.seq - 1) % EPOCH + 1
                    if waited.get(key, 0) >= val:
                        continue
                    eng.wait_ge(sem, val)
                    waited[key] = val
                    if not d.isdma:
                        maxep[d.eng] = max(maxep.get(d.eng, -1), ep)
                if ins.fn is None:
                    continue
                r = ins.fn(eng)
                if ins.isdma:
                    r.then_inc(dsem[ins.dkey], 16)
                elif ins.signal:
                    ep = (ins.seq - 1) // EPOCH
                    r.then_inc(csem[ename][ep], 1)

        with nc.Block() as block:
            @block.tensor
            def _(e):
                run('tensor', e)

            @block.vector
            def _(e):
                run('vector', e)

            @block.scalar
            def _(e):
                run('scalar', e)

            @block.gpsimd
            def _(e):
                run('gpsimd', e)

            @block.sync
            def _(e):
                run('sync', e)


def _size(dt):
    return 2 if dt == BF16 else 4


class SBAlloc:
    def __init__(self, nc):
        self.nc = nc
        self.ptr = SB_BASE
        self.n = 0

    def alloc(self, shape, dtype, name='t'):
        nb = int(np.prod(shape[1:])) * _size(dtype)
        off = (self.ptr + 63) // 64 * 64
        self.ptr = off + nb
        assert self.ptr <= SB_CAP, f"SBUF overflow {self.ptr} at {name}"
        self.n += 1
        h = self.nc.alloc_sbuf_tensor_at(f"{name}_{self.n}", list(shape), dtype, offset=off)
        return Tn(h, name)

    def mark(self):
        return self.ptr

    def reset(self, m):
        self.ptr = m


def bl(x):
    out = []
    for t in x:
        if isinstance(t, Tn):
            out.append(t.b)
        elif isinstance(t, Buf):
            out.append(t)
        else:
            out.extend(t)
    return out


class K:
    def __init__(self, P):
        self.P = P
        self.alt = 0

    def mm(self, out, lhsT, rhs, start, stop, R, W):
        self.P.add('tensor', lambda e: e.matmul(out, lhsT, rhs, start=start, stop=stop), bl(R), bl(W))

    def tr(self, out, in_, ident, R, W):
        self.P.add('tensor', lambda e: e.transpose(out, in_, ident), bl(R), bl(W))

    def act(self, out, in_, func, R, W, bias=None, scale=None, accum=None):
        kw = {}
        if bias is not None:
            kw['bias'] = bias
        if scale is not None:
            kw['scale'] = scale
        if accum is not None:
            kw['accum_out'] = accum
        self.P.add('scalar', lambda e: e.activation(out, in_, func, **kw), bl(R), bl(W))

    def tt(self, eng, out, in0, in1, op, R, W):
        self.P.add(eng, lambda e: e.tensor_tensor(out, in0, in1, op), bl(R), bl(W))

    def ts(self, eng, out, in0, s1, s2, op0, op1, R, W):
        if s2 is None:
            self.P.add(eng, lambda e: e.tensor_scalar(out, in0, s1, None, op0), bl(R), bl(W))
        else:
            self.P.add(eng, lambda e: e.tensor_scalar(out, in0, s1, s2, op0, op1), bl(R), bl(W))

    def stt(self, out, in0, scalar, in1, op0, op1, R, W):
        self.P.add('vector', lambda e: e.scalar_tensor_tensor(out, in0, scalar, in1, op0, op1), bl(R), bl(W))

    def cp(self, eng, out, in_, R, W):
        if eng == 'scalar':
            self.P.add('scalar', lambda e: e.activation(out, in_, AF.Copy), bl(R), bl(W))
        else:
            self.P.add(eng, lambda e: e.tensor_copy(out, in_), bl(R), bl(W))

    def cpa(self, out, in_, R, W):
        self.alt ^= 1
        self.cp('scalar' if self.alt else 'vector', out, in_, R, W)

    def recip(self, out, in_, R, W):
        self.P.add('vector', lambda e: e.reciprocal(out, in_), bl(R), bl(W))

    def scan(self, out, d0, d1, init, op0, op1, R, W):
        self.P.add('vector', lambda e: e.tensor_tensor_scan(out, d0, d1, init, op0, op1), bl(R), bl(W))

    def memset(self, eng, out, val, W):
        self.P.add(eng, lambda e: e.memset(out, val), [], bl(W))

    def dma(self, q, out, in_, R, W):
        self.P.add(q, lambda e: e.dma_start(out=out, in_=in_), bl(R), bl(W), dma=True)


PP = {}
_o = 0
for _n, _w in (('cw0', 64), ('cb0', 16), ('ba', 16), ('bx', 16), ('lam', 16), ('kvn', 4),
               ('cw1', 96), ('cb1', 24), ('dtb', 1), ('ssdn', 16), ('hgn', 16), ('lbs', 32)):
    PP[_n] = (_o, _w)
    _o += _w
NPP = _o
NCST = 640


def pack_params(inp):
    pp = np.zeros((128, NPP), np.float32)

    def put(name, arr):
        o, w = PP[name]
        pp[:arr.shape[0], o:o + w] = arr.reshape(arr.shape[0], -1)
    put('cw0', inp['ab_conv_w'][0].reshape(4, 16, 128).transpose(2, 1, 0))
    put('cb0', inp['ab_conv_b'][0].reshape(16, 128).T)
    put('ba', inp['ab_rg_ba'][0].reshape(16, 128).T)
    put('bx', inp['ab_rg_bx'][0].reshape(16, 128).T)
    put('lam', inp['ab_rg_lambda'][0].reshape(16, 128).T)
    put('kvn', inp['ab_kv_norm'][0].reshape(4, 128).T)
    put('cw1', inp['cd_conv_w'][0].reshape(4, 24, 128).transpose(2, 1, 0))
    put('cb1', inp['cd_conv_b'][0].reshape(24, 128).T)
    put('dtb', inp['cd_dt_bias'][0].reshape(32, 1))
    put('ssdn', inp['cd_ssd_norm'][0].reshape(16, 128).T)
    put('hgn', inp['cd_hgrn_norm'][0].reshape(16, 128).T)
    put('lbs', inp['hgrn_lower_bounds'].reshape(2, 16, 128).transpose(2, 0, 1))
    return pp


def make_consts():
    c = np.zeros((128, NCST), np.float32)
    p = np.arange(128)[:, None]
    j = np.arange(128)[None, :]
    c[:, 0:128] = (p == j)
    c[:, 128:256] = (p <= j)
    c[:, 256:384] = (p > j)
    c[:, 384:512] = np.where(j <= p, 0.0, NEG)
    c[:, 512:640] = 1.0
    rst = np.ones((128, T), np.float32)
    rst[:, ::128] = 0.0
    return c, rst


def build(debug=False, stop_after=None):
    nc = bass.Bass("TRN2", target_bir_lowering=False)
    P = Prog(nc)
    k = K(P)
    sb = SBAlloc(nc)

    def din(name, shape, dt=F32):
        return Tn(nc.dram_tensor(name, list(shape), dt, kind="ExternalInput").ap(), name)

    def dscr(name, shape, dt, out=False):
        kind = "ExternalOutput" if (out or debug) else "Internal"
        return Tn(nc.dram_tensor(name, list(shape), dt, kind=kind).ap(), name)

    x = din("x", [T, D])
    norm_w = din("norm_w", [2, D])
    final_norm = din("final_norm", [D])
    ab_in_w = din("ab_in_w", [D, AB_COLS])
    wa = din("ab_rg_wa", [16, 128, 128])
    wx = din("ab_rg_wx", [16, 128, 128])
    w_uk = din("ab_w_uk", [16, 512, 128])
    w_uv = din("ab_w_uv", [16, 512, 128])
    ab_out_w = din("ab_out_w", [D, D])
    cd_in_w = din("cd_in_w", [D, CD_COLS])
    a_log = din("cd_a_log", [32])
    d_skip = din("cd_d_skip", [32])
    cd_out_w = din("cd_out_w", [D, D])
    pp_d = din("pp", [128, NPP])
    cst_d = din("cst", [128, NCST])
    rst_d = din("rst", [128, T])
    out_d = dscr("out", [T, D], F32, out=True)

    yT = dscr("s_yT", [D, T], BF16)
    h1 = dscr("s_h1", [T, D], F32)
    h2 = dscr("s_h2", [T, D], F32)
    ckv_raw = dscr("s_ckv", [512, T], F32)
    qlat_s = dscr("s_qlat", [16, 4, 128, T], BF16)
    qiT_s = dscr("s_qi", [2048, T], BF16)
    sgb_s = dscr("s_sgb", [2048, T], BF16)
    sz_s = dscr("s_sz", [2048, T], BF16)
    xc_s = dscr("s_xc", [2048, T], BF16)
    BT_s = dscr("s_B", [512, T], BF16)
    CT_s = dscr("s_C", [512, T], BF16)
    qt_s = dscr("s_qt", [2048, T], BF16)
    kt_s = dscr("s_kt", [2048, T], BF16)
    kd_s = dscr("s_kd", [2048, T], BF16)
    sgd_s = dscr("s_sgd", [2048, T], BF16)
    v_s = dscr("s_v", [T, 2048], BF16)

    ps = nc.alloc_psum_tensor("ps", [128, 4096], F32)
    pb = [Buf(f"bank{i}") for i in range(8)]

    def bank(i):
        return ps[:, i * 512:(i + 1) * 512]

    def bank_bf(i):
        return ps[:, i * 512:(i + 1) * 512].bitcast(BF16)

    class Grp:
        def __init__(self, g):
            self.ap = ps[:, g * 2048:(g + 1) * 2048]
            self.bufs = pb[g * 4:(g + 1) * 4]
    G = [Grp(0), Grp(1)]

    cst = sb.alloc([128, NCST], F32, 'cst')
    pp = sb.alloc([128, NPP], F32, 'pp')
    cbf = sb.alloc([128, 384], BF16, 'cbf')
    k.dma('sync', cst[:, :], cst_d[:, :], [], [cst])
    k.dma('sync', pp[:, :], pp_d[:, :], [], [pp])
    k.cp('vector', cbf[:, 0:256], cst[:, 0:256], [cst], [cbf])
    k.cp('vector', cbf[:, 256:384], cst[:, 512:640], [cst], [cbf])
    ident_f = cst[:, 0:128]
    le_f = cst[:, 128:256]
    gt_f = cst[:, 256:384]
    cbias_f = cst[:, 384:512]
    ones_f = cst[:, 512:640]
    ident_b = cbf[:, 0:128]
    le_b = cbf[:, 128:256]
    ones_b = cbf[:, 256:384]

    def ppc(name, j=0, n=1):
        o, w = PP[name]
        return pp[:, o + j:o + j + n]

    kiT2 = sb.alloc([128, T], BF16, 'kiT2')
    wi_tok = sb.alloc([128, 16, 32], F32, 'wi_tok')
    dt_tok = sb.alloc([128, 16, 32], F32, 'dt_tok')
    dtA_tok = sb.alloc([128, 16, 32], F32, 'dtA_tok')
    ebl = sb.alloc([128, 16, 16], F32, 'ebl')
    base_mark = sb.mark()

    def phase_norm_T(src, nw_ap, hnT):
        m = sb.mark()
        nwb = sb.alloc([128, D], F32, 'nwb')
        xt_2 = [sb.alloc([128, D], F32, f'xt{i}') for i in range(2)]
        xn = sb.alloc([128, D], BF16, 'xn')
        junk = xn
        st = sb.alloc([128, 4], F32, 'st')
        k.dma('sync', nwb[:, :], nw_ap.partition_broadcast(128), [], [nwb])
        for tt in range(16):
            xt = xt_2[tt % 2]
            k.dma('sync', xt[:, :], src[tt * 128:(tt + 1) * 128, :], [src], [xt])
            k.act(junk[:, :], xt[:, :], AF.Square, [xt], [junk, st], accum=st[:, 0:1])
            k.ts('vector', st[:, 1:2], st[:, 0:1], 1.0 / D, 1e-6, ALU.mult, ALU.add, [st], [st])
            k.act(st[:, 2:3], st[:, 1:2], AF.Sqrt, [st], [st])
            k.recip(st[:, 3:4], st[:, 2:3], [st], [st])
            k.stt(xn[:, :], xt[:, :], st[:, 3:4], nwb[:, :], ALU.mult, ALU.mult, [xt, st, nwb], [xn])
            for g in range(4):
                bi = (tt * 4 + g) % 8
                bb = bank_bf(bi)
                for j in range(8):
                    c = g * 8 + j
                    k.tr(bb[:, j * 128:(j + 1) * 128], xn[:, c * 128:(c + 1) * 128], ident_b,
                         [xn, cbf], [pb[bi]])
                k.cpa(hnT[:, g * 8:(g + 1) * 8, tt * 128:(tt + 1) * 128],
                      bb.rearrange("p (a b) -> p a b", a=8), [pb[bi]], [hnT])
        P.barrier()
        sb.reset(m)

    def linear_fm(hnT, W, blocks, wslots, evac, wtag):
        for bi, segs in enumerate(blocks):
            ws = wslots[bi % len(wslots)]
            ncols = 0
            for (d0, s0, n) in segs:
                k.dma('gpsimd', ws[:, :, d0:d0 + n],
                      W[:, s0:s0 + n].rearrange("(kc p) c -> p kc c", p=128), [], [ws])
                ncols = max(ncols, d0 + n)
            grp = G[bi % 2]
            for kc in range(32):
                for tc in range(4):
                    k.mm(grp.ap[0:ncols, tc * 512:(tc + 1) * 512], ws[:, kc, 0:ncols],
                         hnT[:, kc, tc * 512:(tc + 1) * 512], kc == 0, kc == 31, [ws, hnT], [grp.bufs])
            evac(bi, grp, ncols)

    def conv_block(grp, a1, a2, cwname, cbname, blk):
        k.cp('scalar', a1[:, 3:3 + T], grp.ap, [grp.bufs], [a1])
        k.act(a2[:, :], a1[:, 3:3 + T], AF.Identity, [a1, pp], [a2],
              bias=ppc(cbname, blk), scale=ppc(cwname, blk * 4 + 3))
        for kk in range(3):
            k.stt(a2[:, :], a1[:, kk:kk + T], ppc(cwname, blk * 4 + kk), a2[:, :], ALU.mult, ALU.add,
                  [a1, pp, a2], [a2])

    hnT = sb.alloc([128, 32, T], BF16, 'hnT')
    phase_norm_T(x, norm_w[0, :], hnT)
    l0_mark = sb.mark()
    wslots = [sb.alloc([128, 32, 128], BF16, f'w{i}') for i in range(2)]

    m = sb.mark()
    a1 = sb.alloc([128, T + 4], F32, 'a1')
    a2 = sb.alloc([128, T], F32, 'a2')
    a3 = sb.alloc([128, T], F32, 'a3')
    xcb = sb.alloc([128, T], BF16, 'xcb')
    sga = sb.alloc([128, T], BF16, 'sga')
    yab = sb.alloc([128, T], BF16, 'yab')
    gw2 = [sb.alloc([128, 2, 128], BF16, f'gw{i}') for i in range(2)]
    s1 = sb.alloc([128, 48], F32, 's1')
    k.memset('vector', a1[:, 0:3], 0.0, [a1])
    k.act(s1[:, 0:16], ppc('lam', 0, 16), AF.Exp, [pp], [s1], scale=-1.0)
    k.act(s1[:, 0:16], s1[:, 0:16], AF.Ln, [s1], [s1], bias=1.0)
    k.ts('vector', s1[:, 16:32], s1[:, 0:16], -8.0, None, ALU.mult, None, [s1], [s1])
    k.ts('vector', s1[:, 32:48], s1[:, 0:16], -16.0, None, ALU.mult, None, [s1], [s1])

    evq = [sb.alloc([128, T], BF16, f'evq{i}') for i in range(2)]

    def rg_gates(g):
        gw = gw2[g % 2]
        g0 = G[0]
        for tc in range(4):
            k.mm(g0.ap[:, tc * 512:(tc + 1) * 512], gw[:, 0, :], xcb[:, tc * 512:(tc + 1) * 512],
                 True, True, [gw, xcb], [g0.bufs])
        k.act(a1[:, 0:T], g0.ap, AF.Sigmoid, [g0.bufs, pp], [a1], bias=ppc('ba', g))
        g1 = G[1]
        for tc in range(4):
            k.mm(g1.ap[:, tc * 512:(tc + 1) * 512], gw[:, 1, :], xcb[:, tc * 512:(tc + 1) * 512],
                 True, True, [gw, xcb], [g1.bufs])
        k.act(a3[:, :], a1[:, 0:T], AF.Exp, [a1, s1], [a3], scale=s1[:, 32 + g:33 + g])
        k.act(a1[:, 0:T], a1[:, 0:T], AF.Exp, [a1, s1], [a1], scale=s1[:, 16 + g:17 + g])
        k.act(a3[:, :], a3[:, :], AF.Sqrt, [a3], [a3], bias=1.0, scale=-1.0)
        k.tt('vector', a3[:, :], a3[:, :], a2[:, :], ALU.mult, [a3, a2], [a3])
        k.act(a2[:, :], g1.ap, AF.Sigmoid, [g1.bufs, pp], [a2], bias=ppc('bx', g))
        k.tt('vector', a3[:, :], a3[:, :], a2[:, :], ALU.mult, [a3, a2], [a3])
        k.scan(a2[:, :], a1[:, 0:T], a3[:, :], 0.0, ALU.mult, ALU.add, [a1, a3], [a2])
        k.tt('vector', yab[:, :], a2[:, :], sga[:, :], ALU.mult, [a2, sga], [yab])
        k.dma('sync', yT[g * 128:(g + 1) * 128, :], yab[:, :], [yab], [yT])
        k.memset('vector', a1[:, 0:3], 0.0, [a1])

    def evac_rg(bi, grp, ncols):
        g = bi // 4
        kind = bi % 4
        if kind == 0:
            conv_block(grp, a1, a2, 'cw0', 'cb0', g)
            k.cp('scalar', xcb[:, :], a2[:, :], [a2], [xcb])
            gw = gw2[g % 2]
            k.dma('gpsimd', gw[:, 0, :], wa[g, :, :], [], [gw])
            k.dma('gpsimd', gw[:, 1, :], wx[g, :, :], [], [gw])
        elif kind == 1:
            k.act(sga[:, :], grp.ap, AF.Silu, [grp.bufs], [sga])
        elif kind == 2:
            ob = evq[0]
            k.act(ob[:, :], grp.ap, AF.Silu, [grp.bufs], [ob])
            k.dma('sync', sgb_s[g * 128:(g + 1) * 128, :], ob[:, :], [ob], [sgb_s])
        else:
            ob = evq[1]
            k.cp('vector', ob[:, :], grp.ap, [grp.bufs], [ob])
            k.dma('sync', qiT_s[g * 128:(g + 1) * 128, :], ob[:, :], [ob], [qiT_s])
            rg_gates(g)

    blocks = []
    for g in range(16):
        blocks.append([(0, g * 128, 128)])
        blocks.append([(0, 2048 + g * 128, 128)])
        blocks.append([(0, 8800 + g * 128, 128)])
        blocks.append([(0, 6656 + g * 128, 128)])
    linear_fm(hnT, ab_in_w, blocks, wslots, evac_rg, 'rg')
    P.barrier()
    sb.reset(m)

    sb.reset(l0_mark)
    wslots = [sb.alloc([128, 32, 128], BF16, f'w{i}') for i in range(2)]
    m = sb.mark()
    ev = [sb.alloc([128, T], F32, f'ev{i}') for i in range(1)]
    evb = [sb.alloc([128, T], BF16, f'evb{i}') for i in range(3)]
    wukT = sb.alloc([128, 16, 512], BF16, 'wukT')
    wuk_ld2 = [sb.alloc([128, 4, 128], BF16, f'wuk_ld{i}') for i in range(2)]
    cnt = [0]

    for h in range(16):
        wuk_ld = wuk_ld2[h % 2]
        k.dma('gpsimd', wuk_ld[:, :, :], w_uk[h, :, :].rearrange("(cc p) d -> p cc d", p=128), [], [wuk_ld])
        bi = h % 8
        bb = bank_bf(bi)
        for cc in range(4):
            k.tr(bb[:, cc * 128:(cc + 1) * 128], wuk_ld[:, cc, :], ident_b, [wuk_ld, cbf], [pb[bi]])
        k.cpa(wukT[:, h, :], bb[:, 0:512], [pb[bi]], [wukT])

    def evac_ckv(bi, grp, ncols):
        t_ = ev[0]
        k.cpa(t_[:, :], grp.ap, [grp.bufs], [t_])
        k.dma('sync', ckv_raw[bi * 128:(bi + 1) * 128, :], t_[:, :], [t_], [ckv_raw])
    linear_fm(hnT, ab_in_w, [[(0, 6144 + i * 128, 128)] for i in range(4)], wslots, evac_ckv, 'ckv')

    def evac_q(h, grp, ncols):
        qb = evb[2]
        k.cp('scalar', qb[:, :], grp.ap, [grp.bufs], [qb])
        for cc in range(4):
            g2 = G[(h + 1 + cc) % 2]
            for tc in range(4):
                k.mm(g2.ap[:, tc * 512:(tc + 1) * 512], wukT[:, h, cc * 128:(cc + 1) * 128],
                     qb[:, tc * 512:(tc + 1) * 512], True, True, [wukT, qb], [g2.bufs])
            ob = evb[cc % 2]
            if cc % 2 == 0:
                k.act(ob[:, :], g2.ap, AF.Copy, [g2.bufs], [ob], scale=float(128 ** -0.5))
            else:
                k.ts('vector', ob[:, :], g2.ap, float(128 ** -0.5), None, ALU.mult, None, [g2.bufs], [ob])
            k.dma('sync', qlat_s[h, cc, :, :], ob[:, :], [ob], [qlat_s])
    linear_fm(hnT, ab_in_w, [[(0, 4096 + h * 128, 128)] for h in range(16)], wslots, evac_q, 'q')

    def evac_ki(bi, grp, ncols):
        k.cp('scalar', kiT2[:, :], grp.ap, [grp.bufs], [kiT2])
    linear_fm(hnT, ab_in_w, [[(0, 8704, 64), (64, 8704, 64)]], wslots, evac_ki, 'ki')

    def evac_wi(bi, grp, ncols):
        t_ = ev[0]
        k.cp('scalar', t_[0:32, :], grp.ap[0:32, :], [grp.bufs], [t_])
        for tt in range(16):
            bi2 = 4 + tt % 4
            k.tr(bank(bi2)[:, 0:32], t_[0:32, tt * 128:(tt + 1) * 128], ident_f[0:32, 0:32], [t_, cst], [pb[bi2]])
            k.cp('vector', wi_tok[:, tt, :], bank(bi2)[:, 0:32], [pb[bi2]], [wi_tok])
    linear_fm(hnT, ab_in_w, [[(0, 8768, 32)]], wslots, evac_wi, 'wi')

    P.barrier()
    sb.reset(m)

    sb.reset(base_mark)
    ckvT = sb.alloc([128, 4, T], BF16, 'ckvT')
    ckv_tok = sb.alloc([128, 16, 512], BF16, 'ckv_tok')
    wuv = sb.alloc([128, 16, 4, 128], BF16, 'wuv')
    thr_c = sb.alloc([128, 1], F32, 'thr_c')
    k.memset('vector', thr_c[:, :], -1.0e29, [thr_c])
    for h in range(16):
        k.dma('gpsimd', wuv[:, h, :, :], w_uv[h, :, :].rearrange("(cc p) d -> p cc d", p=128), [], [wuv])
    m2 = sb.mark()
    craw = sb.alloc([128, 4, T], F32, 'craw')
    csq = sb.alloc([128, 4, T], F32, 'csq')
    rstd = sb.alloc([128, T], F32, 'rstd')
    k.dma('sync', craw[:, :, :], ckv_raw[:, :].rearrange("(cc p) t -> p cc t", p=128), [ckv_raw], [craw])
    k.act(csq[:, :, :], craw[:, :, :], AF.Square, [craw], [csq])
    for tc in range(4):
        for cc in range(4):
            k.mm(G[0].ap[:, tc * 512:(tc + 1) * 512], ones_f, csq[:, cc, tc * 512:(tc + 1) * 512],
                 cc == 0, cc == 3, [cst, csq], [G[0].bufs])
    k.ts('vector', rstd[:, :], G[0].ap, 1.0 / 512, 1e-6, ALU.mult, ALU.add, [G[0].bufs], [rstd])
    k.act(rstd[:, :], rstd[:, :], AF.Sqrt, [rstd], [rstd])
    k.recip(rstd[:, :], rstd[:, :], [rstd], [rstd])
    for cc in range(4):
        k.stt(ckvT[:, cc, :], craw[:, cc, :], ppc('kvn', cc), rstd[:, :], ALU.mult, ALU.mult,
              [craw, pp, rstd], [ckvT])
    for tt in range(16):
        bi = tt % 8
        bb = bank_bf(bi)
        for cc in range(4):
            k.tr(bb[:, cc * 128:(cc + 1) * 128], ckvT[:, cc, tt * 128:(tt + 1) * 128], ident_b,
                 [ckvT, cbf], [pb[bi]])
        k.cpa(ckv_tok[:, tt, :], bb[:, 0:512], [pb[bi]], [ckv_tok])
    P.barrier()
    sb.reset(m2)

    acc2 = [sb.alloc([128, T], F32, f'acc{i}') for i in range(2)]
    wk = [sb.alloc([128, T], F32, f'wk{i}') for i in range(2)]
    rt = [sb.alloc([128, T], F32, f'rt{i}') for i in range(4)]
    sc_cnt = [0]
    m8 = sb.alloc([128, 8], F32, 'm8')
    m01 = sb.alloc([128, T], BF16, 'm01')
    maskT2 = [sb.alloc([128, 16, 128], BF16, f'maskT{i}') for i in range(2)]
    qi_t = [sb.alloc([128, 16, 128], BF16, f'qi_t{i}') for i in range(2)]
    ql_t = [sb.alloc([128, 16, 4, 128], BF16, f'ql_t{i}') for i in range(2)]
    sgb_t = [sb.alloc([128, 16, 128], BF16, f'sgb_t{i}') for i in range(2)]
    ptb = [sb.alloc([128, 512], BF16, f'ptb{i}') for i in range(2)]
    rden = sb.alloc([128, 512], F32, 'rden')
    olb = sb.alloc([128, 4, 512], BF16, 'olb')
    yt1 = sb.alloc([128, 512], F32, 'yt1')
    yob = [sb.alloc([128, 4, 128], BF16, f'yob{i}') for i in range(2)]

    def steps_SC(i):
        S = (i + 1) * 128
        qit = qi_t[i % 2]
        acc = acc2[i % 2]
        nsc = (S + 511) // 512
        st = []

        def s_load():
            k.dma('sync', qit[:, :, :], qiT_s[:, i * 128:(i + 1) * 128].rearrange("(b p) t -> p b t", p=128),
                  [qiT_s], [qit])
        st.append(s_load)

        def mk_head(hh):
            def f():
                blk, half = hh // 2, hh % 2
                r_ = rt[hh % 4]
                for sc in range(nsc):
                    w_ = min(512, S - sc * 512)
                    sbk = 7 if (sc_cnt[0] % 2) else 1
                    sc_cnt[0] += 1
                    k.mm(bank(sbk)[:, 0:w_], qit[half * 64:(half + 1) * 64, blk, :],
                         kiT2[half * 64:(half + 1) * 64, sc * 512:sc * 512 + w_], True, True, [qit, kiT2], [pb[sbk]])
                    k.act(r_[:, sc * 512:sc * 512 + w_], bank(sbk)[:, 0:w_], AF.Relu, [pb[sbk]], [r_])
                if hh == 0:
                    k.ts('vector', acc[:, 0:S], r_[:, 0:S], wi_tok[:, i, 0:1], None, ALU.mult, None,
                         [r_, wi_tok], [acc])
                else:
                    k.stt(acc[:, 0:S], r_[:, 0:S], wi_tok[:, i, hh:hh + 1], acc[:, 0:S], ALU.mult, ALU.add,
                          [r_, wi_tok, acc], [acc])
            return f
        for hh in range(32):
            st.append(mk_head(hh))

        def s_bias():
            k.tt('vector', acc[:, i * 128:S], acc[:, i * 128:S], cbias_f, ALU.add, [acc, cst], [acc])
        st.append(s_bias)
        return st

    def steps_TK(i):
        S = (i + 1) * 128
        acc = acc2[i % 2]
        maskT = maskT2[i % 2]
        qlt = ql_t[i % 2]
        sgt = sgb_t[i % 2]
        st = []

        def s_load():
            k.dma('sync', qlt[:, :, :, :], qlat_s[:, :, :, i * 128:(i + 1) * 128].rearrange("h c p t -> p h c t"),
                  [qlat_s], [qlt])
            k.dma('sync', sgt[:, :, :], sgb_s[:, i * 128:(i + 1) * 128].rearrange("(b p) t -> p b t", p=128),
                  [sgb_s], [sgt])
        st.append(s_load)
        if i >= 2:
            def mk_round(r):
                def f():
                    cur = acc if r == 0 else wk[(r - 1) % 2]
                    k.P.add('vector', (lambda c, S_: (lambda e: e.max(m8[:, :], c[:, 0:S_])))(cur, S), bl([cur]), bl([m8]))
                    if r < 31:
                        nxt = wk[r % 2]
                        k.P.add('vector', (lambda c, n, S_: (lambda e: e.match_replace(n[:, 0:S_], m8[:, :], c[:, 0:S_], -3.0e38)))(cur, nxt, S),
                                bl([cur, m8]), bl([nxt]))
                return f
            for r in range(32):
                st.append(mk_round(r))

        def s_mask():
            if i >= 2:
                thr, thr_b = m8[:, 7:8], [m8]
            else:
                thr, thr_b = thr_c[:, 0:1], [thr_c]
            k.ts('vector', m01[:, 0:S], acc[:, 0:S], thr, None, ALU.is_lt, None, [acc] + thr_b, [m01])
            k.ts('vector', m01[:, 0:S], m01[:, 0:S], -30000.0, None, ALU.mult, None, [m01], [m01])
            for j0 in range(0, i + 1, 8):
                bb = bank_bf(7)
                nj = min(8, i + 1 - j0)
                for j in range(j0, j0 + nj):
                    k.tr(bb[:, (j - j0) * 128:(j - j0 + 1) * 128], m01[:, j * 128:(j + 1) * 128], ident_b,
                         [m01, cbf], [pb[7]])
                k.cp('scalar', maskT[:, j0:j0 + nj, :], bb[:, 0:nj * 128].rearrange("p (a b) -> p a b", a=nj),
                     [pb[7]], [maskT])
        st.append(s_mask)
        return st

    def steps_AT(i):
        maskT = maskT2[i % 2]
        qlt = ql_t[i % 2]
        sgt = sgb_t[i % 2]
        st = []

        def L(hg, j):
            lb_ = 0
            for cc in range(4):
                k.mm(bank(lb_).rearrange("p (a b) -> p a b", a=4), ckvT[:, cc, j * 128:(j + 1) * 128],
                     qlt[:, hg * 4:(hg + 1) * 4, cc, :], cc == 0, False, [ckvT, qlt], [pb[lb_]])
            k.mm(bank(lb_).rearrange("p (a b) -> p a b", a=4), ident_b,
                 maskT[:, j:j + 1, :].broadcast_to([128, 4, 128]), False, True, [cbf, maskT], [pb[lb_]])

        def mk_j(hg, j):
            def f():
                if j == 0:
                    L(hg, 0)
                p_ = ptb[j % 2]
                k.act(p_[:, :], bank(0), AF.Exp, [pb[0]], [p_])
                if j + 1 <= i:
                    L(hg, j + 1)
                for cc in range(4):
                    k.mm(bank(2 + cc), ckv_tok[:, j, cc * 128:(cc + 1) * 128], p_[:, :],
                         j == 0, j == i, [ckv_tok, p_], [pb[2 + cc]])
                k.mm(bank(6), ones_b, p_[:, :], j == 0, j == i, [cbf, p_], [pb[6]])
            return f

        def mk_tail(hg):
            def f():
                k.act(rden[:, :], bank(6), AF.Ln, [pb[6]], [rden])
                k.act(rden[:, :], rden[:, :], AF.Exp, [rden], [rden], scale=-1.0)
                for cc in range(4):
                    k.cp('scalar', olb[:, cc, :], bank(2 + cc), [pb[2 + cc]], [olb])
                for hl in range(4):
                    h = hg * 4 + hl
                    for cc in range(4):
                        k.mm(bank(7)[:, hl * 128:(hl + 1) * 128], wuv[:, h, cc, :], olb[:, cc, hl * 128:(hl + 1) * 128],
                             cc == 0, cc == 3, [wuv, olb], [pb[7]])
                k.cp('scalar', yt1[:, :], bank(7), [pb[7]], [yt1])
                k.tt('gpsimd', yt1[:, :], yt1[:, :], rden[:, :], ALU.mult, [yt1, rden], [yt1])
                yo = yob[hg % 2]
                k.tt('gpsimd', yo[:, :, :], yt1[:, :].rearrange("p (a b) -> p a b", a=4), sgt[:, hg * 4:(hg + 1) * 4, :],
                     ALU.mult, [yt1, sgt], [yo])
                k.dma('sync', yT[2048 + hg * 512:2048 + (hg + 1) * 512, i * 128:(i + 1) * 128].rearrange("(a p) t -> p a t", p=128),
                      yo[:, :, :], [yo], [yT])
            return f
        for hg in range(4):
            for j in range(i + 1):
                st.append(mk_j(hg, j))
            st.append(mk_tail(hg))
        return st

    def merge(lists):
        items = []
        for li, l in enumerate(lists):
            n = len(l)
            for idx, f in enumerate(l):
                items.append(((idx + 0.5) / n, li, idx, f))
        items.sort(key=lambda t: (t[0], t[1], t[2]))
        return [t[3] for t in items]

    n_qt = 16
    for f in steps_SC(0) + steps_TK(0) + steps_SC(1):
        f()
    for s_ in range(n_qt):
        lists = [steps_AT(s_)]
        if s_ + 1 < n_qt:
            lists.append(steps_TK(s_ + 1))
        if s_ + 2 < n_qt:
            lists.append(steps_SC(s_ + 2))
        for f in merge(lists):
            f()
    P.barrier()
    sb.reset(base_mark)

    def out_proj(Wout, res_src, dst):
        m = sb.mark()
        yres = sb.alloc([128, 32, 1024], BF16, 'yres')
        wo = [sb.alloc([128, 32, 512], BF16, f'wo{i}') for i in range(2)]
        xr = [sb.alloc([128, 512], F32, f'xr{i}') for i in range(4)]
        n = 0
        for th in range(2):
            for q4 in range(4):
                k.dma('sync', yres[:, q4 * 8:(q4 + 1) * 8, :],
                      yT[q4 * 1024:(q4 + 1) * 1024, th * 1024:(th + 1) * 1024].rearrange("(fc p) t -> p fc t", p=128),
                      [yT], [yres])
            for cb in range(8):
                w_ = wo[n % 2]
                n += 1
                for hf in range(2):
                    k.dma('gpsimd', w_[:, hf * 16:(hf + 1) * 16, :],
                          Wout[hf * 2048:(hf + 1) * 2048, cb * 512:(cb + 1) * 512].rearrange("(fc p) c -> p fc c", p=128),
                          [], [w_])
                for t8 in range(8):
                    tt = th * 8 + t8
                    bi = (cb * 8 + t8) % 8
                    xr_ = xr[(cb * 8 + t8) % 4]
                    k.dma('sync', xr_[:, :], res_src[tt * 128:(tt + 1) * 128, cb * 512:(cb + 1) * 512], [res_src], [xr_])
                    for fc in range(32):
                        k.mm(bank(bi), yres[:, fc, t8 * 128:(t8 + 1) * 128], w_[:, fc, :],
                             fc == 0, fc == 31, [yres, w_], [pb[bi]])
                    k.tt('vector', xr_[:, :], bank(bi), xr_[:, :], ALU.add, [pb[bi], xr_], [xr_])
                    k.dma('sync', dst[tt * 128:(tt + 1) * 128, cb * 512:(cb + 1) * 512], xr_[:, :], [xr_], [dst])
        P.barrier()
        sb.reset(m)

    out_proj(ab_out_w, x, h1)

    if stop_after == 'l0':
        P.emit()
        return nc

    sb.reset(base_mark)
    hnT = sb.alloc([128, 32, T], BF16, 'hnT1')
    phase_norm_T(h1, norm_w[1, :], hnT)
    l1_mark = sb.mark()
    wslots = [sb.alloc([128, 32, 128], BF16, f'w{i}') for i in range(2)]
    m = sb.mark()
    evb = [sb.alloc([128, T], BF16, f'evb{i}') for i in range(3)]
    a1 = sb.alloc([128, T + 4], F32, 'a1')
    a2 = sb.alloc([128, T], F32, 'a2')
    a3 = sb.alloc([128, T], F32, 'a3')
    rstb = sb.alloc([128, T], BF16, 'rstb')
    sm = sb.alloc([128, 160], F32, 'sm')
    k.dma('gpsimd', rstb[:, :], rst_d[:, :], [], [rstb])
    k.memset('vector', a1[:, 0:3], 0.0, [a1])
    lb_ = sm[:, 0:16]
    oml = sm[:, 16:32]
    a_b = sm[:, 32:64]
    o_l, _ = PP['lbs']
    k.tt('vector', lb_, pp[:, o_l + 16:o_l + 32], pp[:, o_l:o_l + 16], ALU.subtract, [pp], [sm])
    k.act(lb_, lb_, AF.Sigmoid, [sm], [sm])
    k.ts('vector', oml, lb_, -1.0, 1.0, ALU.mult, ALU.add, [sm], [sm])
    k.dma('sync', a_b, a_log[:].partition_broadcast(128), [], [sm])
    k.act(a_b, a_b, AF.Exp, [sm], [sm])
    k.ts('vector', a_b, a_b, -1.0, None, ALU.mult, None, [sm], [sm])

    def evac_z(bi, grp, ncols):
        ob = evb[bi % 3]
        k.act(ob[:, :], grp.ap, AF.Silu, [grp.bufs], [ob])
        k.dma('sync', sz_s[bi * 128:(bi + 1) * 128, :], ob[:, :], [ob], [sz_s])
    linear_fm(hnT, cd_in_w, [[(0, i * 128, 128)] for i in range(16)], wslots, evac_z, 'z')

    def evac_xbc(bi, grp, ncols):
        conv_block(grp, a1, a2, 'cw1', 'cb1', bi)
        ob = evb[bi % 3]
        k.act(ob[:, :], a2[:, :], AF.Silu, [a2], [ob])
        if bi < 16:
            k.dma('sync', xc_s[bi * 128:(bi + 1) * 128, :], ob[:, :], [ob], [xc_s])
        elif bi < 20:
            k.dma('sync', BT_s[(bi - 16) * 128:(bi - 15) * 128, :], ob[:, :], [ob], [BT_s])
        else:
            k.dma('sync', CT_s[(bi - 20) * 128:(bi - 19) * 128, :], ob[:, :], [ob], [CT_s])
    linear_fm(hnT, cd_in_w, [[(0, 2048 + i * 128, 128)] for i in range(24)], wslots, evac_xbc, 'xbc')

    def evac_dt(bi, grp, ncols):
        k.act(a3[0:32, :], grp.ap[0:32, :], AF.Exp, [grp.bufs, pp], [a3], bias=pp[0:32, PP['dtb'][0]:PP['dtb'][0] + 1])
        k.act(a3[0:32, :], a3[0:32, :], AF.Ln, [a3], [a3], bias=1.0)
        for tt in range(16):
            bi2 = 4 + tt % 4
            k.tr(bank(bi2)[:, 0:32], a3[0:32, tt * 128:(tt + 1) * 128], ident_f[0:32, 0:32], [a3, cst], [pb[bi2]])
            k.cp('vector', dt_tok[:, tt, :], bank(bi2)[:, 0:32], [pb[bi2]], [dt_tok])
        k.tt('vector', dtA_tok[:, :, :], dt_tok[:, :, :], a_b.unsqueeze(1).broadcast_to([128, 16, 32]), ALU.mult,
             [dt_tok, sm], [dtA_tok])
    linear_fm(hnT, cd_in_w, [[(0, 5120, 32)]], wslots, evac_dt, 'dt')

    def evac_fq(bi, grp, ncols):
        h = bi // 2
        if bi % 2 == 0:
            k.act(a1[:, 0:T], grp.ap, AF.Sigmoid, [grp.bufs], [a1])
            k.ts('vector', a1[:, 0:T], a1[:, 0:T], oml[:, h:h + 1], lb_[:, h:h + 1], ALU.mult, ALU.add, [a1, sm], [a1])
            k.act(a2[:, :], a1[:, 0:T], AF.Ln, [a1], [a2])
            k.scan(a3[:, :], rstb[:, :], a2[:, :], 0.0, ALU.mult, ALU.add, [rstb, a2], [a3])
            k.ts('vector', a1[:, 0:T], a1[:, 0:T], -1.0, 1.0, ALU.mult, ALU.add, [a1], [a1])
            k.act(a2[:, :], a3[:, :], AF.Exp, [a3], [a2], scale=-1.0)
            ob = evb[0]
            k.tt('vector', ob[:, :], a1[:, 0:T], a2[:, :], ALU.mult, [a1, a2], [ob])
            k.dma('sync', kt_s[h * 128:(h + 1) * 128, :], ob[:, :], [ob], [kt_s])
            a3v = a3[:, :].rearrange("p (c l) -> p c l", c=16)
            a2v = a2[:, :].rearrange("p (c l) -> p c l", c=16)
            k.tt('vector', a2v, a3v[:, :, 127:128].broadcast_to([128, 16, 128]), a3v, ALU.subtract, [a3], [a2])
            k.act(a2[:, :], a2[:, :], AF.Exp, [a2], [a2])
            ob = evb[1]
            k.tt('vector', ob[:, :], a1[:, 0:T], a2[:, :], ALU.mult, [a1, a2], [ob])
            k.dma('sync', kd_s[h * 128:(h + 1) * 128, :], ob[:, :], [ob], [kd_s])
            k.act(ebl[:, h, :], a3v[:, :, 127], AF.Exp, [a3], [ebl])
        else:
            k.act(a1[:, 0:T], grp.ap, AF.Silu, [grp.bufs], [a1])
            k.act(a2[:, :], a3[:, :], AF.Exp, [a3], [a2])
            ob = evb[2]
            k.tt('vector', ob[:, :], a1[:, 0:T], a2[:, :], ALU.mult, [a1, a2], [ob])
            k.dma('sync', qt_s[h * 128:(h + 1) * 128, :], ob[:, :], [ob], [qt_s])
    blocks = []
    for h in range(16):
        blocks.append([(0, 7200 + h * 128, 128)])
        blocks.append([(0, 5152 + h * 128, 128)])
    linear_fm(hnT, cd_in_w, blocks, wslots, evac_fq, 'fq')

    def evac_gd(bi, grp, ncols):
        ob = evb[bi % 3]
        k.act(ob[:, :], grp.ap, AF.Silu, [grp.bufs], [ob])
        k.dma('sync', sgd_s[bi * 128:(bi + 1) * 128, :], ob[:, :], [ob], [sgd_s])
    linear_fm(hnT, cd_in_w, [[(0, 11296 + i * 128, 128)] for i in range(16)], wslots, evac_gd, 'gd')
    P.barrier()
    sb.reset(l1_mark)

    wo = [sb.alloc([128, 32, 256], BF16, f'wv{i}') for i in range(2)]
    vb = [sb.alloc([128, 256], BF16, f'vb{i}') for i in range(4)]
    n = 0
    for sl in range(8):
        w_ = wo[sl % 2]
        k.dma('gpsimd', w_[:, :, :], cd_in_w[:, 9248 + sl * 256:9248 + (sl + 1) * 256].rearrange("(kc p) c -> p kc c", p=128),
              [], [w_])
        for tt in range(16):
            bi = n % 8
            v_ = vb[n % 4]
            n += 1
            for kc in range(32):
                k.mm(bank(bi)[:, 0:256], hnT[:, kc, tt * 128:(tt + 1) * 128], w_[:, kc, :], kc == 0, kc == 31,
                     [hnT, w_], [pb[bi]])
            k.cpa(v_[:, :], bank(bi)[:, 0:256], [pb[bi]], [v_])
            k.dma('sync', v_s[tt * 128:(tt + 1) * 128, sl * 256:(sl + 1) * 256], v_[:, :], [v_], [v_s])
    P.barrier()
    sb.reset(base_mark)

    sprev = sb.alloc([128, 4, 512], F32, 'sprev')
    sprev_b = sb.alloc([128, 4, 512], BF16, 'sprev_b')
    hprev = sb.alloc([128, 16, 128], F32, 'hprev')
    hprev_b = sb.alloc([128, 16, 128], BF16, 'hprev_b')
    Db = sb.alloc([128, 32], F32, 'Db')
    k.memset('vector', sprev[:, :, :], 0.0, [sprev])
    k.memset('vector', sprev_b[:, :, :], 0.0, [sprev_b])
    k.memset('vector', hprev[:, :, :], 0.0, [hprev])
    k.memset('vector', hprev_b[:, :, :], 0.0, [hprev_b])
    k.dma('sync', Db[:, :], d_skip[:].partition_broadcast(128), [], [Db])

    def dbl(shape, dt, name):
        return [sb.alloc(shape, dt, f'{name}{i}') for i in range(2)]
    xcT_c = dbl([128, 16, 128], BF16, 'xcT_c')
    BT_c = dbl([128, 4, 128], BF16, 'BT_c')
    CT_c = dbl([128, 4, 128], BF16, 'CT_c')
    sz_c = dbl([128, 16, 128], BF16, 'sz_c')
    qt_c = dbl([128, 16, 128], BF16, 'qt_c')
    kt_c = dbl([128, 16, 128], BF16, 'kt_c')
    kd_c = dbl([128, 16, 128], BF16, 'kd_c')
    sgd_c = dbl([128, 16, 128], BF16, 'sgd_c')
    v_c = dbl([128, 2048], BF16, 'v_c')
    x_tok = sb.alloc([128, 32, 64], BF16, 'x_tok')
    B_tok = sb.alloc([128, 4, 128], BF16, 'B_tok')
    sml = sb.alloc([128, 256], F32, 'sml')
    xdt = sb.alloc([128, 32, 64], BF16, 'xdt')
    xdtd = sb.alloc([128, 32, 64], BF16, 'xdtd')
    cbm = sb.alloc([128, 4, 128], F32, 'cbm')
    yoff = sb.alloc([128, 32, 64], F32, 'yoff')
    Lm4 = [sb.alloc([128, 4, 128], F32, f'Lm{i}') for i in range(2)]
    seg4 = [sb.alloc([128, 512], F32, f'seg{i}') for i in range(2)]
    MT4 = [sb.alloc([128, 4, 128], BF16, f'MT{i}') for i in range(2)]
    yv = sb.alloc([128, 2048], F32, 'yv')
    gT = sb.alloc([128, 16, 128], F32, 'gT')
    sq = sb.alloc([128, 16, 128], F32, 'sq')
    t2v = sq[:, :, :].rearrange("p a b -> p (a b)")
    rs = sb.alloc([128, 4, 128], F32, 'rs')
    ycb = dbl([128, 16, 128], BF16, 'ycb')
    kd_tok = sb.alloc([128, 16, 128], BF16, 'kd_tok')
    attm4 = [sb.alloc([128, 4, 128], BF16, f'attm{i}') for i in range(2)]
    rsh = sb.alloc([128, 2048], F32, 'rsh')
    ydb = dbl([128, 16, 128], BF16, 'ydb')
    o_hg, _ = PP['hgn']

    osb = sb.alloc([128, 2048], F32, 'osb')
    sqh = sb.alloc([128, 2048], F32, 'sqh')

    def chunk_steps(c):
        cs = slice(c * 128, (c + 1) * 128)
        p2 = c % 2
        xT, Bc, Cc, szc = xcT_c[p2], BT_c[p2], CT_c[p2], sz_c[p2]
        qtc, ktc, kdc, sgc, vc = qt_c[p2], kt_c[p2], kd_c[p2], sgd_c[p2], v_c[p2]
        eacs = sml[:, 64:96]
        dte = sml[:, 96:128]
        cdb = sml[:, 128:160]
        x_tok2 = x_tok[:, :, :].rearrange("p a b -> p (a b)")
        ssd = []
        hg_ = []

        def ld(dst, src):
            k.dma('sync', dst[:, :, :], src[:, cs].rearrange("(b p) t -> p b t", p=128), [src], [dst])

        def s_load_all():
            ld(xT, xc_s)
            ld(Bc, BT_s)
            ld(Cc, CT_s)
            ld(szc, sz_s)
            ld(qtc, qt_s)
            ld(ktc, kt_s)
            ld(kdc, kd_s)
            ld(sgc, sgd_s)
            k.dma('sync', vc[:, :], v_s[cs, :], [v_s], [vc])

        def s_pro1():
            for half in range(2):
                bb = bank_bf(4)
                for j in range(8):
                    k.tr(bb[:, j * 128:(j + 1) * 128], xT[:, half * 8 + j, :], ident_b, [xT, cbf], [pb[4]])
                k.cpa(x_tok2[:, half * 1024:(half + 1) * 1024], bb[:, 0:1024], [pb[4]], [x_tok])
            bb = bank_bf(4)
            for g in range(4):
                k.tr(bb[:, g * 128:(g + 1) * 128], Bc[:, g, :], ident_b, [Bc, cbf], [pb[4]])
            k.cpa(B_tok[:, :, :].rearrange("p a b -> p (a b)"), bb[:, 0:512], [pb[4]], [B_tok])
            k.mm(bank(4)[:, 0:32], le_f, dtA_tok[:, c, :], True, True, [cst, dtA_tok], [pb[4]])
            k.mm(bank(4)[:, 32:64], ones_f, dtA_tok[:, c, :], True, True, [cst, dtA_tok], [pb[4]])
            k.cp('scalar', sml[:, 0:64], bank(4)[:, 0:64], [pb[4]], [sml])
            k.act(sml[:, 64:96], sml[:, 0:32], AF.Exp, [sml], [sml])
            k.tt('vector', sml[:, 96:128], sml[:, 32:64], sml[:, 0:32], ALU.subtract, [sml], [sml])
            k.act(sml[:, 96:128], sml[:, 96:128], AF.Exp, [sml], [sml])
            k.act(sml[:, 128:160], sml[:, 32:64], AF.Exp, [sml], [sml])
            k.tt('vector', xdt[:, :, :], x_tok[:, :, :], dt_tok[:, c, :].unsqueeze(2).broadcast_to([128, 32, 64]), ALU.mult,
                 [x_tok, dt_tok], [xdt])
            k.tt('vector', xdtd[:, :, :], xdt[:, :, :], dte.unsqueeze(2).broadcast_to([128, 32, 64]), ALU.mult,
                 [xdt, sml], [xdtd])
        ssd.append(s_pro1)

        def s_pro2():
            for g in range(4):
                k.mm(bank(4)[:, g * 128:(g + 1) * 128], Bc[:, g, :], Cc[:, g, :], True, True, [Bc, Cc], [pb[4]])
            k.tt('vector', cbm[:, :, :], bank(4).rearrange("p (a b) -> p a b", a=4),
                 le_f.unsqueeze(1).broadcast_to([128, 4, 128]), ALU.mult, [pb[4], cst], [cbm])
            for g in range(4):
                k.mm(bank(4), Cc[:, g, :], sprev_b[:, g, :], True, True, [Cc, sprev_b], [pb[4]])
                k.tt('vector', yoff[:, g * 8:(g + 1) * 8, :], bank(4).rearrange("p (a b) -> p a b", a=8),
                     eacs[:, g * 8:(g + 1) * 8].unsqueeze(2).broadcast_to([128, 8, 64]), ALU.mult, [pb[4], sml], [yoff])
        ssd.append(s_pro2)

        def mk_ssd_batch(bq):
            def f():
                hd0 = bq * 4
                g = bq // 2
                yb_ = g % 2
                L4 = Lm4[bq % 2]
                k.tt('gpsimd' if bq % 2 else 'vector', L4[:, :, :], gt_f.unsqueeze(1).broadcast_to([128, 4, 128]),
                     dtA_tok[:, c, hd0:hd0 + 4].unsqueeze(2).broadcast_to([128, 4, 128]), ALU.mult, [cst, dtA_tok], [L4])
                for q in range(4):
                    k.mm(bank(6)[:, q * 128:(q + 1) * 128], L4[:, q, :], le_f, True, True, [L4, cst], [pb[6]])
                s4 = seg4[bq % 2]
                k.act(s4[:, :], bank(6), AF.Exp, [pb[6]], [s4])
                M4 = MT4[bq % 2]
                k.tt('vector', M4[:, :, :], s4[:, :].rearrange("p (a b) -> p a b", a=4),
                     cbm[:, g:g + 1, :].broadcast_to([128, 4, 128]), ALU.mult, [s4, cbm], [M4])
                for q in range(4):
                    hd = hd0 + q
                    k.mm(bank(yb_)[:, (hd % 8) * 64:(hd % 8 + 1) * 64], M4[:, q, :], xdt[:, hd, :], True, True,
                         [M4, xdt], [pb[yb_]])
                if bq % 2 == 1:
                    k.tt('vector', yv[:, g * 512:(g + 1) * 512], bank(yb_),
                         yoff[:, g * 8:(g + 1) * 8, :].rearrange("p a b -> p (a b)"), ALU.add, [pb[yb_], yoff], [yv])
            return f
        for bq in range(8):
            ssd.append(mk_ssd_batch(bq))

        def s_states():
            for g in range(4):
                k.mm(bank(4), B_tok[:, g, :], xdtd[:, g * 8:(g + 1) * 8, :].rearrange("p a b -> p (a b)"), True, True,
                     [B_tok, xdtd], [pb[4]])
                spv = sprev[:, g, :].rearrange("p (a b) -> p a b", a=8)
                k.tt('vector', spv, spv, cdb[:, g * 8:(g + 1) * 8].unsqueeze(2).broadcast_to([128, 8, 64]), ALU.mult,
                     [sprev, sml], [sprev])
                k.tt('vector', sprev[:, g, :], sprev[:, g, :], bank(4), ALU.add, [sprev, pb[4]], [sprev])
                k.cp('scalar', sprev_b[:, g, :], sprev[:, g, :], [sprev], [sprev_b])
            k.tt('gpsimd', t2v.rearrange("p (a b) -> p a b", a=32), x_tok[:, :, :], Db[:, :].unsqueeze(2).broadcast_to([128, 32, 64]), ALU.mult,
                 [x_tok, Db], [sq])
            k.tt('vector', yv[:, :], yv[:, :], t2v, ALU.add, [yv, sq], [yv])
        ssd.append(s_states)

        def s_epi():
            for q4 in range(4):
                for j in range(4):
                    fc = q4 * 4 + j
                    k.tr(bank(4)[:, j * 128:(j + 1) * 128], yv[:, fc * 128:(fc + 1) * 128], ident_f, [yv, cst], [pb[4]])
                k.tt('vector', gT[:, q4 * 4:(q4 + 1) * 4, :], bank(4).rearrange("p (a b) -> p a b", a=4),
                     szc[:, q4 * 4:(q4 + 1) * 4, :], ALU.mult, [pb[4], szc], [gT])
            k.act(sq[:, :, :], gT[:, :, :], AF.Square, [gT], [sq])
            for g in range(4):
                for j in range(4):
                    k.mm(bank(4)[:, g * 128:(g + 1) * 128], ones_f, sq[:, g * 4 + j, :], j == 0, j == 3, [cst, sq], [pb[4]])
            rs2 = rs[:, :, :].rearrange("p a b -> p (a b)")
            k.ts('vector', rs2, bank(4), 1.0 / 512, 1e-6, ALU.mult, ALU.add, [pb[4]], [rs])
            k.act(rs2, rs2, AF.Sqrt, [rs], [rs])
            k.recip(rs2, rs2, [rs], [rs])
            yc_ = ycb[p2]
            for fc in range(16):
                k.stt(yc_[:, fc, :], gT[:, fc, :], ppc('ssdn', fc), rs[:, fc // 4, :], ALU.mult, ALU.mult,
                      [gT, pp, rs], [yc_])
            k.dma('sync', yT[0:2048, cs].rearrange("(b p) t -> p b t", p=128), yc_[:, :, :], [yc_], [yT])
        ssd.append(s_epi)


        def h_pro():
            for half in range(2):
                bb = bank_bf(5)
                for j in range(8):
                    k.tr(bb[:, j * 128:(j + 1) * 128], kdc[:, half * 8 + j, :], ident_b, [kdc, cbf], [pb[5]])
                k.cpa(kd_tok[:, half * 8:(half + 1) * 8, :].rearrange("p a b -> p (a b)"), bb[:, 0:1024], [pb[5]], [kd_tok])
        hg_.append(h_pro)

        def mk_h_batch(hq):
            def f():
                h0 = hq * 4
                ob_ = 2 + hq % 2
                for q in range(4):
                    h = h0 + q
                    k.mm(bank(7)[:, q * 128:(q + 1) * 128], ktc[:, h, :], qtc[:, h, :], True, True, [ktc, qtc], [pb[7]])
                am4 = attm4[hq % 2]
                k.tt('vector', am4[:, :, :], bank(7).rearrange("p (a b) -> p a b", a=4),
                     le_f.unsqueeze(1).broadcast_to([128, 4, 128]), ALU.mult, [pb[7], cst], [am4])
                for q in range(4):
                    h = h0 + q
                    hs = slice(h * 128, (h + 1) * 128)
                    oc = slice(q * 128, (q + 1) * 128)
                    k.mm(bank(ob_)[:, oc], vc[:, hs], am4[:, q, :], True, False, [vc, am4], [pb[ob_]])
                    k.mm(bank(ob_)[:, oc], hprev_b[:, h, :], qtc[:, h, :], False, True, [hprev_b, qtc], [pb[ob_]])
                for q in range(4):
                    h = h0 + q
                    hs = slice(h * 128, (h + 1) * 128)
                    k.mm(bank(5)[:, q * 128:(q + 1) * 128], kd_tok[:, h, :], vc[:, hs], True, True, [kd_tok, vc], [pb[5]])
                hp4 = hprev[:, h0:h0 + 4, :]
                k.tt('vector', hp4, hp4, ebl[:, h0:h0 + 4, c:c + 1].broadcast_to([128, 4, 128]), ALU.mult,
                     [hprev, ebl], [hprev])
                k.tt('vector', hp4, hp4, bank(5).rearrange("p (a b) -> p a b", a=4), ALU.add, [hprev, pb[5]], [hprev])
                k.cp('scalar', hprev_b[:, h0:h0 + 4, :], hp4, [hprev], [hprev_b])
                k.cp('scalar', osb[:, hq * 512:(hq + 1) * 512], bank(ob_), [pb[ob_]], [osb])
                k.act(sqh[:, hq * 512:(hq + 1) * 512], bank(ob_), AF.Square, [pb[ob_]], [sqh])
            return f
        for hq in range(4):
            hg_.append(mk_h_batch(hq))

        def h_epi():
            for q4 in range(4):
                qs = slice(q4 * 512, (q4 + 1) * 512)
                k.mm(bank(5), ones_f, sqh[:, qs], True, True, [cst, sqh], [pb[5]])
                k.ts('vector', rsh[:, qs], bank(5), 1.0 / 128, 1e-6, ALU.mult, ALU.add, [pb[5]], [rsh])
            k.act(rsh[:, :], rsh[:, :], AF.Sqrt, [rsh], [rsh])
            k.recip(rsh[:, :], rsh[:, :], [rsh], [rsh])
            k.tt('vector', rsh[:, :], osb[:, :], rsh[:, :], ALU.mult, [osb, rsh], [rsh])
            rsh3 = rsh[:, :].rearrange("p (a b) -> p a b", a=16)
            k.tt('gpsimd', rsh3, rsh3, pp[:, o_hg:o_hg + 16].unsqueeze(2).broadcast_to([128, 16, 128]), ALU.mult,
                 [rsh, pp], [rsh])
            yd_ = ydb[p2]
            k.tt('vector', yd_[:, :, :], rsh3, sgc[:, :, :], ALU.mult, [rsh, sgc], [yd_])
            k.dma('sync', yT[2048:4096, cs].rearrange("(b p) t -> p b t", p=128), yd_[:, :, :], [yd_], [yT])
        hg_.append(h_epi)
        return ssd, hg_, s_load_all

    def merge2(lists):
        items = []
        for li, l in enumerate(lists):
            n = len(l)
            for idx, f in enumerate(l):
                items.append(((idx + 0.5) / n, li, idx, f))
        items.sort(key=lambda t: (t[0], t[1], t[2]))
        return [t[3] for t in items]

    cks = [chunk_steps(c) for c in range(16)]
    cks[0][2]()
    for c in range(16):
        a_, b_, _ = cks[c]
        if c + 1 < 16:
            cks[c + 1][2]()
        for f in merge2([a_, b_]):
            f()
    P.barrier()
    sb.reset(base_mark)

    out_proj(cd_out_w, h1, h2)

    nwb = sb.alloc([128, D], F32, 'fnw')
    xt2 = [sb.alloc([128, D], F32, f'fx{i}') for i in range(2)]
    xo2 = [sb.alloc([128, D], F32, f'fo{i}') for i in range(2)]
    junk = sb.alloc([128, D], BF16, 'fjunk')
    st = sb.alloc([128, 4], F32, 'fst')
    k.dma('sync', nwb[:, :], final_norm[:].partition_broadcast(128), [], [nwb])
    for tt in range(16):
        xt = xt2[tt % 2]
        xo = xo2[tt % 2]
        k.dma('sync', xt[:, :], h2[tt * 128:(tt + 1) * 128, :], [h2], [xt])
        k.act(junk[:, :], xt[:, :], AF.Square, [xt], [junk, st], accum=st[:, 0:1])
        k.ts('vector', st[:, 1:2], st[:, 0:1], 1.0 / D, 1e-6, ALU.mult, ALU.add, [st], [st])
        k.act(st[:, 2:3], st[:, 1:2], AF.Sqrt, [st], [st])
        k.recip(st[:, 3:4], st[:, 2:3], [st], [st])
        k.stt(xo[:, :], xt[:, :], st[:, 3:4], nwb[:, :], ALU.mult, ALU.mult, [xt, st, nwb], [xo])
        k.dma('sync', out_d[tt * 128:(tt + 1) * 128, :], xo[:, :], [xo], [out_d])
    P.emit()
    return nc


_W_KEYS = (("norm_w", None), ("final_norm", None), ("ab_in_w", 0), ("ab_rg_wa", 0), ("ab_rg_wx", 0),
           ("ab_w_uk", 0), ("ab_w_uv", 0), ("ab_out_w", 0), ("cd_in_w", 0), ("cd_a_log", 0),
           ("cd_d_skip", 0), ("cd_out_w", 0))


def make_in_maps(inp, batches):
    common = {}
    for name, idx in _W_KEYS:
        a = np.asarray(inp[name], dtype=np.float32)
        common[name] = np.ascontiguousarray(a if idx is None else a[idx])
    common["pp"] = pack_params(inp)
    cst, rst = make_consts()
    common["cst"] = cst
    common["rst"] = rst
    maps = []
    for b in batches:
        d = dict(common)
        d["x"] = np.ascontiguousarray(np.asarray(inp["x"][b], dtype=np.float32))
        maps.append(d)
    return maps


def kernel(**inputs):
    nc = build()
    in_maps = make_in_maps(inputs, range(8))
    res = run_bass_kernel_spmd(nc, in_maps, core_ids=list(range(8)))
    return np.stack([np.asarray(res.results[b]["out"], dtype=np.float32) for b in range(8)], axis=0)
```

```python
import numpy as np
import concourse.bass as bass
import concourse.mybir as mybir
from concourse.bass_utils import run_bass_kernel_spmd

F32 = mybir.dt.float32
BF16 = mybir.dt.bfloat16
AF = mybir.ActivationFunctionType
ALU = mybir.AluOpType
T = 2048
D = 4096
AB_COLS = 10848
CD_COLS = 13344
EPOCH = 30000
SB_BASE = 16640
SB_CAP = SB_BASE + 207 * 1024
NEG = -1.0e30


class Ins:
    __slots__ = ('eng', 'fn', 'deps', 'signal', 'seq', 'dkey', 'dcnt', 'order', 'isdma')


class Buf:
    __slots__ = ('name', 'w', 'r')

    def __init__(self, name=''):
        self.name = name
        self.w = {}
        self.r = {}


class Tn:
    def __init__(self, h, name=''):
        self.h = h
        self.b = Buf(name)

    def __getitem__(self, k):
        return self.h[k]


class Prog:
    ENGS = ('tensor', 'vector', 'scalar', 'gpsimd', 'sync')

    def __init__(self, nc, ndma=12):
        self.nc = nc
        self.streams = {e: [] for e in self.ENGS}
        self.order = 0
        self.ndma = ndma
        self.rr = {}
        self.dlast = {}
        self.dcount = {}
        self.lastc = {}
        self.barrier_deps = {e: None for e in self.ENGS}

    def add(self, eng, fn, reads=(), writes=(), dma=False):
        ins = Ins()
        ins.eng = eng
        ins.fn = fn
        ins.signal = False
        ins.seq = None
        ins.isdma = dma
        ins.dcnt = 0
        ins.order = self.order
        self.order += 1
        deps = {}

        def need(d):
            k = d.dkey
            if k not in deps or deps[k].order < d.order:
                deps[k] = d
        for b in reads:
            for d in b.w.values():
                need(d)
        for b in writes:
            for d in b.w.values():
                need(d)
            for d in b.r.values():
                need(d)
        bd = self.barrier_deps[eng]
        if bd is not None:
            for d in bd:
                need(d)
            self.barrier_deps[eng] = None
        if dma:
            k = self.rr.get(eng, 0)
            self.rr[eng] = (k + 1) % self.ndma
            ins.dkey = ('d', eng, k)
            prev = self.dlast.get(ins.dkey)
            if prev is not None:
                need(prev)
            self.dlast[ins.dkey] = ins
            ins.dcnt = self.dcount.get(ins.dkey, 0) + 16
            self.dcount[ins.dkey] = ins.dcnt
        else:
            ins.dkey = eng
            if eng == 'tensor':
                deps.pop('tensor', None)
            self.lastc[eng] = ins
        for d in deps.values():
            if not d.isdma:
                d.signal = True
        ins.deps = list(deps.values())
        for b in writes:
            if b.r:
                b.w = {}
                b.r = {}
            b.w[ins.dkey] = ins
        for b in reads:
            b.r[ins.dkey] = ins
        self.streams[eng].append(ins)
        return ins

    def barrier(self):
        deps = list(self.lastc.values()) + list(self.dlast.values())
        for e in self.ENGS:
            self.barrier_deps[e] = list(deps)

    def emit(self):
        nc = self.nc
        self.barrier()
        for e in self.ENGS:
            self.add(e, None)
        nsig = {}
        for e, st in self.streams.items():
            n = 0
            for ins in st:
                if (not ins.isdma) and ins.signal:
                    n += 1
                    ins.seq = n
            nsig[e] = n
        csem = {e: [nc.alloc_semaphore(name=f"c_{e}_{i}") for i in range((nsig[e] + EPOCH - 1) // EPOCH)]
                for e in self.ENGS}
        dsem = {k: nc.alloc_semaphore(name=f"d_{k[1]}_{k[2]}") for k in self.dcount}
        streams = self.streams

        def run(ename, eng):
            waited = {}
            maxep = {}
            for ins in streams[ename]:
                for d in ins.deps:
                    if d.isdma:
                        key = d.dkey
                        sem = dsem[key]
                        val = d.dcnt
                    else:
                        ep = (d.seq - 1) // EPOCH
                        if maxep.get(d.eng, -1) > ep:
                            continue
                        key = (d.eng, ep)
                        sem = csem[d.eng][ep]
                        val = (d.seq - 1) % EPOCH + 1
                    if waited.get(key, 0) >= val:
                        continue
                    eng.wait_ge(sem, val)
                    waited[key] = val
                    if not d.isdma:
                        maxep[d.eng] = max(maxep.get(d.eng, -1), ep)
                if ins.fn is None:
                    continue
                r = ins.fn(eng)
                if ins.isdma:
                    r.then_inc(dsem[ins.dkey], 16)
                elif ins.signal:
                    ep = (ins.seq - 1) // EPOCH
                    r.then_inc(csem[ename][ep], 1)

        with nc.Block() as block:
            @block.tensor
            def _(e):
                run('tensor', e)

            @block.vector
            def _(e):
                run('vector', e)

            @block.scalar
            def _(e):
                run('scalar', e)

            @block.gpsimd
            def _(e):
                run('gpsimd', e)

            @block.sync
            def _(e):
                run('sync', e)


def _size(dt):
    return 2 if dt == BF16 else 4


class SBAlloc:
    def __init__(self, nc):
        self.nc = nc
        self.ptr = SB_BASE
        self.n = 0

    def alloc(self, shape, dtype, name='t'):
        nb = int(np.prod(shape[1:])) * _size(dtype)
        off = (self.ptr + 63) // 64 * 64
        self.ptr = off + nb
        assert self.ptr <= SB_CAP, f"SBUF overflow {self.ptr} at {name}"
        self.n += 1
        h = self.nc.alloc_sbuf_tensor_at(f"{name}_{self.n}", list(shape), dtype, offset=off)
        return Tn(h, name)

    def mark(self):
        return self.ptr

    def reset(self, m):
        self.ptr = m


def bl(x):
    out = []
    for t in x:
        if isinstance(t, Tn):
            out.append(t.b)
        elif isinstance(t, Buf):
            out.append(t)
        else:
            out.extend(t)
    return out


class K:
    def __init__(self, P):
        self.P = P
        self.alt = 0

    def mm(self, out, lhsT, rhs, start, stop, R, W):
        self.P.add('tensor', lambda e: e.matmul(out, lhsT, rhs, start=start, stop=stop), bl(R), bl(W))

    def tr(self, out, in_, ident, R, W):
        self.P.add('tensor', lambda e: e.transpose(out, in_, ident), bl(R), bl(W))

    def act(self, out, in_, func, R, W, bias=None, scale=None, accum=None):
        kw = {}
        if bias is not None:
            kw['bias'] = bias
        if scale is not None:
            kw['scale'] = scale
        if accum is not None:
            kw['accum_out'] = accum
        self.P.add('scalar', lambda e: e.activation(out, in_, func, **kw), bl(R), bl(W))

    def tt(self, eng, out, in0, in1, op, R, W):
        self.P.add(eng, lambda e: e.tensor_tensor(out, in0, in1, op), bl(R), bl(W))

    def ts(self, eng, out, in0, s1, s2, op0, op1, R, W):
        if s2 is None:
            self.P.add(eng, lambda e: e.tensor_scalar(out, in0, s1, None, op0), bl(R), bl(W))
        else:
            self.P.add(eng, lambda e: e.tensor_scalar(out, in0, s1, s2, op0, op1), bl(R), bl(W))

    def stt(self, out, in0, scalar, in1, op0, op1, R, W):
        self.P.add('vector', lambda e: e.scalar_tensor_tensor(out, in0, scalar, in1, op0, op1), bl(R), bl(W))

    def cp(self, eng, out, in_, R, W):
        if eng == 'scalar':
            self.P.add('scalar', lambda e: e.activation(out, in_, AF.Copy), bl(R), bl(W))
        else:
            self.P.add(eng, lambda e: e.tensor_copy(out, in_), bl(R), bl(W))

    def cpa(self, out, in_, R, W):
        self.alt ^= 1
        self.cp('scalar' if self.alt else 'vector', out, in_, R, W)

    def recip(self, out, in_, R, W):
        self.P.add('vector', lambda e: e.reciprocal(out, in_), bl(R), bl(W))

    def scan(self, out, d0, d1, init, op0, op1, R, W):
        self.P.add('vector', lambda e: e.tensor_tensor_scan(out, d0, d1, init, op0, op1), bl(R), bl(W))

    def memset(self, eng, out, val, W):
        self.P.add(eng, lambda e: e.memset(out, val), [], bl(W))

    def dma(self, q, out, in_, R, W):
        self.P.add(q, lambda e: e.dma_start(out=out, in_=in_), bl(R), bl(W), dma=True)


PP = {}
_o = 0
for _n, _w in (('cw0', 64), ('cb0', 16), ('ba', 16), ('bx', 16), ('lam', 16), ('kvn', 4),
               ('cw1', 96), ('cb1', 24), ('dtb', 1), ('ssdn', 16), ('hgn', 16), ('lbs', 32)):
    PP[_n] = (_o, _w)
    _o += _w
NPP = _o
NCST = 640


def pack_params(inp):
    pp = np.zeros((128, NPP), np.float32)

    def put(name, arr):
        o, w = PP[name]
        pp[:arr.shape[0], o:o + w] = arr.reshape(arr.shape[0], -1)
    put('cw0', inp['ab_conv_w'][0].reshape(4, 16, 128).transpose(2, 1, 0))
    put('cb0', inp['ab_conv_b'][0].reshape(16, 128).T)
    put('ba', inp['ab_rg_ba'][0].reshape(16, 128).T)
    put('bx', inp['ab_rg_bx'][0].reshape(16, 128).T)
    put('lam', inp['ab_rg_lambda'][0].reshape(16, 128).T)
    put('kvn', inp['ab_kv_norm'][0].reshape(4, 128).T)
    put('cw1', inp['cd_conv_w'][0].reshape(4, 24, 128).transpose(2, 1, 0))
    put('cb1', inp['cd_conv_b'][0].reshape(24, 128).T)
    put('dtb', inp['cd_dt_bias'][0].reshape(32, 1))
    put('ssdn', inp['cd_ssd_norm'][0].reshape(16, 128).T)
    put('hgn', inp['cd_hgrn_norm'][0].reshape(16, 128).T)
    put('lbs', inp['hgrn_lower_bounds'].reshape(2, 16, 128).transpose(2, 0, 1))
    return pp


def make_consts():
    c = np.zeros((128, NCST), np.float32)
    p = np.arange(128)[:, None]
    j = np.arange(128)[None, :]
    c[:, 0:128] = (p == j)
    c[:, 128:256] = (p <= j)
    c[:, 256:384] = (p > j)
    c[:, 384:512] = np.where(j <= p, 0.0, NEG)
    c[:, 512:640] = 1.0
    rst = np.ones((128, T), np.float32)
    rst[:, ::128] = 0.0
    return c, rst


def build(debug=False, stop_after=None):
    nc = bass.Bass("TRN2", target_bir_lowering=False)
    P = Prog(nc)
    k = K(P)
    sb = SBAlloc(nc)

    def din(name, shape, dt=F32):
        return Tn(nc.dram_tensor(name, list(shape), dt, kind="ExternalInput").ap(), name)

    def dscr(name, shape, dt, out=False):
        kind = "ExternalOutput" if (out or debug) else "Internal"
        return Tn(nc.dram_tensor(name, list(shape), dt, kind=kind).ap(), name)

    x = din("x", [T, D])
    norm_w = din("norm_w", [2, D])
    final_norm = din("final_norm", [D])
    ab_in_w = din("ab_in_w", [D, AB_COLS])
    wa = din("ab_rg_wa", [16, 128, 128])
    wx = din("ab_rg_wx", [16, 128, 128])
    w_uk = din("ab_w_uk", [16, 512, 128])
    w_uv = din("ab_w_uv", [16, 512, 128])
    ab_out_w = din("ab_out_w", [D, D])
    cd_in_w = din("cd_in_w", [D, CD_COLS])
    a_log = din("cd_a_log", [32])
    d_skip = din("cd_d_skip", [32])
    cd_out_w = din("cd_out_w", [D, D])
    pp_d = din("pp", [128, NPP])
    cst_d = din("cst", [128, NCST])
    rst_d = din("rst", [128, T])
    out_d = dscr("out", [T, D], F32, out=True)

    yT = dscr("s_yT", [D, T], BF16)
    h1 = dscr("s_h1", [T, D], F32)
    h2 = dscr("s_h2", [T, D], F32)
    ckv_raw = dscr("s_ckv", [512, T], F32)
    qlat_s = dscr("s_qlat", [16, 4, 128, T], BF16)
    qiT_s = dscr("s_qi", [2048, T], BF16)
    sgb_s = dscr("s_sgb", [2048, T], BF16)
    sz_s = dscr("s_sz", [2048, T], BF16)
    xc_s = dscr("s_xc", [2048, T], BF16)
    BT_s = dscr("s_B", [512, T], BF16)
    CT_s = dscr("s_C", [512, T], BF16)
    qt_s = dscr("s_qt", [2048, T], BF16)
    kt_s = dscr("s_kt", [2048, T], BF16)
    kd_s = dscr("s_kd", [2048, T], BF16)
    sgd_s = dscr("s_sgd", [2048, T], BF16)
    v_s = dscr("s_v", [T, 2048], BF16)

    ps = nc.alloc_psum_tensor("ps", [128, 4096], F32)
    pb = [Buf(f"bank{i}") for i in range(8)]

    def bank(i):
        return ps[:, i * 512:(i + 1) * 512]

    def bank_bf(i):
        return ps[:, i * 512:(i + 1) * 512].bitcast(BF16)

    class Grp:
        def __init__(self, g):
            self.ap = ps[:, g * 2048:(g + 1) * 2048]
            self.bufs = pb[g * 4:(g + 1) * 4]
    G = [Grp(0), Grp(1)]

    cst = sb.alloc([128, NCST], F32, 'cst')
    pp = sb.alloc([128, NPP], F32, 'pp')
    cbf = sb.alloc([128, 384], BF16, 'cbf')
    k.dma('sync', cst[:, :], cst_d[:, :], [], [cst])
    k.dma('sync', pp[:, :], pp_d[:, :], [], [pp])
    k.cp('vector', cbf[:, 0:256], cst[:, 0:256], [cst], [cbf])
    k.cp('vector', cbf[:, 256:384], cst[:, 512:640], [cst], [cbf])
    ident_f = cst[:, 0:128]
    le_f = cst[:, 128:256]
    gt_f = cst[:, 256:384]
    cbias_f = cst[:, 384:512]
    ones_f = cst[:, 512:640]
    ident_b = cbf[:, 0:128]
    le_b = cbf[:, 128:256]
    ones_b = cbf[:, 256:384]

    def ppc(name, j=0, n=1):
        o, w = PP[name]
        return pp[:, o + j:o + j + n]

    kiT2 = sb.alloc([128, T], BF16, 'kiT2')
    wi_tok = sb.alloc([128, 16, 32], F32, 'wi_tok')
    dt_tok = sb.alloc([128, 16, 32], F32, 'dt_tok')
    dtA_tok = sb.alloc([128, 16, 32], F32, 'dtA_tok')
    ebl = sb.alloc([128, 16, 16], F32, 'ebl')
    base_mark = sb.mark()

    def phase_norm_T(src, nw_ap, hnT):
        m = sb.mark()
        nwb = sb.alloc([128, D], F32, 'nwb')
        xt_2 = [sb.alloc([128, D], F32, f'xt{i}') for i in range(2)]
        xn = sb.alloc([128, D], BF16, 'xn')
        junk = xn
        st = sb.alloc([128, 4], F32, 'st')
        k.dma('sync', nwb[:, :], nw_ap.partition_broadcast(128), [], [nwb])
        for tt in range(16):
            xt = xt_2[tt % 2]
            k.dma('sync', xt[:, :], src[tt * 128:(tt + 1) * 128, :], [src], [xt])
            k.act(junk[:, :], xt[:, :], AF.Square, [xt], [junk, st], accum=st[:, 0:1])
            k.ts('vector', st[:, 1:2], st[:, 0:1], 1.0 / D, 1e-6, ALU.mult, ALU.add, [st], [st])
            k.act(st[:, 2:3], st[:, 1:2], AF.Sqrt, [st], [st])
            k.recip(st[:, 3:4], st[:, 2:3], [st], [st])
            k.stt(xn[:, :], xt[:, :], st[:, 3:4], nwb[:, :], ALU.mult, ALU.mult, [xt, st, nwb], [xn])
            for g in range(4):
                bi = (tt * 4 + g) % 8
                bb = bank_bf(bi)
                for j in range(8):
                    c = g * 8 + j
                    k.tr(bb[:, j * 128:(j + 1) * 128], xn[:, c * 128:(c + 1) * 128], ident_b,
                         [xn, cbf], [pb[bi]])
                k.cpa(hnT[:, g * 8:(g + 1) * 8, tt * 128:(tt + 1) * 128],
                      bb.rearrange("p (a b) -> p a b", a=8), [pb[bi]], [hnT])
        P.barrier()
        sb.reset(m)

    def linear_fm(hnT, W, blocks, wslots, evac, wtag):
        for bi, segs in enumerate(blocks):
            ws = wslots[bi % len(wslots)]
            ncols = 0
            for (d0, s0, n) in segs:
                k.dma('gpsimd', ws[:, :, d0:d0 + n],
                      W[:, s0:s0 + n].rearrange("(kc p) c -> p kc c", p=128), [], [ws])
                ncols = max(ncols, d0 + n)
            grp = G[bi % 2]
            for kc in range(32):
                for tc in range(4):
                    k.mm(grp.ap[0:ncols, tc * 512:(tc + 1) * 512], ws[:, kc, 0:ncols],
                         hnT[:, kc, tc * 512:(tc + 1) * 512], kc == 0, kc == 31, [ws, hnT], [grp.bufs])
            evac(bi, grp, ncols)

    def conv_block(grp, a1, a2, cwname, cbname, blk):
        k.cp('scalar', a1[:, 3:3 + T], grp.ap, [grp.bufs], [a1])
        k.act(a2[:, :], a1[:, 3:3 + T], AF.Identity, [a1, pp], [a2],
              bias=ppc(cbname, blk), scale=ppc(cwname, blk * 4 + 3))
        for kk in range(3):
            k.stt(a2[:, :], a1[:, kk:kk + T], ppc(cwname, blk * 4 + kk), a2[:, :], ALU.mult, ALU.add,
                  [a1, pp, a2], [a2])

    hnT = sb.alloc([128, 32, T], BF16, 'hnT')
    phase_norm_T(x, norm_w[0, :], hnT)
    l0_mark = sb.mark()
    wslots = [sb.alloc([128, 32, 128], BF16, f'w{i}') for i in range(2)]

    m = sb.mark()
    a1 = sb.alloc([128, T + 4], F32, 'a1')
    a2 = sb.alloc([128, T], F32, 'a2')
    a3 = sb.alloc([128, T], F32, 'a3')
    xcb = sb.alloc([128, T], BF16, 'xcb')
    sga = sb.alloc([128, T], BF16, 'sga')
    yab = sb.alloc([128, T], BF16, 'yab')
    gw2 = [sb.alloc([128, 2, 128], BF16, f'gw{i}') for i in range(2)]
    s1 = sb.alloc([128, 48], F32, 's1')
    k.memset('vector', a1[:, 0:3], 0.0, [a1])
    k.act(s1[:, 0:16], ppc('lam', 0, 16), AF.Exp, [pp], [s1], scale=-1.0)
    k.act(s1[:, 0:16], s1[:, 0:16], AF.Ln, [s1], [s1], bias=1.0)
    k.ts('vector', s1[:, 16:32], s1[:, 0:16], -8.0, None, ALU.mult, None, [s1], [s1])
    k.ts('vector', s1[:, 32:48], s1[:, 0:16], -16.0, None, ALU.mult, None, [s1], [s1])

    evq = [sb.alloc([128, T], BF16, f'evq{i}') for i in range(2)]

    def rg_gates(g):
        gw = gw2[g % 2]
        g0 = G[0]
        for tc in range(4):
            k.mm(g0.ap[:, tc * 512:(tc + 1) * 512], gw[:, 0, :], xcb[:, tc * 512:(tc + 1) * 512],
                 True, True, [gw, xcb], [g0.bufs])
        k.act(a1[:, 0:T], g0.ap, AF.Sigmoid, [g0.bufs, pp], [a1], bias=ppc('ba', g))
        g1 = G[1]
        for tc in range(4):
            k.mm(g1.ap[:, tc * 512:(tc + 1) * 512], gw[:, 1, :], xcb[:, tc * 512:(tc + 1) * 512],
                 True, True, [gw, xcb], [g1.bufs])
        k.act(a3[:, :], a1[:, 0:T], AF.Exp, [a1, s1], [a3], scale=s1[:, 32 + g:33 + g])
        k.act(a1[:, 0:T], a1[:, 0:T], AF.Exp, [a1, s1], [a1], scale=s1[:, 16 + g:17 + g])
        k.act(a3[:, :], a3[:, :], AF.Sqrt, [a3], [a3], bias=1.0, scale=-1.0)
        k.tt('vector', a3[:, :], a3[:, :], a2[:, :], ALU.mult, [a3, a2], [a3])
        k.act(a2[:, :], g1.ap, AF.Sigmoid, [g1.bufs, pp], [a2], bias=ppc('bx', g))
        k.tt('vector', a3[:, :], a3[:, :], a2[:, :], ALU.mult, [a3, a2], [a3])
        k.scan(a2[:, :], a1[:, 0:T], a3[:, :], 0.0, ALU.mult, ALU.add, [a1, a3], [a2])
        k.tt('vector', yab[:, :], a2[:, :], sga[:, :], ALU.mult, [a2, sga], [yab])
        k.dma('sync', yT[g * 128:(g + 1) * 128, :], yab[:, :], [yab], [yT])
        k.memset('vector', a1[:, 0:3], 0.0, [a1])

    def evac_rg(bi, grp, ncols):
        g = bi // 4
        kind = bi % 4
        if kind == 0:
            conv_block(grp, a1, a2, 'cw0', 'cb0', g)
            k.cp('scalar', xcb[:, :], a2[:, :], [a2], [xcb])
            gw = gw2[g % 2]
            k.dma('gpsimd', gw[:, 0, :], wa[g, :, :], [], [gw])
            k.dma('gpsimd', gw[:, 1, :], wx[g, :, :], [], [gw])
        elif kind == 1:
            k.act(sga[:, :], grp.ap, AF.Silu, [grp.bufs], [sga])
        elif kind == 2:
            ob = evq[0]
            k.act(ob[:, :], grp.ap, AF.Silu, [grp.bufs], [ob])
            k.dma('sync', sgb_s[g * 128:(g + 1) * 128, :], ob[:, :], [ob], [sgb_s])
        else:
            ob = evq[1]
            k.cp('vector', ob[:, :], grp.ap, [grp.bufs], [ob])
            k.dma('sync', qiT_s[g * 128:(g + 1) * 128, :], ob[:, :], [ob], [qiT_s])
            rg_gates(g)

    blocks = []
    for g in range(16):
        blocks.append([(0, g * 128, 128)])
        blocks.append([(0, 2048 + g * 128, 128)])
        blocks.append([(0, 8800 + g * 128, 128)])
        blocks.append([(0, 6656 + g * 128, 128)])
    linear_fm(hnT, ab_in_w, blocks, wslots, evac_rg, 'rg')
    P.barrier()
    sb.reset(m)

    sb.reset(l0_mark)
    wslots = [sb.alloc([128, 32, 128], BF16, f'w{i}') for i in range(2)]
    m = sb.mark()
    ev = [sb.alloc([128, T], F32, f'ev{i}') for i in range(1)]
    evb = [sb.alloc([128, T], BF16, f'evb{i}') for i in range(3)]
    wukT = sb.alloc([128, 16, 512], BF16, 'wukT')
    wuk_ld2 = [sb.alloc([128, 4, 128], BF16, f'wuk_ld{i}') for i in range(2)]
    cnt = [0]

    for h in range(16):
        wuk_ld = wuk_ld2[h % 2]
        k.dma('gpsimd', wuk_ld[:, :, :], w_uk[h, :, :].rearrange("(cc p) d -> p cc d", p=128), [], [wuk_ld])
        bi = h % 8
        bb = bank_bf(bi)
        for cc in range(4):
            k.tr(bb[:, cc * 128:(cc + 1) * 128], wuk_ld[:, cc, :], ident_b, [wuk_ld, cbf], [pb[bi]])
        k.cpa(wukT[:, h, :], bb[:, 0:512], [pb[bi]], [wukT])

    def evac_ckv(bi, grp, ncols):
        t_ = ev[0]
        k.cpa(t_[:, :], grp.ap, [grp.bufs], [t_])
        k.dma('sync', ckv_raw[bi * 128:(bi + 1) * 128, :], t_[:, :], [t_], [ckv_raw])
    linear_fm(hnT, ab_in_w, [[(0, 6144 + i * 128, 128)] for i in range(4)], wslots, evac_ckv, 'ckv')

    def evac_q(h, grp, ncols):
        qb = evb[2]
        k.cp('scalar', qb[:, :], grp.ap, [grp.bufs], [qb])
        for cc in range(4):
            g2 = G[(h + 1 + cc) % 2]
            for tc in range(4):
                k.mm(g2.ap[:, tc * 512:(tc + 1) * 512], wukT[:, h, cc * 128:(cc + 1) * 128],
                     qb[:, tc * 512:(tc + 1) * 512], True, True, [wukT, qb], [g2.bufs])
            ob = evb[cc % 2]
            if cc % 2 == 0:
                k.act(ob[:, :], g2.ap, AF.Copy, [g2.bufs], [ob], scale=float(128 ** -0.5))
            else:
                k.ts('vector', ob[:, :], g2.ap, float(128 ** -0.5), None, ALU.mult, None, [g2.bufs], [ob])
            k.dma('sync', qlat_s[h, cc, :, :], ob[:, :], [ob], [qlat_s])
    linear_fm(hnT, ab_in_w, [[(0, 4096 + h * 128, 128)] for h in range(16)], wslots, evac_q, 'q')

    def evac_ki(bi, grp, ncols):
        k.cp('scalar', kiT2[:, :], grp.ap, [grp.bufs], [kiT2])
    linear_fm(hnT, ab_in_w, [[(0, 8704, 64), (64, 8704, 64)]], wslots, evac_ki, 'ki')

    def evac_wi(bi, grp, ncols):
        t_ = ev[0]
        k.cp('scalar', t_[0:32, :], grp.ap[0:32, :], [grp.bufs], [t_])
        for tt in range(16):
            bi2 = 4 + tt % 4
            k.tr(bank(bi2)[:, 0:32], t_[0:32, tt * 128:(tt + 1) * 128], ident_f[0:32, 0:32], [t_, cst], [pb[bi2]])
            k.cp('vector', wi_tok[:, tt, :], bank(bi2)[:, 0:32], [pb[bi2]], [wi_tok])
    linear_fm(hnT, ab_in_w, [[(0, 8768, 32)]], wslots, evac_wi, 'wi')

    P.barrier()
    sb.reset(m)

    sb.reset(base_mark)
    ckvT = sb.alloc([128, 4, T], BF16, 'ckvT')
    ckv_tok = sb.alloc([128, 16, 512], BF16, 'ckv_tok')
    wuv = sb.alloc([128, 16, 4, 128], BF16, 'wuv')
    thr_c = sb.alloc([128, 1], F32, 'thr_c')
    k.memset('vector', thr_c[:, :], -1.0e29, [thr_c])
    for h in range(16):
        k.dma('gpsimd', wuv[:, h, :, :], w_uv[h, :, :].rearrange("(cc p) d -> p cc d", p=128), [], [wuv])
    m2 = sb.mark()
    craw = sb.alloc([128, 4, T], F32, 'craw')
    csq = sb.alloc([128, 4, T], F32, 'csq')
    rstd = sb.alloc([128, T], F32, 'rstd')
    k.dma('sync', craw[:, :, :], ckv_raw[:, :].rearrange("(cc p) t -> p cc t", p=128), [ckv_raw], [craw])
    k.act(csq[:, :, :], craw[:, :, :], AF.Square, [craw], [csq])
    for tc in range(4):
        for cc in range(4):
            k.mm(G[0].ap[:, tc * 512:(tc + 1) * 512], ones_f, csq[:, cc, tc * 512:(tc + 1) * 512],
                 cc == 0, cc == 3, [cst, csq], [G[0].bufs])
    k.ts('vector', rstd[:, :], G[0].ap, 1.0 / 512, 1e-6, ALU.mult, ALU.add, [G[0].bufs], [rstd])
    k.act(rstd[:, :], rstd[:, :], AF.Sqrt, [rstd], [rstd])
    k.recip(rstd[:, :], rstd[:, :], [rstd], [rstd])
    for cc in range(4):
        k.stt(ckvT[:, cc, :], craw[:, cc, :], ppc('kvn', cc), rstd[:, :], ALU.mult, ALU.mult,
              [craw, pp, rstd], [ckvT])
    for tt in range(16):
        bi = tt % 8
        bb = bank_bf(bi)
        for cc in range(4):
            k.tr(bb[:, cc * 128:(cc + 1) * 128], ckvT[:, cc, tt * 128:(tt + 1) * 128], ident_b,
                 [ckvT, cbf], [pb[bi]])
        k.cpa(ckv_tok[:, tt, :], bb[:, 0:512], [pb[bi]], [ckv_tok])
    P.barrier()
    sb.reset(m2)

    acc2 = [sb.alloc([128, T], F32, f'acc{i}') for i in range(2)]
    wk = [sb.alloc([128, T], F32, f'wk{i}') for i in range(2)]
    rt = [sb.alloc([128, T], F32, f'rt{i}') for i in range(2)]
    m8 = sb.alloc([128, 8], F32, 'm8')
    m01 = sb.alloc([128, T], BF16, 'm01')
    maskT2 = [sb.alloc([128, 16, 128], BF16, f'maskT{i}') for i in range(2)]
    qi_t = [sb.alloc([128, 16, 128], BF16, f'qi_t{i}') for i in range(2)]
    ql_t = [sb.alloc([128, 16, 4, 128], BF16, f'ql_t{i}') for i in range(2)]
    sgb_t = [sb.alloc([128, 16, 128], BF16, f'sgb_t{i}') for i in range(2)]
    eb = [sb.alloc([128, 512], BF16, f'eb{i}') for i in range(2)]
    ptb = [sb.alloc([128, 512], BF16, f'ptb{i}') for i in range(2)]
    rden = sb.alloc([128, 512], F32, 'rden')
    olb = sb.alloc([128, 4, 512], BF16, 'olb')
    yt1 = sb.alloc([128, 512], F32, 'yt1')
    yob = [sb.alloc([128, 4, 128], BF16, f'yob{i}') for i in range(2)]

    def steps_SC(i):
        S = (i + 1) * 128
        qit = qi_t[i % 2]
        acc = acc2[i % 2]
        nsc = (S + 511) // 512
        st = []

        def s_load():
            k.dma('sync', qit[:, :, :], qiT_s[:, i * 128:(i + 1) * 128].rearrange("(b p) t -> p b t", p=128),
                  [qiT_s], [qit])
        st.append(s_load)

        def mk_head(hh):
            def f():
                blk, half = hh // 2, hh % 2
                r_ = rt[hh % 2]
                for sc in range(nsc):
                    w_ = min(512, S - sc * 512)
                    k.mm(bank(7)[:, 0:w_], qit[half * 64:(half + 1) * 64, blk, :],
                         kiT2[half * 64:(half + 1) * 64, sc * 512:sc * 512 + w_], True, True, [qit, kiT2], [pb[7]])
                    k.act(r_[:, sc * 512:sc * 512 + w_], bank(7)[:, 0:w_], AF.Relu, [pb[7]], [r_])
                if hh == 0:
                    k.ts('vector', acc[:, 0:S], r_[:, 0:S], wi_tok[:, i, 0:1], None, ALU.mult, None,
                         [r_, wi_tok], [acc])
                else:
                    k.stt(acc[:, 0:S], r_[:, 0:S], wi_tok[:, i, hh:hh + 1], acc[:, 0:S], ALU.mult, ALU.add,
                          [r_, wi_tok, acc], [acc])
            return f
        for hh in range(32):
            st.append(mk_head(hh))

        def s_bias():
            k.tt('vector', acc[:, i * 128:S], acc[:, i * 128:S], cbias_f, ALU.add, [acc, cst], [acc])
        st.append(s_bias)
        return st

    def steps_TK(i):
        S = (i + 1) * 128
        acc = acc2[i % 2]
        maskT = maskT2[i % 2]
        qlt = ql_t[i % 2]
        sgt = sgb_t[i % 2]
        st = []

        def s_load():
            k.dma('sync', qlt[:, :, :, :], qlat_s[:, :, :, i * 128:(i + 1) * 128].rearrange("h c p t -> p h c t"),
                  [qlat_s], [qlt])
            k.dma('sync', sgt[:, :, :], sgb_s[:, i * 128:(i + 1) * 128].rearrange("(b p) t -> p b t", p=128),
                  [sgb_s], [sgt])
        st.append(s_load)
        if i >= 2:
            def mk_round(r):
                def f():
                    cur = acc if r == 0 else wk[(r - 1) % 2]
                    k.P.add('vector', (lambda c, S_: (lambda e: e.max(m8[:, :], c[:, 0:S_])))(cur, S), bl([cur]), bl([m8]))
                    if r < 31:
                        nxt = wk[r % 2]
                        k.P.add('vector', (lambda c, n, S_: (lambda e: e.match_replace(n[:, 0:S_], m8[:, :], c[:, 0:S_], -3.0e38)))(cur, nxt, S),
                                bl([cur, m8]), bl([nxt]))
                return f
            for r in range(32):
                st.append(mk_round(r))

        def s_mask():
            if i >= 2:
                thr, thr_b = m8[:, 7:8], [m8]
            else:
                thr, thr_b = thr_c[:, 0:1], [thr_c]
            k.ts('vector', m01[:, 0:S], acc[:, 0:S], thr, None, ALU.is_lt, None, [acc] + thr_b, [m01])
            k.ts('vector', m01[:, 0:S], m01[:, 0:S], -30000.0, None, ALU.mult, None, [m01], [m01])
            for j0 in range(0, i + 1, 8):
                bb = bank_bf(7)
                nj = min(8, i + 1 - j0)
                for j in range(j0, j0 + nj):
                    k.tr(bb[:, (j - j0) * 128:(j - j0 + 1) * 128], m01[:, j * 128:(j + 1) * 128], ident_b,
                         [m01, cbf], [pb[7]])
                k.cp('scalar', maskT[:, j0:j0 + nj, :], bb[:, 0:nj * 128].rearrange("p (a b) -> p a b", a=nj),
                     [pb[7]], [maskT])
        st.append(s_mask)
        return st

    def steps_AT(i):
        maskT = maskT2[i % 2]
        qlt = ql_t[i % 2]
        sgt = sgb_t[i % 2]
        st = []

        def L(hg, j):
            lb_ = j % 2
            for cc in range(4):
                k.mm(bank(lb_).rearrange("p (a b) -> p a b", a=4), ckvT[:, cc, j * 128:(j + 1) * 128],
                     qlt[:, hg * 4:(hg + 1) * 4, cc, :], cc == 0, False, [ckvT, qlt], [pb[lb_]])
            k.mm(bank(lb_).rearrange("p (a b) -> p a b", a=4), ident_b,
                 maskT[:, j:j + 1, :].broadcast_to([128, 4, 128]), False, True, [cbf, maskT], [pb[lb_]])

        def mk_j(hg, j):
            def f():
                if j == 0:
                    L(hg, 0)
                p_ = ptb[j % 2]
                k.act(p_[:, :], bank(j % 2), AF.Exp, [pb[j % 2]], [p_])
                if j + 1 <= i:
                    L(hg, j + 1)
                for cc in range(4):
                    k.mm(bank(2 + cc), ckv_tok[:, j, cc * 128:(cc + 1) * 128], p_[:, :],
                         j == 0, j == i, [ckv_tok, p_], [pb[2 + cc]])
                k.mm(bank(6), ones_b, p_[:, :], j == 0, j == i, [cbf, p_], [pb[6]])
            return f

        def mk_tail(hg):
            def f():
                k.act(rden[:, :], bank(6), AF.Ln, [pb[6]], [rden])
                k.act(rden[:, :], rden[:, :], AF.Exp, [rden], [rden], scale=-1.0)
                for cc in range(4):
                    k.cp('scalar', olb[:, cc, :], bank(2 + cc), [pb[2 + cc]], [olb])
                for hl in range(4):
                    h = hg * 4 + hl
                    for cc in range(4):
                        k.mm(bank(7)[:, hl * 128:(hl + 1) * 128], wuv[:, h, cc, :], olb[:, cc, hl * 128:(hl + 1) * 128],
                             cc == 0, cc == 3, [wuv, olb], [pb[7]])
                k.cp('scalar', yt1[:, :], bank(7), [pb[7]], [yt1])
                k.tt('gpsimd', yt1[:, :], yt1[:, :], rden[:, :], ALU.mult, [yt1, rden], [yt1])
                yo = yob[hg % 2]
                k.tt('gpsimd', yo[:, :, :], yt1[:, :].rearrange("p (a b) -> p a b", a=4), sgt[:, hg * 4:(hg + 1) * 4, :],
                     ALU.mult, [yt1, sgt], [yo])
                k.dma('sync', yT[2048 + hg * 512:2048 + (hg + 1) * 512, i * 128:(i + 1) * 128].rearrange("(a p) t -> p a t", p=128),
                      yo[:, :, :], [yo], [yT])
            return f
        for hg in range(4):
            for j in range(i + 1):
                st.append(mk_j(hg, j))
            st.append(mk_tail(hg))
        return st

    def merge(lists):
        items = []
        for li, l in enumerate(lists):
            n = len(l)
            for idx, f in enumerate(l):
                items.append(((idx + 0.5) / n, li, idx, f))
        items.sort(key=lambda t: (t[0], t[1], t[2]))
        return [t[3] for t in items]

    n_qt = 16
    for f in steps_SC(0) + steps_TK(0) + steps_SC(1):
        f()
    for s_ in range(n_qt):
        lists = [steps_AT(s_)]
        if s_ + 1 < n_qt:
            lists.append(steps_TK(s_ + 1))
        if s_ + 2 < n_qt:
            lists.append(steps_SC(s_ + 2))
        for f in merge(lists):
            f()
    P.barrier()
    sb.reset(base_mark)

    def out_proj(Wout, res_src, dst):
        m = sb.mark()
        yres = sb.alloc([128, 32, 1024], BF16, 'yres')
        wo = [sb.alloc([128, 32, 512], BF16, f'wo{i}') for i in range(2)]
        xr = [sb.alloc([128, 512], F32, f'xr{i}') for i in range(4)]
        n = 0
        for th in range(2):
            for q4 in range(4):
                k.dma('sync', yres[:, q4 * 8:(q4 + 1) * 8, :],
                      yT[q4 * 1024:(q4 + 1) * 1024, th * 1024:(th + 1) * 1024].rearrange("(fc p) t -> p fc t", p=128),
                      [yT], [yres])
            for cb in range(8):
                w_ = wo[n % 2]
                n += 1
                for hf in range(2):
                    k.dma('gpsimd', w_[:, hf * 16:(hf + 1) * 16, :],
                          Wout[hf * 2048:(hf + 1) * 2048, cb * 512:(cb + 1) * 512].rearrange("(fc p) c -> p fc c", p=128),
                          [], [w_])
                for t8 in range(8):
                    tt = th * 8 + t8
                    bi = (cb * 8 + t8) % 8
                    xr_ = xr[(cb * 8 + t8) % 4]
                    k.dma('sync', xr_[:, :], res_src[tt * 128:(tt + 1) * 128, cb * 512:(cb + 1) * 512], [res_src], [xr_])
                    for fc in range(32):
                        k.mm(bank(bi), yres[:, fc, t8 * 128:(t8 + 1) * 128], w_[:, fc, :],
                             fc == 0, fc == 31, [yres, w_], [pb[bi]])
                    k.tt('vector', xr_[:, :], bank(bi), xr_[:, :], ALU.add, [pb[bi], xr_], [xr_])
                    k.dma('sync', dst[tt * 128:(tt + 1) * 128, cb * 512:(cb + 1) * 512], xr_[:, :], [xr_], [dst])
        P.barrier()
        sb.reset(m)

    out_proj(ab_out_w, x, h1)

    if stop_after == 'l0':
        P.emit()
        return nc

    sb.reset(base_mark)
    hnT = sb.alloc([128, 32, T], BF16, 'hnT1')
    phase_norm_T(h1, norm_w[1, :], hnT)
    l1_mark = sb.mark()
    wslots = [sb.alloc([128, 32, 128], BF16, f'w{i}') for i in range(2)]
    m = sb.mark()
    evb = [sb.alloc([128, T], BF16, f'evb{i}') for i in range(3)]
    a1 = sb.alloc([128, T + 4], F32, 'a1')
    a2 = sb.alloc([128, T], F32, 'a2')
    a3 = sb.alloc([128, T], F32, 'a3')
    rstb = sb.alloc([128, T], BF16, 'rstb')
    sm = sb.alloc([128, 160], F32, 'sm')
    k.dma('gpsimd', rstb[:, :], rst_d[:, :], [], [rstb])
    k.memset('vector', a1[:, 0:3], 0.0, [a1])
    lb_ = sm[:, 0:16]
    oml = sm[:, 16:32]
    a_b = sm[:, 32:64]
    o_l, _ = PP['lbs']
    k.tt('vector', lb_, pp[:, o_l + 16:o_l + 32], pp[:, o_l:o_l + 16], ALU.subtract, [pp], [sm])
    k.act(lb_, lb_, AF.Sigmoid, [sm], [sm])
    k.ts('vector', oml, lb_, -1.0, 1.0, ALU.mult, ALU.add, [sm], [sm])
    k.dma('sync', a_b, a_log[:].partition_broadcast(128), [], [sm])
    k.act(a_b, a_b, AF.Exp, [sm], [sm])
    k.ts('vector', a_b, a_b, -1.0, None, ALU.mult, None, [sm], [sm])

    def evac_z(bi, grp, ncols):
        ob = evb[bi % 3]
        k.act(ob[:, :], grp.ap, AF.Silu, [grp.bufs], [ob])
        k.dma('sync', sz_s[bi * 128:(bi + 1) * 128, :], ob[:, :], [ob], [sz_s])
    linear_fm(hnT, cd_in_w, [[(0, i * 128, 128)] for i in range(16)], wslots, evac_z, 'z')

    def evac_xbc(bi, grp, ncols):
        conv_block(grp, a1, a2, 'cw1', 'cb1', bi)
        ob = evb[bi % 3]
        k.act(ob[:, :], a2[:, :], AF.Silu, [a2], [ob])
        if bi < 16:
            k.dma('sync', xc_s[bi * 128:(bi + 1) * 128, :], ob[:, :], [ob], [xc_s])
        elif bi < 20:
            k.dma('sync', BT_s[(bi - 16) * 128:(bi - 15) * 128, :], ob[:, :], [ob], [BT_s])
        else:
            k.dma('sync', CT_s[(bi - 20) * 128:(bi - 19) * 128, :], ob[:, :], [ob], [CT_s])
    linear_fm(hnT, cd_in_w, [[(0, 2048 + i * 128, 128)] for i in range(24)], wslots, evac_xbc, 'xbc')

    def evac_dt(bi, grp, ncols):
        k.act(a3[0:32, :], grp.ap[0:32, :], AF.Exp, [grp.bufs, pp], [a3], bias=pp[0:32, PP['dtb'][0]:PP['dtb'][0] + 1])
        k.act(a3[0:32, :], a3[0:32, :], AF.Ln, [a3], [a3], bias=1.0)
        for tt in range(16):
            bi2 = 4 + tt % 4
            k.tr(bank(bi2)[:, 0:32], a3[0:32, tt * 128:(tt + 1) * 128], ident_f[0:32, 0:32], [a3, cst], [pb[bi2]])
            k.cp('vector', dt_tok[:, tt, :], bank(bi2)[:, 0:32], [pb[bi2]], [dt_tok])
        k.tt('vector', dtA_tok[:, :, :], dt_tok[:, :, :], a_b.unsqueeze(1).broadcast_to([128, 16, 32]), ALU.mult,
             [dt_tok, sm], [dtA_tok])
    linear_fm(hnT, cd_in_w, [[(0, 5120, 32)]], wslots, evac_dt, 'dt')

    def evac_fq(bi, grp, ncols):
        h = bi // 2
        if bi % 2 == 0:
            k.act(a1[:, 0:T], grp.ap, AF.Sigmoid, [grp.bufs], [a1])
            k.ts('vector', a1[:, 0:T], a1[:, 0:T], oml[:, h:h + 1], lb_[:, h:h + 1], ALU.mult, ALU.add, [a1, sm], [a1])
            k.act(a2[:, :], a1[:, 0:T], AF.Ln, [a1], [a2])
            k.scan(a3[:, :], rstb[:, :], a2[:, :], 0.0, ALU.mult, ALU.add, [rstb, a2], [a3])
            k.ts('vector', a1[:, 0:T], a1[:, 0:T], -1.0, 1.0, ALU.mult, ALU.add, [a1], [a1])
            k.act(a2[:, :], a3[:, :], AF.Exp, [a3], [a2], scale=-1.0)
            ob = evb[0]
            k.tt('vector', ob[:, :], a1[:, 0:T], a2[:, :], ALU.mult, [a1, a2], [ob])
            k.dma('sync', kt_s[h * 128:(h + 1) * 128, :], ob[:, :], [ob], [kt_s])
            a3v = a3[:, :].rearrange("p (c l) -> p c l", c=16)
            a2v = a2[:, :].rearrange("p (c l) -> p c l", c=16)
            k.tt('vector', a2v, a3v[:, :, 127:128].broadcast_to([128, 16, 128]), a3v, ALU.subtract, [a3], [a2])
            k.act(a2[:, :], a2[:, :], AF.Exp, [a2], [a2])
            ob = evb[1]
            k.tt('vector', ob[:, :], a1[:, 0:T], a2[:, :], ALU.mult, [a1, a2], [ob])
            k.dma('sync', kd_s[h * 128:(h + 1) * 128, :], ob[:, :], [ob], [kd_s])
            k.act(ebl[:, h, :], a3v[:, :, 127], AF.Exp, [a3], [ebl])
        else:
            k.act(a1[:, 0:T], grp.ap, AF.Silu, [grp.bufs], [a1])
            k.act(a2[:, :], a3[:, :], AF.Exp, [a3], [a2])
            ob = evb[2]
            k.tt('vector', ob[:, :], a1[:, 0:T], a2[:, :], ALU.mult, [a1, a2], [ob])
            k.dma('sync', qt_s[h * 128:(h + 1) * 128, :], ob[:, :], [ob], [qt_s])
    blocks = []
    for h in range(16):
        blocks.append([(0, 7200 + h * 128, 128)])
        blocks.append([(0, 5152 + h * 128, 128)])
    linear_fm(hnT, cd_in_w, blocks, wslots, evac_fq, 'fq')

    def evac_gd(bi, grp, ncols):
        ob = evb[bi % 3]
        k.act(ob[:, :], grp.ap, AF.Silu, [grp.bufs], [ob])
        k.dma('sync', sgd_s[bi * 128:(bi + 1) * 128, :], ob[:, :], [ob], [sgd_s])
    linear_fm(hnT, cd_in_w, [[(0, 11296 + i * 128, 128)] for i in range(16)], wslots, evac_gd, 'gd')
    P.barrier()
    sb.reset(l1_mark)

    wo = [sb.alloc([128, 32, 256], BF16, f'wv{i}') for i in range(2)]
    vb = [sb.alloc([128, 256], BF16, f'vb{i}') for i in range(4)]
    n = 0
    for sl in range(8):
        w_ = wo[sl % 2]
        k.dma('gpsimd', w_[:, :, :], cd_in_w[:, 9248 + sl * 256:9248 + (sl + 1) * 256].rearrange("(kc p) c -> p kc c", p=128),
              [], [w_])
        for tt in range(16):
            bi = n % 8
            v_ = vb[n % 4]
            n += 1
            for kc in range(32):
                k.mm(bank(bi)[:, 0:256], hnT[:, kc, tt * 128:(tt + 1) * 128], w_[:, kc, :], kc == 0, kc == 31,
                     [hnT, w_], [pb[bi]])
            k.cpa(v_[:, :], bank(bi)[:, 0:256], [pb[bi]], [v_])
            k.dma('sync', v_s[tt * 128:(tt + 1) * 128, sl * 256:(sl + 1) * 256], v_[:, :], [v_], [v_s])
    P.barrier()
    sb.reset(base_mark)

    sprev = sb.alloc([128, 4, 512], F32, 'sprev')
    sprev_b = sb.alloc([128, 4, 512], BF16, 'sprev_b')
    hprev = sb.alloc([128, 16, 128], F32, 'hprev')
    hprev_b = sb.alloc([128, 16, 128], BF16, 'hprev_b')
    Db = sb.alloc([128, 32], F32, 'Db')
    k.memset('vector', sprev[:, :, :], 0.0, [sprev])
    k.memset('vector', sprev_b[:, :, :], 0.0, [sprev_b])
    k.memset('vector', hprev[:, :, :], 0.0, [hprev])
    k.memset('vector', hprev_b[:, :, :], 0.0, [hprev_b])
    k.dma('sync', Db[:, :], d_skip[:].partition_broadcast(128), [], [Db])

    def dbl(shape, dt, name):
        return [sb.alloc(shape, dt, f'{name}{i}') for i in range(2)]
    xcT_c = dbl([128, 16, 128], BF16, 'xcT_c')
    BT_c = dbl([128, 4, 128], BF16, 'BT_c')
    CT_c = dbl([128, 4, 128], BF16, 'CT_c')
    sz_c = dbl([128, 16, 128], BF16, 'sz_c')
    qt_c = dbl([128, 16, 128], BF16, 'qt_c')
    kt_c = dbl([128, 16, 128], BF16, 'kt_c')
    kd_c = dbl([128, 16, 128], BF16, 'kd_c')
    sgd_c = dbl([128, 16, 128], BF16, 'sgd_c')
    v_c = dbl([128, 2048], BF16, 'v_c')
    x_tok = sb.alloc([128, 32, 64], BF16, 'x_tok')
    B_tok = sb.alloc([128, 4, 128], BF16, 'B_tok')
    sml = sb.alloc([128, 256], F32, 'sml')
    xdt = sb.alloc([128, 32, 64], BF16, 'xdt')
    xdtd = sb.alloc([128, 32, 64], BF16, 'xdtd')
    cbm = sb.alloc([128, 4, 128], F32, 'cbm')
    yoff = sb.alloc([128, 32, 64], F32, 'yoff')
    Lm4 = [sb.alloc([128, 4, 128], F32, f'Lm{i}') for i in range(2)]
    seg4 = [sb.alloc([128, 512], F32, f'seg{i}') for i in range(2)]
    MT4 = [sb.alloc([128, 4, 128], BF16, f'MT{i}') for i in range(2)]
    yv = sb.alloc([128, 2048], F32, 'yv')
    gT = sb.alloc([128, 16, 128], F32, 'gT')
    sq = sb.alloc([128, 16, 128], F32, 'sq')
    t2v = sq[:, :, :].rearrange("p a b -> p (a b)")
    rs = sb.alloc([128, 4, 128], F32, 'rs')
    ycb = dbl([128, 16, 128], BF16, 'ycb')
    kd_tok = sb.alloc([128, 16, 128], BF16, 'kd_tok')
    attm4 = [sb.alloc([128, 4, 128], BF16, f'attm{i}') for i in range(2)]
    rsh = sb.alloc([128, 2048], F32, 'rsh')
    ydb = dbl([128, 16, 128], BF16, 'ydb')
    o_hg, _ = PP['hgn']

    osb = sb.alloc([128, 2048], F32, 'osb')
    sqh = sb.alloc([128, 2048], F32, 'sqh')

    def chunk_steps(c):
        cs = slice(c * 128, (c + 1) * 128)
        p2 = c % 2
        xT, Bc, Cc, szc = xcT_c[p2], BT_c[p2], CT_c[p2], sz_c[p2]
        qtc, ktc, kdc, sgc, vc = qt_c[p2], kt_c[p2], kd_c[p2], sgd_c[p2], v_c[p2]
        eacs = sml[:, 64:96]
        dte = sml[:, 96:128]
        cdb = sml[:, 128:160]
        x_tok2 = x_tok[:, :, :].rearrange("p a b -> p (a b)")
        ssd = []
        hg_ = []

        def ld(dst, src):
            k.dma('sync', dst[:, :, :], src[:, cs].rearrange("(b p) t -> p b t", p=128), [src], [dst])

        def s_load_all():
            ld(xT, xc_s)
            ld(Bc, BT_s)
            ld(Cc, CT_s)
            ld(szc, sz_s)
            ld(qtc, qt_s)
            ld(ktc, kt_s)
            ld(kdc, kd_s)
            ld(sgc, sgd_s)
            k.dma('sync', vc[:, :], v_s[cs, :], [v_s], [vc])

        def s_pro1():
            for half in range(2):
                bb = bank_bf(4)
                for j in range(8):
                    k.tr(bb[:, j * 128:(j + 1) * 128], xT[:, half * 8 + j, :], ident_b, [xT, cbf], [pb[4]])
                k.cpa(x_tok2[:, half * 1024:(half + 1) * 1024], bb[:, 0:1024], [pb[4]], [x_tok])
            bb = bank_bf(4)
            for g in range(4):
                k.tr(bb[:, g * 128:(g + 1) * 128], Bc[:, g, :], ident_b, [Bc, cbf], [pb[4]])
            k.cpa(B_tok[:, :, :].rearrange("p a b -> p (a b)"), bb[:, 0:512], [pb[4]], [B_tok])
            k.mm(bank(4)[:, 0:32], le_f, dtA_tok[:, c, :], True, True, [cst, dtA_tok], [pb[4]])
            k.mm(bank(4)[:, 32:64], ones_f, dtA_tok[:, c, :], True, True, [cst, dtA_tok], [pb[4]])
            k.cp('scalar', sml[:, 0:64], bank(4)[:, 0:64], [pb[4]], [sml])
            k.act(sml[:, 64:96], sml[:, 0:32], AF.Exp, [sml], [sml])
            k.tt('vector', sml[:, 96:128], sml[:, 32:64], sml[:, 0:32], ALU.subtract, [sml], [sml])
            k.act(sml[:, 96:128], sml[:, 96:128], AF.Exp, [sml], [sml])
            k.act(sml[:, 128:160], sml[:, 32:64], AF.Exp, [sml], [sml])
            k.tt('vector', xdt[:, :, :], x_tok[:, :, :], dt_tok[:, c, :].unsqueeze(2).broadcast_to([128, 32, 64]), ALU.mult,
                 [x_tok, dt_tok], [xdt])
            k.tt('gpsimd', xdtd[:, :, :], xdt[:, :, :], dte.unsqueeze(2).broadcast_to([128, 32, 64]), ALU.mult,
                 [xdt, sml], [xdtd])
        ssd.append(s_pro1)

        def s_pro2():
            for g in range(4):
                k.mm(bank(4)[:, g * 128:(g + 1) * 128], Bc[:, g, :], Cc[:, g, :], True, True, [Bc, Cc], [pb[4]])
            k.tt('vector', cbm[:, :, :], bank(4).rearrange("p (a b) -> p a b", a=4),
                 le_f.unsqueeze(1).broadcast_to([128, 4, 128]), ALU.mult, [pb[4], cst], [cbm])
            for g in range(4):
                k.mm(bank(4), Cc[:, g, :], sprev_b[:, g, :], True, True, [Cc, sprev_b], [pb[4]])
                k.tt('vector', yoff[:, g * 8:(g + 1) * 8, :], bank(4).rearrange("p (a b) -> p a b", a=8),
                     eacs[:, g * 8:(g + 1) * 8].unsqueeze(2).broadcast_to([128, 8, 64]), ALU.mult, [pb[4], sml], [yoff])
        ssd.append(s_pro2)

        def mk_ssd_batch(bq):
            def f():
                hd0 = bq * 4
                g = bq // 2
                yb_ = g % 2
                L4 = Lm4[bq % 2]
                k.tt('gpsimd' if bq % 2 else 'vector', L4[:, :, :], gt_f.unsqueeze(1).broadcast_to([128, 4, 128]),
                     dtA_tok[:, c, hd0:hd0 + 4].unsqueeze(2).broadcast_to([128, 4, 128]), ALU.mult, [cst, dtA_tok], [L4])
                for q in range(4):
                    k.mm(bank(6)[:, q * 128:(q + 1) * 128], L4[:, q, :], le_f, True, True, [L4, cst], [pb[6]])
                s4 = seg4[bq % 2]
                k.act(s4[:, :], bank(6), AF.Exp, [pb[6]], [s4])
                M4 = MT4[bq % 2]
                k.tt('vector', M4[:, :, :], s4[:, :].rearrange("p (a b) -> p a b", a=4),
                     cbm[:, g:g + 1, :].broadcast_to([128, 4, 128]), ALU.mult, [s4, cbm], [M4])
                for q in range(4):
                    hd = hd0 + q
                    k.mm(bank(yb_)[:, (hd % 8) * 64:(hd % 8 + 1) * 64], M4[:, q, :], xdt[:, hd, :], True, True,
                         [M4, xdt], [pb[yb_]])
                if bq % 2 == 1:
                    k.tt('vector', yv[:, g * 512:(g + 1) * 512], bank(yb_),
                         yoff[:, g * 8:(g + 1) * 8, :].rearrange("p a b -> p (a b)"), ALU.add, [pb[yb_], yoff], [yv])
            return f
        for bq in range(8):
            ssd.append(mk_ssd_batch(bq))

        def s_states():
            for g in range(4):
                k.mm(bank(4), B_tok[:, g, :], xdtd[:, g * 8:(g + 1) * 8, :].rearrange("p a b -> p (a b)"), True, True,
                     [B_tok, xdtd], [pb[4]])
                spv = sprev[:, g, :].rearrange("p (a b) -> p a b", a=8)
                k.tt('vector', spv, spv, cdb[:, g * 8:(g + 1) * 8].unsqueeze(2).broadcast_to([128, 8, 64]), ALU.mult,
                     [sprev, sml], [sprev])
                k.tt('vector', sprev[:, g, :], sprev[:, g, :], bank(4), ALU.add, [sprev, pb[4]], [sprev])
                k.cp('scalar', sprev_b[:, g, :], sprev[:, g, :], [sprev], [sprev_b])
            k.tt('gpsimd', t2v.rearrange("p (a b) -> p a b", a=32), x_tok[:, :, :], Db[:, :].unsqueeze(2).broadcast_to([128, 32, 64]), ALU.mult,
                 [x_tok, Db], [sq])
            k.tt('vector', yv[:, :], yv[:, :], t2v, ALU.add, [yv, sq], [yv])
        ssd.append(s_states)

        def s_epi():
            for q4 in range(4):
                for j in range(4):
                    fc = q4 * 4 + j
                    k.tr(bank(4)[:, j * 128:(j + 1) * 128], yv[:, fc * 128:(fc + 1) * 128], ident_f, [yv, cst], [pb[4]])
                k.tt('vector', gT[:, q4 * 4:(q4 + 1) * 4, :], bank(4).rearrange("p (a b) -> p a b", a=4),
                     szc[:, q4 * 4:(q4 + 1) * 4, :], ALU.mult, [pb[4], szc], [gT])
            k.act(sq[:, :, :], gT[:, :, :], AF.Square, [gT], [sq])
            for g in range(4):
                for j in range(4):
                    k.mm(bank(4)[:, g * 128:(g + 1) * 128], ones_f, sq[:, g * 4 + j, :], j == 0, j == 3, [cst, sq], [pb[4]])
            rs2 = rs[:, :, :].rearrange("p a b -> p (a b)")
            k.ts('vector', rs2, bank(4), 1.0 / 512, 1e-6, ALU.mult, ALU.add, [pb[4]], [rs])
            k.act(rs2, rs2, AF.Ln, [rs], [rs])
            k.act(rs2, rs2, AF.Exp, [rs], [rs], scale=-0.5)
            yc_ = ycb[p2]
            for fc in range(16):
                k.stt(yc_[:, fc, :], gT[:, fc, :], ppc('ssdn', fc), rs[:, fc // 4, :], ALU.mult, ALU.mult,
                      [gT, pp, rs], [yc_])
            k.dma('sync', yT[0:2048, cs].rearrange("(b p) t -> p b t", p=128), yc_[:, :, :], [yc_], [yT])
        ssd.append(s_epi)


        def h_pro():
            for half in range(2):
                bb = bank_bf(5)
                for j in range(8):
                    k.tr(bb[:, j * 128:(j + 1) * 128], kdc[:, half * 8 + j, :], ident_b, [kdc, cbf], [pb[5]])
                k.cpa(kd_tok[:, half * 8:(half + 1) * 8, :].rearrange("p a b -> p (a b)"), bb[:, 0:1024], [pb[5]], [kd_tok])
        hg_.append(h_pro)

        def mk_h_batch(hq):
            def f():
                h0 = hq * 4
                ob_ = 2 + hq % 2
                for q in range(4):
                    h = h0 + q
                    k.mm(bank(7)[:, q * 128:(q + 1) * 128], ktc[:, h, :], qtc[:, h, :], True, True, [ktc, qtc], [pb[7]])
                am4 = attm4[hq % 2]
                k.tt('vector', am4[:, :, :], bank(7).rearrange("p (a b) -> p a b", a=4),
                     le_f.unsqueeze(1).broadcast_to([128, 4, 128]), ALU.mult, [pb[7], cst], [am4])
                for q in range(4):
                    h = h0 + q
                    hs = slice(h * 128, (h + 1) * 128)
                    oc = slice(q * 128, (q + 1) * 128)
                    k.mm(bank(ob_)[:, oc], vc[:, hs], am4[:, q, :], True, False, [vc, am4], [pb[ob_]])
                    k.mm(bank(ob_)[:, oc], hprev_b[:, h, :], qtc[:, h, :], False, True, [hprev_b, qtc], [pb[ob_]])
                for q in range(4):
                    h = h0 + q
                    hs = slice(h * 128, (h + 1) * 128)
                    k.mm(bank(5)[:, q * 128:(q + 1) * 128], kd_tok[:, h, :], vc[:, hs], True, True, [kd_tok, vc], [pb[5]])
                hp4 = hprev[:, h0:h0 + 4, :]
                k.tt('vector', hp4, hp4, ebl[:, h0:h0 + 4, c:c + 1].broadcast_to([128, 4, 128]), ALU.mult,
                     [hprev, ebl], [hprev])
                k.tt('vector', hp4, hp4, bank(5).rearrange("p (a b) -> p a b", a=4), ALU.add, [hprev, pb[5]], [hprev])
                k.cp('scalar', hprev_b[:, h0:h0 + 4, :], hp4, [hprev], [hprev_b])
                k.cp('scalar', osb[:, hq * 512:(hq + 1) * 512], bank(ob_), [pb[ob_]], [osb])
                k.act(sqh[:, hq * 512:(hq + 1) * 512], bank(ob_), AF.Square, [pb[ob_]], [sqh])
            return f
        for hq in range(4):
            hg_.append(mk_h_batch(hq))

        def h_epi():
            for q4 in range(4):
                qs = slice(q4 * 512, (q4 + 1) * 512)
                k.mm(bank(5), ones_f, sqh[:, qs], True, True, [cst, sqh], [pb[5]])
                k.ts('vector', rsh[:, qs], bank(5), 1.0 / 128, 1e-6, ALU.mult, ALU.add, [pb[5]], [rsh])
            k.act(rsh[:, :], rsh[:, :], AF.Ln, [rsh], [rsh])
            k.act(rsh[:, :], rsh[:, :], AF.Exp, [rsh], [rsh], scale=-0.5)
            k.tt('vector', rsh[:, :], osb[:, :], rsh[:, :], ALU.mult, [osb, rsh], [rsh])
            rsh3 = rsh[:, :].rearrange("p (a b) -> p a b", a=16)
            k.tt('gpsimd', rsh3, rsh3, pp[:, o_hg:o_hg + 16].unsqueeze(2).broadcast_to([128, 16, 128]), ALU.mult,
                 [rsh, pp], [rsh])
            yd_ = ydb[p2]
            k.tt('gpsimd', yd_[:, :, :], rsh3, sgc[:, :, :], ALU.mult, [rsh, sgc], [yd_])
            k.dma('sync', yT[2048:4096, cs].rearrange("(b p) t -> p b t", p=128), yd_[:, :, :], [yd_], [yT])
        hg_.append(h_epi)
        return ssd, hg_, s_load_all

    def merge2(lists):
        items = []
        for li, l in enumerate(lists):
            n = len(l)
            for idx, f in enumerate(l):
                items.append(((idx + 0.5) / n, li, idx, f))
        items.sort(key=lambda t: (t[0], t[1], t[2]))
        return [t[3] for t in items]

    cks = [chunk_steps(c) for c in range(16)]
    cks[0][2]()
    for c in range(16):
        a_, b_, _ = cks[c]
        if c + 1 < 16:
            cks[c + 1][2]()
        for f in merge2([a_, b_]):
            f()
    P.barrier()
    sb.reset(base_mark)

    out_proj(cd_out_w, h1, h2)

    nwb = sb.alloc([128, D], F32, 'fnw')
    xt2 = [sb.alloc([128, D], F32, f'fx{i}') for i in range(2)]
    xo2 = [sb.alloc([128, D], F32, f'fo{i}') for i in range(2)]
    junk = sb.alloc([128, D], BF16, 'fjunk')
    st = sb.alloc([128, 4], F32, 'fst')
    k.dma('sync', nwb[:, :], final_norm[:].partition_broadcast(128), [], [nwb])
    for tt in range(16):
        xt = xt2[tt % 2]
        xo = xo2[tt % 2]
        k.dma('sync', xt[:, :], h2[tt * 128:(tt + 1) * 128, :], [h2], [xt])
        k.act(junk[:, :], xt[:, :], AF.Square, [xt], [junk, st], accum=st[:, 0:1])
        k.ts('vector', st[:, 1:2], st[:, 0:1], 1.0 / D, 1e-6, ALU.mult, ALU.add, [st], [st])
        k.act(st[:, 2:3], st[:, 1:2], AF.Sqrt, [st], [st])
        k.recip(st[:, 3:4], st[:, 2:3], [st], [st])
        k.stt(xo[:, :], xt[:, :], st[:, 3:4], nwb[:, :], ALU.mult, ALU.mult, [xt, st, nwb], [xo])
        k.dma('sync', out_d[tt * 128:(tt + 1) * 128, :], xo[:, :], [xo], [out_d])
    P.emit()
    return nc


_W_KEYS = (("norm_w", None), ("final_norm", None), ("ab_in_w", 0), ("ab_rg_wa", 0), ("ab_rg_wx", 0),
           ("ab_w_uk", 0), ("ab_w_uv", 0), ("ab_out_w", 0), ("cd_in_w", 0), ("cd_a_log", 0),
           ("cd_d_skip", 0), ("cd_out_w", 0))


def make_in_maps(inp, batches):
    common = {}
    for name, idx in _W_KEYS:
        a = np.asarray(inp[name], dtype=np.float32)
        common[name] = np.ascontiguousarray(a if idx is None else a[idx])
    common["pp"] = pack_params(inp)
    cst, rst = make_consts()
    common["cst"] = cst
    common["rst"] = rst
    maps = []
    for b in batches:
        d = dict(common)
        d["x"] = np.ascontiguousarray(np.asarray(inp["x"][b], dtype=np.float32))
        maps.append(d)
    return maps


def kernel(**inputs):
    nc = build()
    in_maps = make_in_maps(inputs, range(8))
    res = run_bass_kernel_spmd(nc, in_maps, core_ids=list(range(8)))
    return np.stack([np.asarray(res.results[b]["out"], dtype=np.float32) for b in range(8)], axis=0)
```
